# Optimizing a Trainium2 kernel written in Bass

```python
import math
import jax, jax.numpy as jnp
from jax import lax
import numpy as np

D_MODEL = 1024
BATCH = 8
SEQ = 4096
DEPTH = 2

CTX_LEN = 256
GRID_W = 64
EPS = 1e-6
A_HEADS = 8
A_QK_DIM = 64
A_V_DIM = 2 * A_QK_DIM
A_WIDTH = A_HEADS * A_V_DIM
QK_SIZE = A_HEADS * 2 * A_QK_DIM
ROPE_THETA = 10000.0
Q_BLOCK = 128
B_WIDTH = 1024
CONV_WIDTH = 31
C_WIDTH = 1024
S5_GROUP = 16
S5_GROUPS = C_WIDTH // S5_GROUP
S5_STATE = 64
EVEN_SIZES = (QK_SIZE, QK_SIZE, A_WIDTH, A_WIDTH, B_WIDTH, B_WIDTH, B_WIDTH)
EVEN_IN = sum(EVEN_SIZES)
EVEN_OUT = A_WIDTH + B_WIDTH
ODD_IN = 2 * C_WIDTH
N_EVEN = (DEPTH + 1) // 2
N_ODD = DEPTH // 2

kernel_name = 'hybrid_diffattn_conformer_s5_prefix_dit'


def rms_norm(x, g):
    xf = x.astype(jnp.float32)
    y = xf * lax.rsqrt(jnp.mean(xf * xf, axis=-1, keepdims=True) + EPS)
    return (y * g.astype(jnp.float32)).astype(x.dtype)


def layer_norm(x, g, b):
    xf = x.astype(jnp.float32)
    mu = jnp.mean(xf, axis=-1, keepdims=True)
    xc = xf - mu
    y = xc * lax.rsqrt(jnp.mean(xc * xc, axis=-1, keepdims=True) + EPS)
    return (y * g.astype(jnp.float32) + b.astype(jnp.float32)).astype(x.dtype)


def split_cols(p, sizes):
    return jnp.split(p, [int(s) for s in np.cumsum(sizes)[:-1]], axis=-1)


def ada_ln(c, c_ctx, w, b):
    mx = jax.nn.silu(c) @ w + b
    mc = jax.nn.silu(c_ctx) @ w + b
    return jnp.split(mx[:, None, :], 3, axis=-1), jnp.split(mc[None, None, :], 3, axis=-1)


def axial_rope_tables(n_tokens):
    rows = n_tokens // GRID_W
    row = jnp.repeat(jnp.arange(rows, dtype=jnp.float32), GRID_W)
    col = jnp.tile(jnp.arange(GRID_W, dtype=jnp.float32), rows)
    n_freq = A_QK_DIM // 4
    freqs = ROPE_THETA ** (-jnp.arange(n_freq, dtype=jnp.float32) / n_freq)
    ang = jnp.concatenate([row[:, None] * freqs, col[:, None] * freqs], axis=-1)
    return jnp.cos(ang), jnp.sin(ang)


def apply_rope(x, cos, sin):
    half = x.shape[-1] // 2
    x1, x2 = x[..., :half], x[..., half:]
    cs = cos[None, :, None, None, :].astype(x.dtype)
    sn = sin[None, :, None, None, :].astype(x.dtype)
    return jnp.concatenate([x1 * cs - x2 * sn, x2 * cs + x1 * sn], axis=-1)


def to_qk_heads(t):
    return t.reshape(t.shape[0], t.shape[1], A_HEADS, 2, A_QK_DIM)


def to_v_heads(t):
    return t.reshape(t.shape[0], t.shape[1], A_HEADS, A_V_DIM)


def diff_attention(q, k, v, lam):
    s = jnp.einsum('bqhcd,bkhcd->bhcqk', q, k).astype(jnp.float32) * (A_QK_DIM ** -0.5)
    p = jax.nn.softmax(s, axis=-1)
    w = p[:, :, 0] - lam * p[:, :, 1]
    return jnp.einsum('bhqk,bkhd->bqhd', w.astype(v.dtype), v)


def blocked_diff_attention(q, k, v, lam):
    b, n = q.shape[0], q.shape[1]
    nblk = n // Q_BLOCK
    qb = q.reshape(b, nblk, Q_BLOCK, A_HEADS, 2, A_QK_DIM).swapaxes(0, 1)
    out = lax.map(lambda qq: diff_attention(qq, k, v, lam), qb)
    return out.swapaxes(0, 1).reshape(b, n, A_HEADS, A_V_DIM)


def diff_head_out(o, subln_g, lam_init):
    o = rms_norm(o, subln_g) * (1.0 - lam_init)
    return o.reshape(o.shape[0], o.shape[1], A_WIDTH)


def depthwise_conv(x, w, b):
    y = lax.conv_general_dilated(
        x, w[:, None, :].astype(x.dtype), window_strides=(1,),
        padding=[(CONV_WIDTH // 2, CONV_WIDTH // 2)],
        dimension_numbers=('NWC', 'WIO', 'NWC'), feature_group_count=x.shape[-1])
    return y + b.astype(x.dtype)


def conformer_conv(glu_a, glu_b, dw_w, dw_b, ln_g, ln_b):
    h = glu_a * jax.nn.sigmoid(glu_b)
    h = depthwise_conv(h, dw_w, dw_b)
    h = layer_norm(h, ln_g, ln_b)
    return jax.nn.silu(h)


def even_layer(x, ctx, c, c_ctx, mod_w, mod_b, norm_g, in_w, lam_q1, lam_k1, lam_q2, lam_k2,
               subln_g, dw_w, dw_b, cln_g, cln_b, out_w, lam_init, rope, update_ctx):
    (shx, scx, gtx), (shc, scc, gtc) = ada_ln(c, c_ctx, mod_w, mod_b)
    hx = rms_norm(x, norm_g) * (1.0 + scx) + shx
    hc = rms_norm(ctx, norm_g) * (1.0 + scc) + shc
    qx, kx, vx, gax, glax, glbx, gbx = split_cols(hx @ in_w, EVEN_SIZES)
    cos, sin = rope
    qx = apply_rope(to_qk_heads(qx), cos, sin)
    kx = apply_rope(to_qk_heads(kx), cos, sin)
    vx = to_v_heads(vx)
    if update_ctx:
        qc, kc, vc, gac, glac, glbc, gbc = split_cols(hc @ in_w, EVEN_SIZES)
    else:
        kc, vc = split_cols(hc @ in_w[:, QK_SIZE:2 * QK_SIZE + A_WIDTH], (QK_SIZE, A_WIDTH))
    kc = to_qk_heads(kc)
    vc = to_v_heads(vc)
    f32 = jnp.float32
    lam = (jnp.exp(jnp.sum(lam_q1.astype(f32) * lam_k1.astype(f32)))
           - jnp.exp(jnp.sum(lam_q2.astype(f32) * lam_k2.astype(f32))) + lam_init)
    k_all = jnp.concatenate([kc, kx], axis=1)
    v_all = jnp.concatenate([vc, vx], axis=1)
    a_x = diff_head_out(blocked_diff_attention(qx, k_all, v_all, lam), subln_g, lam_init)
    b_x = conformer_conv(glax, glbx, dw_w, dw_b, cln_g, cln_b)
    mix_x = jnp.concatenate([a_x * jax.nn.silu(gax), b_x * jax.nn.silu(gbx)], axis=-1) @ out_w
    x = x + gtx * mix_x
    if not update_ctx:
        return x, None
    a_c = diff_head_out(diff_attention(to_qk_heads(qc), kc, vc, lam), subln_g, lam_init)
    b_c = conformer_conv(glac, glbc, dw_w, dw_b, cln_g, cln_b)
    mix_c = jnp.concatenate([a_c * jax.nn.silu(gac), b_c * jax.nn.silu(gbc)], axis=-1) @ out_w
    ctx = ctx + gtc * mix_c
    return x, ctx


def s5_discretise(lam_re, lam_im, log_dt, b_re, b_im):
    dt = jnp.exp(log_dt)[:, None]
    mag = jnp.exp(lam_re * dt)
    ar, ai = mag * jnp.cos(lam_im * dt), mag * jnp.sin(lam_im * dt)
    den = lam_re * lam_re + lam_im * lam_im
    fr = ((ar - 1.0) * lam_re + ai * lam_im) / den
    fi = (ai * lam_re - (ar - 1.0) * lam_im) / den
    bbr = fr[..., None] * b_re - fi[..., None] * b_im
    bbi = fr[..., None] * b_im + fi[..., None] * b_re
    return ar, ai, bbr, bbi


def complex_linear_combine(e1, e2):
    a1r, a1i, b1r, b1i = e1
    a2r, a2i, b2r, b2i = e2
    return (a1r * a2r - a1i * a2i,
            a1r * a2i + a1i * a2r,
            a2r * b1r - a2i * b1i + b2r,
            a2r * b1i + a2i * b1r + b2i)


def s5_scan(u, ar, ai, bbr, bbi, reverse, h0=None):
    bu_re = jnp.einsum('blgh,gph->blgp', u, bbr)
    bu_im = jnp.einsum('blgh,gph->blgp', u, bbi)
    n = u.shape[1]
    a_re = jnp.broadcast_to(ar, (1, n) + ar.shape)
    a_im = jnp.broadcast_to(ai, (1, n) + ai.shape)
    acum_re, acum_im, h_re, h_im = lax.associative_scan(
        complex_linear_combine, (a_re, a_im, bu_re, bu_im), reverse=reverse, axis=1)
    if h0 is not None:
        h0r, h0i = h0[0][:, None], h0[1][:, None]
        h_re, h_im = (h_re + acum_re * h0r - acum_im * h0i,
                      h_im + acum_re * h0i + acum_im * h0r)
    return h_re, h_im


def s5_readout(h_re, h_im, c_re, c_im):
    return jnp.einsum('blgp,ghp->blgh', h_re, c_re) - jnp.einsum('blgp,ghp->blgh', h_im, c_im)


def s5_bidirectional(u_x, u_c, lam_re, lam_im, log_dt, b_re, b_im, c_re, c_im, d, update_ctx):
    f32 = jnp.float32
    bx, lx = u_x.shape[0], u_x.shape[1]
    lc = u_c.shape[1]
    ux = u_x.astype(f32).reshape(bx, lx, S5_GROUPS, S5_GROUP)
    uc = u_c.astype(f32).reshape(bx, lc, S5_GROUPS, S5_GROUP)
    ys_x, ys_c = [], []
    for dr in range(2):
        reverse = dr == 1
        ar, ai, bbr, bbi = s5_discretise(lam_re[dr].astype(f32), lam_im[dr].astype(f32),
                                         log_dt[dr].astype(f32), b_re[dr].astype(f32), b_im[dr].astype(f32))
        cr, ci = c_re[dr].astype(f32), c_im[dr].astype(f32)
        hc_re, hc_im = s5_scan(uc, ar, ai, bbr, bbi, reverse)
        end = 0 if reverse else lc - 1
        hx_re, hx_im = s5_scan(ux, ar, ai, bbr, bbi, reverse, (hc_re[:, end], hc_im[:, end]))
        ys_x.append(s5_readout(hx_re, hx_im, cr, ci))
        if update_ctx:
            ys_c.append(s5_readout(hc_re, hc_im, cr, ci))
    df = d.astype(f32)
    y_x = ((ys_x[0] + ys_x[1]).reshape(bx, lx, C_WIDTH) + df * u_x.astype(f32)).astype(u_x.dtype)
    if not update_ctx:
        return y_x, None
    y_c = ((ys_c[0] + ys_c[1]).reshape(bx, lc, C_WIDTH) + df * u_c.astype(f32)).astype(u_c.dtype)
    return y_x, y_c


def s5_head(y, z, glu_w, glu_b, out_w):
    y = jax.nn.gelu(y)
    y = y * jax.nn.sigmoid(y @ glu_w + glu_b)
    return (y * jax.nn.silu(z)) @ out_w


def odd_layer(x, ctx, c, c_ctx, mod_w, mod_b, norm_g, in_w, lam_re, lam_im, log_dt, b_re, b_im,
              c_re, c_im, d, glu_w, glu_b, out_w, update_ctx):
    (shx, scx, gtx), (shc, scc, gtc) = ada_ln(c, c_ctx, mod_w, mod_b)
    hx = rms_norm(x, norm_g) * (1.0 + scx) + shx
    hc = rms_norm(ctx, norm_g) * (1.0 + scc) + shc
    u_x, z_x = split_cols(hx @ in_w, (C_WIDTH, C_WIDTH))
    if update_ctx:
        u_c, z_c = split_cols(hc @ in_w, (C_WIDTH, C_WIDTH))
    else:
        u_c = hc @ in_w[:, :C_WIDTH]
    y_x, y_c = s5_bidirectional(u_x, u_c, lam_re, lam_im, log_dt, b_re, b_im, c_re, c_im, d, update_ctx)
    x = x + gtx * s5_head(y_x, z_x, glu_w, glu_b, out_w)
    if not update_ctx:
        return x, None
    ctx = ctx + gtc * s5_head(y_c, z_c, glu_w, glu_b, out_w)
    return x, ctx


def setup_inputs(seed: int = 0) -> dict:
    key = jax.random.key(seed)
    keys = list(jax.random.split(key, 48))
    f32 = jnp.float32
    cnt = [0]

    def nk():
        k = keys[cnt[0]]
        cnt[0] += 1
        return k

    def normal(shape, scale):
        return scale * jax.random.normal(nk(), shape, f32)

    D = D_MODEL
    G, P, H = S5_GROUPS, S5_STATE, S5_GROUP
    inp = {}
    inp['x'] = normal((BATCH, SEQ, D), 1.0)
    inp['c'] = normal((BATCH, D), 1.0)
    inp['ctx'] = normal((BATCH, CTX_LEN, D), 1.0)
    inp['c_ctx'] = normal((D,), 1.0)
    inp['ev_mod_w'] = normal((N_EVEN, D, 3 * D), 0.5 * D ** -0.5)
    inp['ev_mod_b'] = normal((N_EVEN, 3 * D), 0.01)
    inp['ev_norm_g'] = 1.0 + normal((N_EVEN, D), 0.01)
    inp['ev_in_w'] = normal((N_EVEN, D, EVEN_IN), D ** -0.5)
    inp['ev_lam_q1'] = normal((N_EVEN, A_QK_DIM), 0.1)
    inp['ev_lam_k1'] = normal((N_EVEN, A_QK_DIM), 0.1)
    inp['ev_lam_q2'] = normal((N_EVEN, A_QK_DIM), 0.1)
    inp['ev_lam_k2'] = normal((N_EVEN, A_QK_DIM), 0.1)
    inp['ev_subln_g'] = 1.0 + normal((N_EVEN, A_V_DIM), 0.01)
    inp['ev_dw_w'] = normal((N_EVEN, CONV_WIDTH, B_WIDTH), CONV_WIDTH ** -0.5)
    inp['ev_dw_b'] = normal((N_EVEN, B_WIDTH), 0.01)
    inp['ev_cln_g'] = 1.0 + normal((N_EVEN, B_WIDTH), 0.01)
    inp['ev_cln_b'] = normal((N_EVEN, B_WIDTH), 0.01)
    inp['ev_out_w'] = normal((N_EVEN, EVEN_OUT, D), EVEN_OUT ** -0.5)
    inp['od_mod_w'] = normal((N_ODD, D, 3 * D), 0.5 * D ** -0.5)
    inp['od_mod_b'] = normal((N_ODD, 3 * D), 0.01)
    inp['od_norm_g'] = 1.0 + normal((N_ODD, D), 0.01)
    inp['od_in_w'] = normal((N_ODD, D, ODD_IN), D ** -0.5)
    inp['s5_lam_re'] = -0.5 + normal((N_ODD, 2, G, P), 0.01)
    inp['s5_lam_im'] = jnp.pi * jnp.arange(P, dtype=f32) + normal((N_ODD, 2, G, P), 0.01)
    inp['s5_log_dt'] = jax.random.uniform(nk(), (N_ODD, 2, G), f32, math.log(1e-3), math.log(1e-1))
    inp['s5_b_re'] = normal((N_ODD, 2, G, P, H), (2.0 * H) ** -0.5)
    inp['s5_b_im'] = normal((N_ODD, 2, G, P, H), (2.0 * H) ** -0.5)
    inp['s5_c_re'] = normal((N_ODD, 2, G, H, P), (2.0 * P) ** -0.5 * 4.0)
    inp['s5_c_im'] = normal((N_ODD, 2, G, H, P), (2.0 * P) ** -0.5 * 4.0)
    inp['s5_d'] = normal((N_ODD, C_WIDTH), 1.0)
    inp['od_glu_w'] = normal((N_ODD, C_WIDTH, C_WIDTH), C_WIDTH ** -0.5)
    inp['od_glu_b'] = normal((N_ODD, C_WIDTH), 0.01)
    inp['od_out_w'] = normal((N_ODD, C_WIDTH, D), C_WIDTH ** -0.5)
    inp['final_g'] = 1.0 + normal((D,), 0.01)
    return inp


def reference(x, c, ctx, c_ctx,
              ev_mod_w, ev_mod_b, ev_norm_g, ev_in_w, ev_lam_q1, ev_lam_k1, ev_lam_q2, ev_lam_k2,
              ev_subln_g, ev_dw_w, ev_dw_b, ev_cln_g, ev_cln_b, ev_out_w,
              od_mod_w, od_mod_b, od_norm_g, od_in_w, s5_lam_re, s5_lam_im, s5_log_dt,
              s5_b_re, s5_b_im, s5_c_re, s5_c_im, s5_d, od_glu_w, od_glu_b, od_out_w,
              final_g):
    rope = axial_rope_tables(x.shape[1])
    for i in range(DEPTH):
        update_ctx = i < DEPTH - 1
        j = i // 2
        if i % 2 == 0:
            lam_init = 0.8 - 0.6 * math.exp(-0.3 * i)
            x, ctx = even_layer(x, ctx, c, c_ctx, ev_mod_w[j], ev_mod_b[j], ev_norm_g[j], ev_in_w[j],
                                ev_lam_q1[j], ev_lam_k1[j], ev_lam_q2[j], ev_lam_k2[j], ev_subln_g[j],
                                ev_dw_w[j], ev_dw_b[j], ev_cln_g[j], ev_cln_b[j], ev_out_w[j],
                                lam_init, rope, update_ctx)
        else:
            x, ctx = odd_layer(x, ctx, c, c_ctx, od_mod_w[j], od_mod_b[j], od_norm_g[j], od_in_w[j],
                               s5_lam_re[j], s5_lam_im[j], s5_log_dt[j], s5_b_re[j], s5_b_im[j],
                               s5_c_re[j], s5_c_im[j], s5_d[j], od_glu_w[j], od_glu_b[j], od_out_w[j],
                               update_ctx)
    return rms_norm(x, final_g)
```

```python
import math
import re
import numpy as np
import ml_dtypes
from contextlib import ExitStack
import concourse.bass as bass
import concourse.mybir as mybir
from concourse.bass_utils import run_bass_kernel_spmd

F32 = mybir.dt.float32
BF16 = mybir.dt.bfloat16
I32 = mybir.dt.int32
AF = mybir.ActivationFunctionType
ALU = mybir.AluOpType

ENGS = ("pe", "act", "dve", "pool", "sp")
EPOCH = 30000
DMA_EPOCH = 1800

D = 1024
SEQ = 4096
CTX = 256
NTOK = SEQ + CTX
EPS = 1e-6
NB = 8


class _Op:
    __slots__ = ("eng", "fn", "is_dma", "dkey", "dcount", "pos", "deps", "signal", "sigval", "barrier")

    def __init__(self, eng, fn, is_dma=False, dkey=None):
        self.eng = eng
        self.fn = fn
        self.is_dma = is_dma
        self.dkey = dkey
        self.dcount = 0
        self.pos = 0
        self.deps = []
        self.signal = False
        self.sigval = 0
        self.barrier = False


class Prog:
    def __init__(self, nc):
        self.nc = nc
        self.streams = {e: [] for e in ENGS}
        self.last_w = {}
        self.readers = {}
        self.seen = {e: {} for e in ENGS}
        self.dma_counts = {}
        self.all_dma_last = {}
        self._npos = {e: 0 for e in ENGS}
        self.last_comp = {}

    def _add_dep(self, op, ev):
        if ev is None or ev is op:
            return
        if ev.is_dma:
            k = ("dma", ev.dkey)
            v = ev.dcount
        else:
            k = ("eng", ev.eng)
            v = ev.pos
        s = self.seen[op.eng]
        if s.get(k, 0) >= v:
            return
        s[k] = v
        op.deps.append(ev)
        ev.signal = True

    def op(self, eng, fn, reads=(), writes=(), accum=False):
        psr = [r for r in reads if r.startswith("ps")]
        if psr:
            reads = [r for r in reads if not r.startswith("ps")]
            writes = list(writes) + [r for r in psr if r not in writes]
        o = _Op(eng, fn)
        self._npos[eng] += 1
        o.pos = self._npos[eng]
        for r in reads:
            self._add_dep(o, self.last_w.get(r))
        for w in writes:
            lw = self.last_w.get(w)
            if not (accum and lw is not None and (not lw.is_dma) and lw.eng == eng):
                self._add_dep(o, lw)
            for rd in self.readers.get(w, ()):
                self._add_dep(o, rd)
        for r in reads:
            self.readers.setdefault(r, []).append(o)
        for w in writes:
            self.last_w[w] = o
            self.readers[w] = []
        self.streams[eng].append(o)
        self.last_comp[eng] = o
        return o

    def dma(self, eng, out, in_, reads=(), writes=(), key=None, **kw):
        assert key is not None
        o = _Op(eng, None, is_dma=True, dkey=key)
        c = self.dma_counts.get(key, 0) + 1
        self.dma_counts[key] = c
        o.dcount = c
        o.fn = lambda e, out=out, in_=in_, kw=kw: e.dma_start(out=out, in_=in_, **kw)
        for r in reads:
            self._add_dep(o, self.last_w.get(r))
        for w in writes:
            self._add_dep(o, self.last_w.get(w))
            for rd in self.readers.get(w, ()):
                self._add_dep(o, rd)
        self._add_dep(o, self.all_dma_last.get(key))
        for r in reads:
            self.readers.setdefault(r, []).append(o)
        for w in writes:
            self.last_w[w] = o
            self.readers[w] = []
        self.all_dma_last[key] = o
        self.streams[eng].append(o)
        return o

    def barrier(self, name=None):
        if name is None:
            name = f"b{len(getattr(self, 'bnames', []))}"
        self.bnames = getattr(self, "bnames", []) + [name]
        lasts = list(self.last_comp.values()) + list(self.all_dma_last.values())
        for e in ENGS:
            o = _Op(e, None)
            o.barrier = True
            for ev in lasts:
                self._add_dep(o, ev)
            self.streams[e].append(o)
        self.last_w = {}
        self.readers = {}
        if getattr(self, "mark_tile", None) is not None:
            mt = self.mark_tile
            val = float(len(self.bnames))
            self.op("pool", lambda e, mt=mt, val=val: e.memset(mt, val), writes=["__mark"])

    def emit(self):
        nc = self.nc
        eng_sigs = {}
        for e in ENGS:
            n = 0
            for o in self.streams[e]:
                if o.is_dma or o.barrier:
                    continue
                if o.signal:
                    n += 1
                    o.sigval = n
            eng_sigs[e] = n
        with ExitStack() as es:
            esem = {}
            for e in ENGS:
                ne = eng_sigs[e] // EPOCH + 1
                esem[e] = [es.enter_context(nc.semaphore(f"s_{e}_{i}")) for i in range(ne)]
            dsem = {}
            for k, c in self.dma_counts.items():
                ne = (c - 1) // DMA_EPOCH + 1
                dsem[k] = [es.enter_context(nc.semaphore(f"d_{len(dsem)}_{i}")) for i in range(ne)]
            block = es.enter_context(nc.Block())

            def emit_stream(engname, engobj):
                for o in self.streams[engname]:
                    for ev in o.deps:
                        if ev.is_dma:
                            ep = (ev.dcount - 1) // DMA_EPOCH
                            engobj.wait_ge(dsem[ev.dkey][ep], 16 * (ev.dcount - ep * DMA_EPOCH))
                        else:
                            ep = (ev.sigval - 1) // EPOCH
                            engobj.wait_ge(esem[ev.eng][ep], ev.sigval - ep * EPOCH)
                    if o.barrier:
                        continue
                    inst = o.fn(engobj)
                    if o.is_dma:
                        ep = (o.dcount - 1) // DMA_EPOCH
                        inst.then_inc(dsem[o.dkey][ep], 16)
                    elif o.signal:
                        ep = (o.sigval - 1) // EPOCH
                        inst.then_inc(esem[o.eng][ep], 1)

            @block.tensor
            def _(t):
                emit_stream("pe", t)

            @block.scalar
            def _(t):
                emit_stream("act", t)

            @block.vector
            def _(t):
                emit_stream("dve", t)

            @block.gpsimd
            def _(t):
                emit_stream("pool", t)

            @block.sync
            def _(t):
                emit_stream("sp", t)

    def stats(self):
        return {e: len(self.streams[e]) for e in ENGS}


class Arena:
    def __init__(self, ap2d, nbytes):
        self.ap = ap2d
        self.nbytes = nbytes
        self.off = 0
        self.marks = []
        self.peak = 0

    def alloc(self, shape_free, dtype, parts=128):
        if isinstance(shape_free, int):
            shape_free = [shape_free]
        esz = 2 if dtype == BF16 else 4
        n = int(np.prod(shape_free))
        nb = n * esz
        self.off = (self.off + 31) // 32 * 32
        assert self.off + nb <= self.nbytes, f"SBUF arena overflow {self.off}+{nb}>{self.nbytes}"
        v = self.ap[0:parts, self.off // 2:(self.off + nb) // 2]
        if dtype != BF16:
            v = v.bitcast(dtype)
        self.off += nb
        self.peak = max(self.peak, self.off)
        if len(shape_free) == 2:
            v = v.rearrange("p (a b) -> p a b", a=shape_free[0])
        elif len(shape_free) == 3:
            v = v.rearrange("p (a b c) -> p a b c", a=shape_free[0], b=shape_free[1])
        elif len(shape_free) == 4:
            v = v.rearrange("p (a b c d) -> p a b c d", a=shape_free[0], b=shape_free[1], c=shape_free[2])
        return v

    def mark(self):
        self.marks.append(self.off)

    def release(self):
        self.off = self.marks.pop()


class B:
    def __init__(self, P):
        self.P = P
        self.uid = 0

    def key(self, s):
        self.uid += 1
        return f"{s}#{self.uid}"

    def mm(self, out, lhsT, rhs, start, stop, reads, writes):
        self.P.op("pe", lambda e: e.matmul(out, lhsT=lhsT, rhs=rhs, start=start, stop=stop),
                  reads=reads, writes=writes, accum=True)

    def tr(self, out, in_, ident, reads, writes):
        self.P.op("pe", lambda e: e.transpose(out, in_, ident), reads=reads, writes=writes, accum=True)

    def act(self, out, in_, func, reads, writes, bias=None, scale=None, accum_out=None, eng="act"):
        kw = {}
        if bias is not None:
            kw["bias"] = bias
        if scale is not None:
            kw["scale"] = scale
        if accum_out is not None:
            kw["accum_out"] = accum_out
        self.P.op("act", lambda e: e.activation(out=out, in_=in_, func=func, **kw), reads=reads, writes=writes)

    def tt(self, eng, out, in0, in1, op, reads, writes):
        self.P.op(eng, lambda e: e.tensor_tensor(out=out, in0=in0, in1=in1, op=op), reads=reads, writes=writes)

    def ts(self, eng, out, in0, s1, s2, op0, op1, reads, writes):
        if s2 is None:
            self.P.op(eng, lambda e: e.tensor_scalar(out=out, in0=in0, scalar1=s1, scalar2=None, op0=op0),
                      reads=reads, writes=writes)
        else:
            self.P.op(eng, lambda e: e.tensor_scalar(out=out, in0=in0, scalar1=s1, scalar2=s2, op0=op0, op1=op1),
                      reads=reads, writes=writes)

    def stt(self, out, in0, scalar, in1, op0, op1, reads, writes):
        self.P.op("dve", lambda e: e.scalar_tensor_tensor(out=out, in0=in0, scalar=scalar, in1=in1, op0=op0, op1=op1),
                  reads=reads, writes=writes)

    def copy(self, eng, out, in_, reads, writes):
        if eng == "act":
            self.P.op("act", lambda e: e.activation(out=out, in_=in_, func=AF.Copy), reads=reads, writes=writes)
        else:
            self.P.op(eng, lambda e: e.tensor_copy(out=out, in_=in_), reads=reads, writes=writes)

    def recip(self, out, in_, reads, writes):
        self.P.op("dve", lambda e: e.reciprocal(out=out, in_=in_), reads=reads, writes=writes)

    def memset(self, eng, out, val, writes):
        self.P.op(eng, lambda e: e.memset(out, val), writes=writes)

    def dma(self, eng, out, in_, reads, writes, key, **kw):
        if re.match(r"^(c|a|f|g|h|fk|dbg)\d+$", key):
            self.misc = getattr(self, "misc", 0) + 1
            key = f"misc{self.misc % 4}"
        self.P.dma(eng, out, in_, reads=reads, writes=writes, key=key, **kw)


def build(debug=False, stop=None):
    nc = bass.Bass("TRN2", target_bir_lowering=False)
    IN = {}

    def din(name, shape, dt=F32):
        IN[name] = nc.dram_tensor(name, list(shape), dt, kind="ExternalInput").ap()
        return IN[name]

    dbg_kind = "ExternalOutput" if debug else "Internal"

    def dscr(name, shape, dt):
        return nc.dram_tensor(name, list(shape), dt, kind=dbg_kind).ap()

    x_in = din("x", [SEQ, D])
    ctx_in = din("ctx", [CTX, D])
    cc_in = din("cc", [128, 8, 2])
    ev_mod_w = din("ev_mod_w", [D, 3 * D]); ev_mod_bf = din("ev_mod_bf", [128, 24]); ev_mod_bg = din("ev_mod_bg", [1, D])
    ev_norm_g = din("ev_norm_g", [128, 8])
    ev_in_w = din("ev_in_w", [D, 7168])
    ev_lam = din("ev_lam", [1, 256])
    ev_subln = din("ev_subln", [128, 1])
    ev_dw_w = din("ev_dw_w", [128, 8, 31]); ev_dw_b = din("ev_dw_b", [128, 8])
    ev_cln_g = din("ev_cln_g", [128, 8]); ev_cln_b = din("ev_cln_b", [128, 8])
    ev_out_w = din("ev_out_w", [2048, D])
    od_mod_w = din("od_mod_w", [D, 3 * D]); od_mod_bf = din("od_mod_bf", [128, 24]); od_mod_bg = din("od_mod_bg", [1, D])
    od_norm_g = din("od_norm_g", [128, 8])
    od_in_w = din("od_in_w", [D, 2048])
    s5_lre = din("s5_lre", [128, 64]); s5_lim = din("s5_lim", [128, 64]); s5_ldt = din("s5_ldt", [128, 64])
    s5_bre = din("s5_bre", [128, 64, 16]); s5_bim = din("s5_bim", [128, 64, 16])
    s5_cre = din("s5_cre", [128, 64, 16]); s5_cim = din("s5_cim", [128, 64, 16])
    s5_d = din("s5_d", [128, 64])
    od_glu_w = din("od_glu_w", [D, D]); od_glu_b = din("od_glu_b", [128, 8]); od_out_w = din("od_out_w", [D, D])
    final_g = din("final_g", [1, D])
    c_cos = din("c_cos", [128, SEQ]); c_sin = din("c_sin", [128, SEQ])
    c_perm = din("c_perm", [128, 128], BF16); c_ident = din("c_ident", [128, 128], BF16)
    c_zsel = din("c_zsel", [128, 8, 240], BF16); c_qsel = din("c_qsel", [128, 8, 240], BF16)
    c_maskf = din("c_maskf", [128, 128]); c_maskb = din("c_maskb", [128, 128])
    c_kvec = din("c_kvec", [128, 17])

    out_ap = nc.dram_tensor("out", [SEQ, D], F32, kind="ExternalOutput").ap()

    d_q = dscr("d_q", [8, 128, NTOK], BF16)
    d_k = dscr("d_k", [8, 128, NTOK], BF16)
    d_v = dscr("d_v", [NTOK, D], BF16)
    d_sga = dscr("d_sga", [8, 128, NTOK], BF16)
    d_h = dscr("d_h", [8, 128, NTOK], BF16)
    d_sgb = dscr("d_sgb", [8, 128, NTOK], BF16)
    d_y = dscr("d_y", [8, 128, NTOK], F32)
    d_mix = dscr("d_mix", [16, 128, NTOK], BF16)
    d_x1 = dscr("d_x1", [NTOK, D], F32)
    d_u = dscr("d_u", [8, 128, NTOK], BF16)
    d_sz = dscr("d_sz", [8, 128, SEQ], BF16)
    d_g = dscr("d_g", [8, 128, SEQ], BF16)
    d_kt = dscr("d_kt", [64, 128, 128], BF16)
    d_wbt = dscr("d_wbt", [32, 128, 2, 2, 128], BF16)
    d_wd = dscr("d_wd", [32, 128, 2, 2, 128], BF16)
    d_ush = dscr("d_ush", [64, 128, 544], BF16)

    es = ExitStack()
    ARENA_BYTES = 207 * 1024
    arena_t = es.enter_context(nc.sbuf_tensor("arena", [128, ARENA_BYTES // 2], BF16))
    ps_t = es.enter_context(nc.psum_tensor("ps", [128, 8, 512], F32))
    ar = Arena(arena_t[:, :], ARENA_BYTES)
    P = Prog(nc)
    b = B(P)
    if debug == "mark":
        P.mark_tile = es.enter_context(nc.sbuf_tensor("phasemark", [128, 1], F32))[:, :]

    def PS(i):
        return ps_t[:, i, :]

    dbg_n = [0]

    def dbg(name, ap, shape, dt=F32, keys=()):
        if not debug:
            return
        o = nc.dram_tensor("dbg_" + name, list(shape), dt, kind="ExternalOutput").ap()
        dbg_n[0] += 1
        b.dma("sp", o, ap, list(keys), [], f"dbg{dbg_n[0]}")

    def done():
        P.barrier()
        P.emit()
        es.close()
        return nc, P, ar

    def PSB(i):
        return ps_t[:, i, :].bitcast(BF16)

    ident = ar.alloc([128], BF16)
    perm = ar.alloc([128], BF16)
    ones_bf = ar.alloc([128], BF16)
    ones_f = ar.alloc([128], F32)
    b.dma("sp", ident, c_ident, [], ["ident"], "c1")
    b.dma("sp", perm, c_perm, [], ["perm"], "c2")
    b.memset("pool", ones_bf, 1.0, ["ones_bf"])
    b.memset("pool", ones_f, 1.0, ["ones_f"])
    modA = [ar.alloc([8, 2], F32) for _ in range(2)]
    modB = [ar.alloc([8, 2], F32) for _ in range(2)]
    gtrow = [[ar.alloc([D], F32) for _ in range(2)] for _ in range(2)]
    neglam = ar.alloc([1], F32)
    gsub = ar.alloc([1], F32)
    eps_t = ar.alloc([1], F32)
    b.memset("pool", eps_t, EPS, ["eps_t"])

    ar.mark()
    cc = ar.alloc([8, 2], F32)
    b.dma("sp", cc, cc_in, [], ["cc"], "a1")
    scb = ar.alloc([8, 2], F32)
    b.act(scb, cc, AF.Silu, ["cc"], ["scb"])
    screp = ar.alloc([8, 2, 128], F32)
    b.copy("dve", screp, scb.unsqueeze(3).to_broadcast([128, 8, 2, 128]), ["scb"], ["screp"])
    Wb = ar.alloc([8, 3 * D], F32)
    for L, (mw, mbf, mbg, ng) in enumerate([(ev_mod_w, ev_mod_bf, ev_mod_bg, ev_norm_g),
                                            (od_mod_w, od_mod_bf, od_mod_bg, od_norm_g)]):
        for kc in range(8):
            b.dma("sp", Wb[:, kc, :], mw[kc * 128:(kc + 1) * 128, :], [], [f"Wb{kc}"], f"mstg{kc}")
        wkeys = [f"Wb{kc}" for kc in range(8)]
        for fc in range(24):
            for kc in range(8):
                b.mm(PS(0)[:, fc * 2:fc * 2 + 2], Wb[:, kc, fc * 128:(fc + 1) * 128], scb[:, kc, :],
                     kc == 0, kc == 7, ["scb"] + wkeys, ["psA0"])
        mbf_t = ar.alloc([24], F32)
        b.dma("sp", mbf_t, mbf, [], ["mbf_t"], "a2")
        modT = ar.alloc([24, 2], F32)
        b.tt("dve", modT, PS(0)[:, 0:48].rearrange("p (f c) -> p f c", c=2),
             mbf_t.unsqueeze(2).to_broadcast([128, 24, 2]), ALU.add, ["psA0", "mbf_t"], ["modT"])
        ng_t = ar.alloc([8], F32)
        b.dma("sp", ng_t, ng, [], ["ng_t"], "a3")
        tmpA = ar.alloc([8, 2], F32)
        b.ts("dve", tmpA, modT[:, 8:16, :], 1.0, None, ALU.add, None, ["modT"], ["tmpA"])
        b.tt("dve", modA[L], tmpA, ng_t.unsqueeze(2).to_broadcast([128, 8, 2]), ALU.mult, ["tmpA", "ng_t"], [f"modA{L}"])
        b.copy("dve", modB[L], modT[:, 0:8, :], ["modT"], [f"modB{L}"])
        mbg_t = ar.alloc([D], F32)
        b.dma("sp", mbg_t, mbg.partition_broadcast(128), [], ["mbg_t"], "a4")
        for col in range(2):
            for half in range(2):
                for kc in range(8):
                    b.mm(PS(1 + half), screp[:, kc, col, :], Wb[:, kc, 2048 + half * 512:2048 + (half + 1) * 512],
                         kc == 0, kc == 7, ["screp"] + wkeys, [f"psA{1 + half}"])
                b.tt("dve", gtrow[L][col][:, half * 512:(half + 1) * 512], PS(1 + half), mbg_t[:, half * 512:(half + 1) * 512],
                     ALU.add, [f"psA{1 + half}", "mbg_t"], [f"gtrow{L}{col}"])
    lam_t = ar.alloc([256], F32)
    b.dma("sp", lam_t, ev_lam.partition_broadcast(128), [], ["lam_t"], "a5")
    lprod = ar.alloc([2, 64], F32)
    lam_v = lam_t.rearrange("p (a b) -> p a b", a=4)
    b.tt("dve", lprod[:, 0, :], lam_v[:, 0, :], lam_v[:, 1, :], ALU.mult, ["lam_t"], ["lprod0"])
    b.tt("dve", lprod[:, 1, :], lam_v[:, 2, :], lam_v[:, 3, :], ALU.mult, ["lam_t"], ["lprod1"])
    lsum = ar.alloc([2], F32)
    P.op("dve", lambda e: e.reduce_sum(out=lsum, in_=lprod, axis=mybir.AxisListType.X), reads=["lprod0", "lprod1"], writes=["lsum"])
    lexp = ar.alloc([2], F32)
    b.act(lexp, lsum, AF.Exp, ["lsum"], ["lexp"])
    LAM_INIT0 = 0.8 - 0.6 * math.exp(-0.3 * 0)
    b.tt("dve", neglam, lexp[:, 1:2], lexp[:, 0:1], ALU.subtract, ["lexp"], ["neglam_a"])
    b.ts("dve", neglam, neglam, -LAM_INIT0, None, ALU.add, None, ["neglam_a"], ["neglam"])
    sub_t = ar.alloc([1], F32)
    b.dma("sp", sub_t, ev_subln, [], ["sub_t"], "a6")
    b.ts("dve", gsub, sub_t, 1.0 - LAM_INIT0, None, ALU.mult, None, ["sub_t"], ["gsub"])
    P.barrier()
    for L in range(2):
        dbg(f"modA{L}", modA[L], [128, 8, 2]); dbg(f"modB{L}", modB[L], [128, 8, 2])
        for col in range(2):
            dbg(f"gtrow{L}{col}", gtrow[L][col], [128, D])
    dbg("neglam", neglam, [128, 1]); dbg("gsub", gsub, [128, 1])
    P.barrier()
    ar.release()

    if stop == 'A':
        return done()
    def prologue(L, src_ctx, src_x, hT):
        ar.mark()
        xs = [ar.alloc([D], F32) for _ in range(4)]
        xnb = [ar.alloc([D], BF16) for _ in range(4)]
        junk = ar.alloc([D], BF16)
        ss = [ar.alloc([1], F32) for _ in range(4)]
        sd = [ar.alloc([1], F32) for _ in range(4)]
        rs = [ar.alloc([1], F32) for _ in range(4)]
        evt = [ar.alloc([8, 128], F32) for _ in range(2)]
        hkeys_ = [f"hT{fb}" for fb in range(8)]
        def stage_a(t):
            i = t % 4
            src = src_ctx[t * 128:(t + 1) * 128, :] if t < 2 else src_x[(t - 2) * 128:(t - 1) * 128, :]
            b.dma("sp", xs[i], src, [], [f"xs{i}"], f"pxs{i}")
            b.act(junk, xs[i], AF.Square, [f"xs{i}"], ["junk", f"ss{i}"], accum_out=ss[i])
            b.act(sd[i], ss[i], AF.Sqrt, [f"ss{i}"], [f"sd{i}"], scale=1.0 / D, bias=eps_t)
            b.recip(rs[i], sd[i], [f"sd{i}"], [f"rs{i}"])

        def stage_b(t):
            i = t % 4
            col = 1 if t < 2 else 0
            b.act(xnb[i], xs[i], AF.Copy, [f"xs{i}", f"rs{i}"], [f"xnb{i}"], scale=rs[i])
            pb = PSB(i)
            for fb in range(8):
                b.tr(pb[:, fb * 128:(fb + 1) * 128], xnb[i][:, fb * 128:(fb + 1) * 128], ident,
                     [f"xnb{i}", "ident"], [f"ps{i}"])
            pv = pb.rearrange("p (f t) -> p f t", f=8)
            b.tt("dve", evt[i % 2], pv, modA[L][:, :, col].unsqueeze(2).to_broadcast([128, 8, 128]), ALU.mult,
                 [f"ps{i}", f"modA{L}"], [f"evt{i % 2}"])
            b.tt("dve", hT[:, :, t * 128:(t + 1) * 128], evt[i % 2], modB[L][:, :, col].unsqueeze(2).to_broadcast([128, 8, 128]),
                 ALU.add, [f"evt{i % 2}", f"modB{L}"], hkeys_)

        stage_a(0); stage_a(1)
        for t in range(34):
            if t + 2 < 34:
                stage_a(t + 2)
            stage_b(t)
        P.barrier()
        ar.release()

    hkeys = [f"hT{fb}" for fb in range(8)]

    TT = [(0, 256)] + [(256 + i * 512, 512) for i in range(8)]

    ar.mark()
    hT = ar.alloc([8, NTOK], BF16)
    prologue(0, ctx_in, x_in, hT)
    if stop == 'B':
        P.barrier()
        dbg("hT", hT, [128, 8, NTOK], BF16)

    if stop == 'B':
        return done()
    ar.mark()
    cos_t = ar.alloc([SEQ], F32)
    sin_t = ar.alloc([SEQ], F32)
    b.dma("sp", cos_t, c_cos, [], ["cos_t"], "c3")
    b.dma("sp", sin_t, c_sin, [], ["sin_t"], "c4")
    wst = [ar.alloc([8, 128], F32) for _ in range(2)]
    wbf = [ar.alloc([8, 128], BF16) for _ in range(4)]
    ostg = [ar.alloc([NTOK], BF16) for _ in range(2)]
    qb = [ar.alloc([512], BF16) for _ in range(2)]
    r1 = [ar.alloc([512], F32) for _ in range(2)]
    r2 = [ar.alloc([512], F32) for _ in range(2)]
    sg = [ar.alloc([512], F32) for _ in range(2)]
    cnt = {"w": 0, "o": 0, "ps": 0, "t": 0}

    def proj_fm(wslot_keys, wtile, ps_idx, t0, n):
        for kc in range(8):
            b.mm(PS(ps_idx)[:, 0:n], wtile[:, kc, :], hT[:, kc, t0:t0 + n], kc == 0, kc == 7,
                 hkeys + wslot_keys, [f"ps{ps_idx}"])

    def load_w(c0):
        i = cnt["w"]; cnt["w"] += 1
        s2, s4 = i % 2, i % 4
        b.dma("sp", wst[s2], ev_in_w[:, c0:c0 + 128].rearrange("(kc p) f -> p kc f", p=128), [], [f"wstc{s2}"], f"wstc{s2}")
        b.copy("dve" if i % 2 == 0 else "act", wbf[s4], wst[s2], [f"wstc{s2}"], [f"wbfc{s4}"])
        return wbf[s4], [f"wbfc{s4}"]

    PROJ_BANKS = [0, 1, 4, 5]
    for which, dst in ((0, d_q), (1, d_k)):
        for h in range(8):
            wt, wk = load_w(which * 1024 + h * 128)
            oi = cnt["o"] % 2; cnt["o"] += 1
            og = ostg[oi]
            pend = None
            for (t0, n) in TT:
                pi = PROJ_BANKS[cnt["ps"] % 4]; cnt["ps"] += 1
                proj_fm(wk, wt, pi, t0, n)
                if pend is not None:
                    pend(); pend = None
                if t0 == 0:
                    b.copy("act", og[:, 0:n], PS(pi)[:, 0:n], [f"ps{pi}"], [f"ostg{oi}"])
                else:
                    ti = cnt["t"] % 2; cnt["t"] += 1
                    b.copy("act", qb[ti], PS(pi), [f"ps{pi}"], [f"qb{ti}"])

                    def rope(ti=ti, t0=t0, og=og, oi=oi):
                        b.mm(PS(2 + ti), perm, qb[ti], True, True, ["perm", f"qb{ti}"], [f"ps{2 + ti}"])
                        s0 = t0 - CTX
                        b.tt("dve", r1[ti], qb[ti], cos_t[:, s0:s0 + 512], ALU.mult, [f"qb{ti}", "cos_t"], [f"r1{ti}"])
                        b.tt("dve", r2[ti], PS(2 + ti), sin_t[:, s0:s0 + 512], ALU.mult, [f"ps{2 + ti}", "sin_t"], [f"r2{ti}"])
                        b.tt("dve", og[:, t0:t0 + 512], r1[ti], r2[ti], ALU.add, [f"r1{ti}", f"r2{ti}"], [f"ostg{oi}"])
                    pend = rope
            if pend is not None:
                pend(); pend = None
            b.dma("pool", dst[h], og, [f"ostg{oi}"], [], f"ost{oi}")
    for base, dst in ((3072, d_sga), (6144, d_sgb)):
        for fb in range(8):
            wt, wk = load_w(base + fb * 128)
            oi = cnt["o"] % 2; cnt["o"] += 1
            og = ostg[oi]
            for (t0, n) in TT:
                pi = PROJ_BANKS[cnt["ps"] % 4]; cnt["ps"] += 1
                proj_fm(wk, wt, pi, t0, n)
                b.act(og[:, t0:t0 + n], PS(pi)[:, 0:n], AF.Silu, [f"ps{pi}"], [f"ostg{oi}"])
            b.dma("pool", dst[fb], og, [f"ostg{oi}"], [], f"ost{oi}")
    gc_ = 0
    for fb in range(8):
        wa, wka = load_w(4096 + fb * 128)
        wb_, wkb = load_w(5120 + fb * 128)
        oi = cnt["o"] % 2; cnt["o"] += 1
        og = ostg[oi]
        for (t0, n) in TT:
            ba = 4 + 2 * (gc_ % 2); bb_ = 5 + 2 * (gc_ % 2); gc_ += 1
            proj_fm(wka, wa, ba, t0, n)
            proj_fm(wkb, wb_, bb_, t0, n)
            ti = cnt["t"] % 2; cnt["t"] += 1
            b.act(sg[ti][:, 0:n], PS(bb_)[:, 0:n], AF.Sigmoid, [f"ps{bb_}"], [f"sg{ti}"])
            b.tt("dve", og[:, t0:t0 + n], PS(ba)[:, 0:n], sg[ti][:, 0:n], ALU.mult, [f"ps{ba}", f"sg{ti}"], [f"ostg{oi}"])
        b.dma("pool", d_h[fb], og, [f"ostg{oi}"], [], f"ost{oi}")
    P.barrier()
    ar.release()
    ar.mark()
    wvs = ar.alloc([8, 512], F32)
    wv = ar.alloc([8, D], BF16)
    for half in range(2):
        b.dma("sp", wvs, ev_in_w[:, 2048 + half * 512:2048 + (half + 1) * 512].rearrange("(kc p) f -> p kc f", p=128),
              [], ["wvs"], "wvs")
        b.copy("dve", wv[:, :, half * 512:(half + 1) * 512], wvs, ["wvs"], [f"wv{half}"])
    vst = [ar.alloc([D], BF16) for _ in range(2)]
    for t in range(34):
        i = t % 2
        for half in range(2):
            for kc in range(8):
                b.mm(PS(6 + half), hT[:, kc, t * 128:(t + 1) * 128], wv[:, kc, half * 512:(half + 1) * 512],
                     kc == 0, kc == 7, hkeys + [f"wv{half}"], [f"ps{6 + half}"])
            b.copy("act" if half == 0 else "dve", vst[i][:, half * 512:(half + 1) * 512], PS(6 + half),
                   [f"ps{6 + half}"], [f"vst{i}"])
        b.dma("pool", d_v[t * 128:(t + 1) * 128, :], vst[i], [f"vst{i}"], [], f"vst{i}")
    P.barrier()
    ar.release()
    ar.release()
    if stop == 'C1':
        return done()
    ar.mark()
    qz = [[ar.alloc([NTOK], BF16) for _ in range(2)] for _ in range(2)]
    for i_ in range(2):
        b.memset("pool", qz[i_][0][64:128, :], 0.0, [f"qz{i_}0"])
        b.memset("pool", qz[i_][1][0:64, :], 0.0, [f"qz{i_}1"])
    kT = [ar.alloc([NTOK], BF16) for _ in range(2)]
    vh = [ar.alloc([34, 128], BF16) for _ in range(2)]
    sga = [ar.alloc([NTOK], BF16) for _ in range(2)]
    mixo = [ar.alloc([NTOK], BF16) for _ in range(2)]
    NE = 8
    E = [ar.alloc([512], BF16) for _ in range(NE)]
    EP = [ar.alloc([512], BF16) for _ in range(6)]
    EQ = [ar.alloc([512], BF16) for _ in range(4)]
    R = [ar.alloc([512], F32) for _ in range(2)]
    Osb = [ar.alloc([512], F32) for _ in range(2)]
    o1 = ar.alloc([512], F32); oo = ar.alloc([512], F32)
    sq = ar.alloc([512], BF16); sdn = ar.alloc([512], F32); rsn = ar.alloc([512], F32); an = ar.alloc([512], F32)
    gcnt = {"s": 0, "e": 0, "p": 0, "q": 0}
    for h in range(8):
        hi_ = h % 2
        b.dma("sp", qz[hi_][0][0:64, :], d_q[h][0:64, :], [], [f"qz{hi_}0"], f"aq{hi_}0")
        b.dma("sp", qz[hi_][1][64:128, :], d_q[h][64:128, :], [], [f"qz{hi_}1"], f"aq{hi_}1")
        b.dma("sp", kT[hi_], d_k[h], [], [f"kT{hi_}"], f"ak{hi_}")
        b.dma("sp", vh[hi_], d_v[:, h * 128:(h + 1) * 128].rearrange("(kt p) f -> p kt f", p=128), [], [f"vh{hi_}"], f"av{hi_}")
        b.dma("sp", sga[hi_], d_sga[h], [], [f"sga{hi_}"], f"ag{hi_}")
        for (t0, n) in TT:
            kts = list(range(2)) if t0 == 0 else list(range(34))
            items = [(kt, c) for kt in kts for c in range(2)]
            sb = {}
            eb = {}
            zq = []
            pend = [None, None]
            zstarted = [False, False]

            def emit_S(idx):
                kt, c = items[idx]
                si = gcnt["s"] % 4; gcnt["s"] += 1
                sb[idx] = si
                b.mm(PS(si)[:, 0:n], kT[hi_][:, kt * 128:(kt + 1) * 128],
                     qz[hi_][c][:, t0:t0 + n], True, True, [f"qz{hi_}{c}", f"kT{hi_}"], [f"ps{si}"])

            LOOK = 3
            for idx in range(min(LOOK, len(items))):
                emit_S(idx)
            for idx, (kt, c) in enumerate(items):
                if idx + LOOK < len(items):
                    emit_S(idx + LOOK)
                si = sb[idx]
                ei = gcnt["e"] % NE; gcnt["e"] += 1
                eb[idx] = ei
                b.act(E[ei][:, 0:n], PS(si)[:, 0:n], AF.Exp, [f"ps{si}"], [f"E{ei}"], scale=0.125)
                first = kt == kts[0]; last = kt == kts[-1]
                b.mm(PS(4 + c)[:, 0:n], vh[hi_][:, kt, :], E[ei][:, 0:n], first, last, [f"vh{hi_}", f"E{ei}"], [f"ps{4 + c}"])
                if kt % 2 == 1:
                    e_prev = eb[idx - 2]
                    pi_ = gcnt["p"] % 6; gcnt["p"] += 1
                    b.tt("dve", EP[pi_][:, 0:n], E[e_prev][:, 0:n], E[ei][:, 0:n], ALU.add, [f"E{e_prev}", f"E{ei}"], [f"EP{pi_}"])
                    if pend[c] is None and not last:
                        pend[c] = pi_
                    else:
                        if pend[c] is not None:
                            qi_ = gcnt["q"] % 4; gcnt["q"] += 1
                            b.tt("dve", EQ[qi_][:, 0:n], EP[pend[c]][:, 0:n], EP[pi_][:, 0:n], ALU.add,
                                 [f"EP{pend[c]}", f"EP{pi_}"], [f"EQ{qi_}"])
                            src, skey = EQ[qi_], f"EQ{qi_}"
                            pend[c] = None
                        else:
                            src, skey = EP[pi_], f"EP{pi_}"
                        zq.append((idx, c, src, skey, not zstarted[c], last))
                        zstarted[c] = True
                while zq and (zq[0][0] + 3 <= idx or idx == len(items) - 1):
                    _, c_, src_, sk_, f_, l_ = zq.pop(0)
                    b.mm(PS(6 + c_)[:, 0:n], ones_bf, src_[:, 0:n], f_, l_, ["ones_bf", sk_], [f"ps{6 + c_}"])
            b.copy("act", Osb[0][:, 0:n], PS(4)[:, 0:n], ["ps4"], ["Osb0"])
            b.copy("dve", Osb[1][:, 0:n], PS(5)[:, 0:n], ["ps5"], ["Osb1"])
            b.recip(R[0][:, 0:n], PS(6)[:, 0:n], ["ps6"], ["R0"])
            b.recip(R[1][:, 0:n], PS(7)[:, 0:n], ["ps7"], ["R1"])
            b.tt("dve", Osb[0][:, 0:n], Osb[0][:, 0:n], R[0][:, 0:n], ALU.mult, ["Osb0", "R0"], ["Osb0"])
            b.tt("dve", o1[:, 0:n], Osb[1][:, 0:n], R[1][:, 0:n], ALU.mult, ["Osb1", "R1"], ["o1"])
            b.stt(oo[:, 0:n], o1[:, 0:n], neglam, Osb[0][:, 0:n], ALU.mult, ALU.add, ["Osb0", "o1", "neglam"], ["oo"])
            b.tt("dve", sq[:, 0:n], oo[:, 0:n], oo[:, 0:n], ALU.mult, ["oo"], ["sq"])
            sN = gcnt["s"] % 4; gcnt["s"] += 1
            b.mm(PS(sN)[:, 0:n], ones_bf, sq[:, 0:n], True, True, ["ones_bf", "sq"], [f"ps{sN}"])
            b.ts("dve", sdn[:, 0:n], PS(sN)[:, 0:n], 1.0 / 128, EPS, ALU.mult, ALU.add, [f"ps{sN}"], ["sdn"])
            b.act(rsn[:, 0:n], sdn[:, 0:n], AF.Ln, ["sdn"], ["rsn"])
            b.act(rsn[:, 0:n], rsn[:, 0:n], AF.Exp, ["rsn"], ["rsn"], scale=-0.5)
            b.tt("dve", an[:, 0:n], oo[:, 0:n], rsn[:, 0:n], ALU.mult, ["oo", "rsn"], ["an"])
            b.stt(mixo[hi_][:, t0:t0 + n], an[:, 0:n], gsub, sga[hi_][:, t0:t0 + n], ALU.mult, ALU.mult,
                  ["an", "gsub", f"sga{hi_}"], [f"mixo{hi_}"])
        b.dma("pool", d_mix[h], mixo[hi_], [f"mixo{hi_}"], [], f"amix{hi_}")
    P.barrier()
    ar.release()
    if stop == 'C2':
        return done()
    ar.mark()
    dww = ar.alloc([8, 31], F32); dwb = ar.alloc([8], F32)
    b.dma("sp", dww, ev_dw_w, [], ["dww"], "c5")
    b.dma("sp", dwb, ev_dw_b, [], ["dwb"], "c6")
    identf2 = ar.alloc([128], F32)
    b.copy("dve", identf2, ident, ["ident"], ["identf2"])
    dg = [ar.alloc([31, 128], BF16) for _ in range(2)]
    PADW = 15 + CTX + 30 + SEQ + 15
    XOFF = 15 + CTX + 30
    hp = [ar.alloc([PADW], BF16) for _ in range(2)]
    for i_ in range(2):
        b.memset("pool", hp[i_], 0.0, [f"hp{i_}"])
    yc = [ar.alloc([NTOK], F32) for _ in range(2)]
    ccnt = 0
    DVE_TAPS = [0, 1, 2, 3, 4, 26, 27, 28, 29, 30]
    PE_TAPS = [k for k in range(31) if k not in DVE_TAPS]
    ykeys = []
    for fb in range(8):
        i = fb % 2
        b.dma("sp", hp[i][:, 15:15 + CTX], d_h[fb][:, 0:CTX], [], [f"hp{i}"], f"chc{i}")
        b.dma("sp", hp[i][:, XOFF:XOFF + SEQ], d_h[fb][:, CTX:NTOK], [], [f"hp{i}"], f"chx{i}")
        for k in range(31):
            b.ts("dve", dg[i][:, k, :], identf2, dww[:, fb, k:k + 1], None, ALU.mult, None, ["identf2", "dww"], [f"dg{i}"])
        for (t0, n) in TT:
            base = 15 if t0 == 0 else XOFF + (t0 - CTX)
            pi2 = ccnt % 2; ccnt += 1
            for kk, k in enumerate(PE_TAPS):
                b.mm(PS(pi2)[:, 0:n], dg[i][:, k, :], hp[i][:, base + k - 15:base + k - 15 + n], kk == 0, kk == len(PE_TAPS) - 1,
                     [f"dg{i}", f"hp{i}"], [f"ps{pi2}"])
            yk_ = f"yc{i}_{t0}"
            b.act(yc[i][:, t0:t0 + n], PS(pi2)[:, 0:n], AF.Identity, [f"ps{pi2}", "dwb"], [yk_], bias=dwb[:, fb:fb + 1])
            for k in DVE_TAPS:
                b.stt(yc[i][:, t0:t0 + n], hp[i][:, base + k - 15:base + k - 15 + n], dww[:, fb, k:k + 1], yc[i][:, t0:t0 + n],
                      ALU.mult, ALU.add, [f"hp{i}", "dww", yk_], [yk_])
            ykeys.append(yk_)
        b.dma("pool", d_y[fb], yc[i], ykeys[-len(TT):], [], f"cyc{i}")
    P.barrier()
    ar.release()
    if stop == 'C3a':
        return done()
    ar.mark()
    clg = ar.alloc([8], F32); clb = ar.alloc([8], F32)
    b.dma("sp", clg, ev_cln_g, [], ["clg"], "c7")
    b.dma("sp", clb, ev_cln_b, [], ["clb"], "c8")
    yl = [ar.alloc([8, 512], F32) for _ in range(2)]
    gl = [ar.alloc([8, 512], BF16) for _ in range(2)]
    ysq = ar.alloc([8, 512], F32)
    mean = [ar.alloc([512], F32) for _ in range(2)]; msq = ar.alloc([512], F32); var = ar.alloc([512], F32)
    sdl = ar.alloc([512], F32); rsl = [ar.alloc([512], F32) for _ in range(2)]
    tAll = ar.alloc([8, 512], F32); tC = ar.alloc([8, 512], F32)
    mo = [ar.alloc([8, 512], BF16) for _ in range(2)]

    def ln_a(ti_):
        t0, n = TT[ti_]
        i = ti_ % 2
        b.dma("sp", yl[i][:, :, 0:n], d_y[:, :, t0:t0 + n].rearrange("f p t -> p f t"), [], [f"yl{i}"], f"lyl{i}")
        b.dma("sp", gl[i][:, :, 0:n], d_sgb[:, :, t0:t0 + n].rearrange("f p t -> p f t"), [], [f"gl{i}"], f"lgl{i}")
        b.act(ysq[:, :, 0:n], yl[i][:, :, 0:n], AF.Square, [f"yl{i}"], ["ysq"])
        for fb in range(8):
            b.mm(PS(0)[:, 0:n], ones_f, yl[i][:, fb, 0:n], fb == 0, fb == 7, ["ones_f", f"yl{i}"], ["ps0"])
        for fb in range(8):
            b.mm(PS(1)[:, 0:n], ones_f, ysq[:, fb, 0:n], fb == 0, fb == 7, ["ones_f", "ysq"], ["ps1"])
        b.ts("dve", mean[i][:, 0:n], PS(0)[:, 0:n], 1.0 / D, None, ALU.mult, None, ["ps0"], [f"mean{i}"])
        b.tt("dve", msq[:, 0:n], mean[i][:, 0:n], mean[i][:, 0:n], ALU.mult, [f"mean{i}"], ["msq"])
        b.stt(var[:, 0:n], PS(1)[:, 0:n], 1.0 / D, msq[:, 0:n], ALU.mult, ALU.subtract, ["ps1", "msq"], ["var"])
        b.act(sdl[:, 0:n], var[:, 0:n], AF.Sqrt, ["var"], ["sdl"], bias=eps_t)
        b.recip(rsl[i][:, 0:n], sdl[:, 0:n], ["sdl"], [f"rsl{i}"])

    def ln_b(ti_):
        t0, n = TT[ti_]
        i = ti_ % 2
        b.tt("dve", tAll[:, :, 0:n], yl[i][:, :, 0:n], mean[i][:, 0:n].unsqueeze(1).to_broadcast([128, 8, n]), ALU.subtract,
             [f"yl{i}", f"mean{i}"], ["tAll"])
        b.tt("dve", tAll[:, :, 0:n], tAll[:, :, 0:n], rsl[i][:, 0:n].unsqueeze(1).to_broadcast([128, 8, n]), ALU.mult,
             ["tAll", f"rsl{i}"], ["tAll"])
        for fb in range(8):
            b.act(tC[:, fb, 0:n], tAll[:, fb, 0:n], AF.Silu, ["tAll", "clg", "clb"], ["tC"],
                  scale=clg[:, fb:fb + 1], bias=clb[:, fb:fb + 1])
        b.tt("dve", mo[i][:, :, 0:n], tC[:, :, 0:n], gl[i][:, :, 0:n], ALU.mult, ["tC", f"gl{i}"], [f"mo{i}"])
        b.dma("pool", d_mix[8:16, :, t0:t0 + n].rearrange("f p t -> p f t"), mo[i][:, :, 0:n], [f"mo{i}"], [], f"lmo{i}")

    ln_a(0)
    for ti_ in range(len(TT)):
        if ti_ + 1 < len(TT):
            ln_a(ti_ + 1)
        ln_b(ti_)
    P.barrier()
    ar.release()
    if stop == 'C3b':
        return done()
    ar.mark()
    wo = ar.alloc([16, D], BF16)
    wos = ar.alloc([4, D], F32)
    for q4 in range(4):
        b.dma("sp", wos, ev_out_w[q4 * 512:(q4 + 1) * 512, :].rearrange("(kc p) f -> p kc f", p=128), [], ["wos"], "wos")
        b.copy("dve" if q4 % 2 == 0 else "act", wo[:, q4 * 4:(q4 + 1) * 4, :], wos, ["wos"], [f"wo{q4}"])
    wokeys = [f"wo{q4}" for q4 in range(4)]
    mt = [ar.alloc([16, 128], BF16) for _ in range(2)]
    xr = [ar.alloc([D], F32) for _ in range(2)]
    xo = [ar.alloc([D], F32) for _ in range(2)]
    tmpo = [ar.alloc([512], F32) for _ in range(2)]
    for t in range(34):
        i = t % 2
        col = 1 if t < 2 else 0
        src = ctx_in[t * 128:(t + 1) * 128, :] if t < 2 else x_in[(t - 2) * 128:(t - 1) * 128, :]
        b.dma("sp", mt[i], d_mix[:, :, t * 128:(t + 1) * 128].rearrange("c p t -> p c t"), [], [f"mt{i}"], f"omt{i}")
        b.dma("sp", xr[i], src, [], [f"xr{i}"], f"oxr{i}")
        for half in range(2):
            for ck in range(16):
                b.mm(PS(half), mt[i][:, ck, :], wo[:, ck, half * 512:(half + 1) * 512], ck == 0, ck == 15,
                     [f"mt{i}"] + wokeys, [f"ps{half}"])
            b.tt("dve", tmpo[half], PS(half), gtrow[0][col][:, half * 512:(half + 1) * 512], ALU.mult,
                 [f"ps{half}", f"gtrow0{col}"], [f"tmpo{half}"])
            b.tt("dve", xo[i][:, half * 512:(half + 1) * 512], tmpo[half], xr[i][:, half * 512:(half + 1) * 512], ALU.add,
                 [f"tmpo{half}", f"xr{i}"], [f"xo{i}"])
        b.dma("pool", d_x1[t * 128:(t + 1) * 128, :], xo[i], [f"xo{i}"], [], f"oxo{i}")
    P.barrier()
    ar.release()

    if stop == 'C4':
        return done()
    ar.mark()
    hT = ar.alloc([8, NTOK], BF16)
    prologue(1, d_x1[0:CTX, :], d_x1[CTX:NTOK, :], hT)
    ar.mark()
    wst = [ar.alloc([8, 128], F32) for _ in range(2)]
    wbf = [ar.alloc([8, 128], BF16) for _ in range(4)]
    ostg = [ar.alloc([NTOK], BF16) for _ in range(2)]
    cnt = {"w": 0, "o": 0, "ps": 0}

    def load_w1(c0):
        i = cnt["w"]; cnt["w"] += 1
        s2, s4 = i % 2, i % 4
        b.dma("sp", wst[s2], od_in_w[:, c0:c0 + 128].rearrange("(kc p) f -> p kc f", p=128), [], [f"wstc{s2}"], f"wste{s2}")
        b.copy("dve" if i % 2 == 0 else "act", wbf[s4], wst[s2], [f"wstc{s2}"], [f"wbfc{s4}"])
        return wbf[s4], [f"wbfc{s4}"]

    for fb in range(16):
        wt, wk = load_w1(fb * 128)
        oi = cnt["o"] % 2; cnt["o"] += 1
        og = ostg[oi]
        for (t0, n) in TT:
            if fb >= 8 and t0 == 0:
                continue
            pi = cnt["ps"] % 2; cnt["ps"] += 1
            for kc in range(8):
                b.mm(PS(pi)[:, 0:n], wt[:, kc, :], hT[:, kc, t0:t0 + n], kc == 0, kc == 7, hkeys + wk, [f"ps{pi}"])
            if fb < 8:
                b.copy("act" if pi == 0 else "dve", og[:, t0:t0 + n], PS(pi)[:, 0:n], [f"ps{pi}"], [f"ostg{oi}"])
            else:
                b.act(og[:, t0:t0 + n], PS(pi)[:, 0:n], AF.Silu, [f"ps{pi}"], [f"ostg{oi}"])
        if fb < 8:
            b.dma("pool", d_u[fb], og, [f"ostg{oi}"], [], f"eo{oi}")
        else:
            b.dma("pool", d_sz[fb - 8], og[:, CTX:NTOK], [f"ostg{oi}"], [], f"eo{oi}")
    P.barrier()
    ar.release()
    ar.release()

    if stop == 'E':
        return done()
    A8r = ar.alloc([2, 32], F32)
    A8i = ar.alloc([2, 32], F32)
    A8in = ar.alloc([2, 32], F32)
    ar.mark()
    Kt = ar.alloc([64, 128], BF16)
    Wbt = ar.alloc([2, 32, 2, 128], BF16)
    Wd = ar.alloc([2, 32, 2, 128], BF16)
    ar.mark()
    lre = ar.alloc([64], F32); lim = ar.alloc([64], F32); ldt = ar.alloc([64], F32)
    b.dma("sp", lre, s5_lre, [], ["lre"], "f1")
    b.dma("sp", lim, s5_lim, [], ["lim"], "f2")
    b.dma("sp", ldt, s5_ldt, [], ["ldt"], "f3")
    kvec = ar.alloc([17], F32)
    b.dma("sp", kvec, c_kvec, [], ["kvec"], "f4")
    maskf = ar.alloc([128], F32); maskb = ar.alloc([128], F32)
    b.dma("sp", maskf, c_maskf, [], ["maskf"], "f5")
    b.dma("sp", maskb, c_maskb, [], ["maskb"], "f6")
    dts = ar.alloc([64], F32)
    b.act(dts, ldt, AF.Exp, ["ldt"], ["dts"])
    xr_ = ar.alloc([64], F32); th = ar.alloc([64], F32)
    b.tt("dve", xr_, lre, dts, ALU.mult, ["lre", "dts"], ["xr_"])
    b.tt("dve", th, lim, dts, ALU.mult, ["lim", "dts"], ["th"])
    PWr = ar.alloc([17, 64], F32); PWi = ar.alloc([17, 64], F32)
    ar.mark()
    ARG = ar.alloc([17, 2, 64], F32)
    b.tt("dve", ARG[:, :, 0, :], th.unsqueeze(1).to_broadcast([128, 17, 64]), kvec.unsqueeze(2).to_broadcast([128, 17, 64]),
         ALU.mult, ["th", "kvec"], ["ARG0"])
    b.ts("dve", ARG[:, :, 1, :], ARG[:, :, 0, :], math.pi / 2, None, ALU.add, None, ["ARG0"], ["ARG1"])
    ARGf = ARG.rearrange("p a b c -> p (a b c)")
    T1 = ar.alloc([17 * 2 * 64], F32)
    ki = ar.alloc([17 * 2 * 64], I32)
    b.ts("dve", T1, ARGf, 1.0 / (2 * math.pi), None, ALU.mult, None, ["ARG0", "ARG1"], ["T1"])
    b.copy("dve", ki, T1, ["T1"], ["ki"])
    b.copy("dve", T1, ki, ["ki"], ["T1"])
    b.stt(T1, T1, -2 * math.pi, ARGf, ALU.mult, ALU.add, ["T1", "ARG0", "ARG1"], ["T1"])
    b.ts("dve", T1, T1, 3.1415925, -3.1415925, ALU.min, ALU.max, ["T1"], ["T1"])
    SC = ar.alloc([17, 2, 64], F32)
    b.act(SC.rearrange("p a b c -> p (a b c)"), T1, AF.Sin, ["T1"], ["SC"])
    MG = ar.alloc([17, 64], F32)
    b.tt("dve", MG, xr_.unsqueeze(1).to_broadcast([128, 17, 64]), kvec.unsqueeze(2).to_broadcast([128, 17, 64]),
         ALU.mult, ["xr_", "kvec"], ["MGa"])
    b.act(MG, MG, AF.Exp, ["MGa"], ["MG"])
    b.tt("dve", PWr, MG, SC[:, :, 1, :], ALU.mult, ["MG", "SC"], ["PWr"])
    b.tt("dve", PWi, MG, SC[:, :, 0, :], ALU.mult, ["MG", "SC"], ["PWi"])
    P.barrier()
    ar.release()
    b.copy("dve", A8r.rearrange("p a b -> p (a b)"), PWr[:, 16, :], ["PWr"], ["A8r"])
    b.copy("dve", A8i.rearrange("p a b -> p (a b)"), PWi[:, 16, :], ["PWi"], ["A8i"])
    b.ts("dve", A8in.rearrange("p a b -> p (a b)"), PWi[:, 16, :], -1.0, None, ALU.mult, None, ["PWi"], ["A8in"])
    ar1 = ar.alloc([64], F32); den = ar.alloc([64], F32); t_a = ar.alloc([64], F32); t_b = ar.alloc([64], F32)
    fr = ar.alloc([64], F32); fi = ar.alloc([64], F32); rden = ar.alloc([64], F32)
    b.ts("dve", ar1, PWr[:, 9, :], -1.0, None, ALU.add, None, ["PWr"], ["ar1"])
    b.tt("dve", den, lre, lre, ALU.mult, ["lre"], ["den_a"])
    b.tt("dve", t_a, lim, lim, ALU.mult, ["lim"], ["t_a"])
    b.tt("dve", den, den, t_a, ALU.add, ["den_a", "t_a"], ["den"])
    b.recip(rden, den, ["den"], ["rden"])
    b.tt("dve", t_a, ar1, lre, ALU.mult, ["ar1", "lre", "den"], ["t_a"])
    b.tt("dve", t_b, PWi[:, 9, :], lim, ALU.mult, ["PWi", "lim"], ["t_b"])
    b.tt("dve", t_a, t_a, t_b, ALU.add, ["t_a", "t_b"], ["t_a"])
    b.tt("dve", fr, t_a, rden, ALU.mult, ["t_a", "rden"], ["fr"])
    b.tt("dve", t_a, PWi[:, 9, :], lre, ALU.mult, ["PWi", "lre", "fr"], ["t_a"])
    b.tt("dve", t_b, ar1, lim, ALU.mult, ["ar1", "lim", "t_a"], ["t_b"])
    b.tt("dve", t_a, t_a, t_b, ALU.subtract, ["t_a", "t_b"], ["t_a"])
    b.tt("dve", fi, t_a, rden, ALU.mult, ["t_a", "rden"], ["fi"])
    bre = ar.alloc([64, 16], F32); bim = ar.alloc([64, 16], F32)
    cre = ar.alloc([64, 16], F32); cim = ar.alloc([64, 16], F32)
    b.dma("sp", bre, s5_bre, [], ["bre"], "f7")
    b.dma("sp", bim, s5_bim, [], ["bim"], "f8")
    b.dma("sp", cre, s5_cre, [], ["cre"], "f9")
    b.dma("sp", cim, s5_cim, [], ["cim"], "f10")
    bbr = ar.alloc([64, 16], F32); bbi = ar.alloc([64, 16], F32)
    v1 = ar.alloc([16, 8, 16], F32); v2 = ar.alloc([16, 8, 16], F32)
    u1 = v1.rearrange("p a b c -> p (a b c)")[:, 0:1024].rearrange("p (a b) -> p a b", a=64)
    u2 = v2.rearrange("p a b c -> p (a b c)")[:, 0:1024].rearrange("p (a b) -> p a b", a=64)
    frb = fr.unsqueeze(2).to_broadcast([128, 64, 16]); fib = fi.unsqueeze(2).to_broadcast([128, 64, 16])
    b.tt("dve", u1, bre, frb, ALU.mult, ["bre", "fr"], ["v1"])
    b.tt("dve", u2, bim, fib, ALU.mult, ["bim", "fi"], ["v2"])
    b.tt("dve", bbr, u1, u2, ALU.subtract, ["v1", "v2"], ["bbr"])
    b.tt("dve", u1, bim, frb, ALU.mult, ["bim", "fr", "bbr"], ["v1"])
    b.tt("dve", u2, bre, fib, ALU.mult, ["bre", "fi", "bbr"], ["v2"])
    b.tt("dve", bbi, u1, u2, ALU.add, ["v1", "v2"], ["bbi"])
    dsh = ar.alloc([64], F32)
    b.dma("sp", dsh, s5_d, [], ["dsh"], "f11")
    identf = ar.alloc([128], F32)
    b.copy("dve", identf, ident, ["ident"], ["identf"])
    Xr = ar.alloc([32, 8, 16], BF16); Xi = ar.alloc([32, 8, 16], BF16)

    def cexp(pw_slice, dr, fac_r, fac_i, fkeys, out_r, out_i, kr, ki_, neg_im):
        for hf in range(2):
            q0 = dr * 32 + hf * 16
            pr = PWr[:, pw_slice, q0:q0 + 16].rearrange("p s q -> p q s").unsqueeze(3).to_broadcast([128, 16, 8, 16])
            pi_ = PWi[:, pw_slice, q0:q0 + 16].rearrange("p s q -> p q s").unsqueeze(3).to_broadcast([128, 16, 8, 16])
            fr_ = fac_r[:, q0:q0 + 16, :].unsqueeze(2).to_broadcast([128, 16, 8, 16])
            fi_ = fac_i[:, q0:q0 + 16, :].unsqueeze(2).to_broadcast([128, 16, 8, 16])
            o_r = out_r[:, hf * 16:(hf + 1) * 16]
            o_i = out_i[:, hf * 16:(hf + 1) * 16]
            b.tt("dve", v1, pr, fr_, ALU.mult, ["PWr"] + fkeys, ["v1"])
            b.tt("dve", v2, pi_, fi_, ALU.mult, ["PWi"] + fkeys, ["v2"])
            b.tt("dve", o_r, v1, v2, ALU.subtract, ["v1", "v2"], [kr])
            b.tt("dve", v1, pr, fi_, ALU.mult, ["PWr", kr] + fkeys, ["v1"])
            b.tt("dve", v2, pi_, fr_, ALU.mult, ["PWi", kr] + fkeys, ["v2"])
            if neg_im:
                b.stt(o_i, v1, -1.0, v2, ALU.mult, ALU.subtract, ["v1", "v2"], [ki_])
            else:
                b.tt("dve", o_i, v1, v2, ALU.add, ["v1", "v2"], [ki_])

    for dr in range(2):
        if dr == 0:
            sX = slice(15, 7, -1); sX2 = slice(7, None, -1); sY = slice(9, 17)
        else:
            sX = slice(8, 16); sX2 = slice(0, 8); sY = slice(16, 8, -1)
        Yr = Wd[:, dr, :, 0, :].rearrange("p q (t h) -> p q t h", t=8)
        Yin = Wd[:, dr, :, 1, :].rearrange("p q (t h) -> p q t h", t=8)
        cexp(sY, dr, cre, cim, ["cre", "cim"], Yr, Yin, f"Wdr{dr}", f"Wdi{dr}", True)
        cexp(sX, dr, bbr, bbi, ["bbr", "bbi"], Xr, Xi, "Xr", "Xi", False)
        for pr_ in range(32):
            for ri, Xs, kx in ((0, Xr, "Xr"), (1, Xi, "Xi")):
                pb = PSB(pr_ % 2)
                col = ri * 128
                b.tr(pb[:, col:col + 128], Xs[:, pr_, :, :].rearrange("p s h -> p (s h)"), ident, [kx, "ident"], [f"ps{pr_ % 2}"])
            b.copy("act" if pr_ % 2 == 0 else "dve", Wbt[:, dr, pr_, :, :].rearrange("p r m -> p (r m)"), PSB(pr_ % 2)[:, 0:256],
                   [f"ps{pr_ % 2}"], ["Wbt"])
        cexp(sX2, dr, bbr, bbi, ["bbr", "bbi"], Xr, Xi, "Xr", "Xi", False)
        tmpk = v1.rearrange("p a b c -> p (a b c)")[:, 0:128]
        for g in range(64):
            pr_, g2 = g // 2, g % 2
            rows = slice(g2 * 64, (g2 + 1) * 64)
            pi2 = 2 + g % 2
            b.mm(PS(pi2)[:, 0:128], Xr[rows, pr_, :, :].rearrange("p s h -> p (s h)"), Wd[rows, dr, pr_, 0, :], True, False,
                 ["Xr", f"Wdr{dr}"], [f"ps{pi2}"])
            b.mm(PS(pi2)[:, 0:128], Xi[rows, pr_, :, :].rearrange("p s h -> p (s h)"), Wd[rows, dr, pr_, 1, :], False, True,
                 ["Xi", f"Wdi{dr}"], [f"ps{pi2}"])
            if dr == 0:
                b.tt("dve", tmpk, PS(pi2)[:, 0:128], maskf, ALU.mult, [f"ps{pi2}", "maskf"], ["v1"])
                b.stt(Kt[:, g, :], identf, dsh[:, g:g + 1], tmpk, ALU.mult, ALU.add, ["identf", "dsh", "v1"], ["Kt"])
            else:
                b.tt("dve", tmpk, PS(pi2)[:, 0:128], maskb, ALU.mult, [f"ps{pi2}", "maskb"], ["v1"])
                b.tt("dve", Kt[:, g, :], Kt[:, g, :], tmpk, ALU.add, ["Kt", "v1"], ["Kt"])
    P.barrier()
    ar.release()
    b.dma("sp", d_kt.rearrange("g p m -> p g m"), Kt, ["Kt"], [], "fk1")
    b.dma("sp", d_wbt.rearrange("q p d r m -> p d q r m"), Wbt, ["Wbt"], [], "fk2")
    b.dma("sp", d_wd.rearrange("q p d r m -> p d q r m"), Wd, ["Wdr0", "Wdi0", "Wdr1", "Wdi1"], [], "fk3")
    P.barrier()
    ar.release()

    if stop == 'F':
        return done()
    ar.mark()
    zsel = ar.alloc([8, 240], BF16); qsel = ar.alloc([8, 240], BF16)
    b.dma("sp", zsel, c_zsel, [], ["zsel"], "g1")
    b.dma("sp", qsel, c_qsel, [], ["qsel"], "g2")
    NJ = 544
    S = ar.alloc([2, 2, 32, NJ], BF16)
    AR2 = ar.alloc([2, 2, 32], F32); AIS = ar.alloc([2, 2, 32], F32)
    Z = [ar.alloc([2, 2, 32], F32) for _ in range(2)]
    ta = ar.alloc([2, 2, 32], F32); tb = ar.alloc([2, 2, 32], F32)
    for ri in range(2):
        b.copy("dve", AR2[:, ri, :, :], A8r, ["A8r"], ["AR2"])
    b.copy("dve", AIS[:, 0, :, :], A8in, ["A8in"], ["AIS"])
    b.copy("dve", AIS[:, 1, :, :], A8i, ["A8i"], ["AIS"])
    ar.mark()
    Uf1 = ar.alloc([NTOK], BF16)
    Uf = [Uf1, Uf1]
    wbtf = [ar.alloc([4, 512], BF16) for _ in range(2)]
    Ushp = [ar.alloc([2, NJ], BF16) for _ in range(2)]
    Udi = [ar.alloc([8, NJ], BF16) for _ in range(2)]
    pscnt = 0
    g1pend = [None]
    for fb in range(8):
        fi_ = fb % 2
        b.dma("sp", Uf[fi_], d_u[fb], [], ["Uf"], "gUf")
        b.dma("sp", wbtf[fi_], d_wbt[fb * 4:(fb + 1) * 4].rearrange("q p d r m -> p q (d r m)"), [], [f"wbtf{fi_}"], f"gwb{fi_}")
        Ud = Udi[fi_]
        b.copy("act", Ud[:, :, 32:NJ], Uf[fi_][:, CTX:NTOK].rearrange("p (j s) -> p s j", s=8), ["Uf"], [f"Ud{fi_}"])
        b.copy("dve", Ud[:, :, 0:32], Uf[fi_][:, 0:CTX].rearrange("p (j s) -> p s j", s=8), ["Uf"], [f"Ud{fi_}"])
        Ux = Ud[:, :, 32:NJ]
        Uc = Ud[:, :, 0:32]
        for pl in range(4):
            pr_ = fb * 4 + pl
            ui = pr_ % 2
            ub = Ushp[ui]
            for g2 in range(2):
                gl_ = pl * 2 + g2
                g = fb * 8 + gl_
                pi2 = pscnt % 2; pscnt += 1
                for s_ in range(8):
                    lw = zsel[:, gl_, (7 - s_) * 16:(7 - s_) * 16 + 128]
                    b.mm(PS(pi2), lw, Ux[:, s_, :], s_ == 0, s_ == 7, ["zsel", f"Ud{fi_}"], [f"ps{pi2}"])
                for s_ in range(8):
                    lw = zsel[:, gl_, (7 - s_) * 16:(7 - s_) * 16 + 128]
                    b.mm(PS(2 + pi2)[:, 0:32], lw, Uc[:, s_, :], s_ == 0, s_ == 7, ["zsel", f"Ud{fi_}"], [f"ps{2 + pi2}"])
                b.copy("act", ub[:, g2, 32:NJ], PS(pi2), [f"ps{pi2}"], [f"ub{ui}"])
                b.copy("dve", ub[:, g2, 0:32], PS(2 + pi2)[:, 0:32], [f"ps{2 + pi2}"], [f"ub{ui}"])
                b.dma("pool", d_ush[g], ub[:, g2, :], [f"ub{ui}"], [], f"gus{ui}{g2}")
            def chunk_states(pl=pl, pr_=pr_, ui=ui, ub=ub, fi_=fi_):
                nonlocal pscnt
                for dr in range(2):
                    for ri in range(2):
                        pi2 = pscnt % 2; pscnt += 1
                        c0 = (dr * 2 + ri) * 128
                        for g2 in range(2):
                            rows = slice(g2 * 64, (g2 + 1) * 64)
                            lw = wbtf[fi_][:, pl, c0 + g2 * 64:c0 + (g2 + 1) * 64]
                            b.mm(ps_t[rows, 4 + pi2, :], lw, ub[:, g2, 32:NJ], True, True, [f"wbtf{fi_}", f"ub{ui}"], [f"ps{4 + pi2}"])
                            b.mm(ps_t[rows, 6 + pi2, 0:32], lw, ub[:, g2, 0:32], True, True, [f"wbtf{fi_}", f"ub{ui}"], [f"ps{6 + pi2}"])
                        sk = f"Sw{pr_}_{dr}_{ri}"
                        if dr == 0:
                            b.copy("act", S[:, ri, dr, pr_, 32:NJ], PS(4 + pi2), [f"ps{4 + pi2}"], [sk + "x"])
                            b.copy("dve", S[:, ri, dr, pr_, 0:32], PS(6 + pi2)[:, 0:32], [f"ps{6 + pi2}"], [sk + "c"])
                        else:
                            b.copy("act", S[:, ri, dr, pr_, 543:31:-1], PS(4 + pi2), [f"ps{4 + pi2}"], [sk + "x"])
                            b.copy("dve", S[:, ri, dr, pr_, 31::-1], PS(6 + pi2)[:, 0:32], [f"ps{6 + pi2}"], [sk + "c"])
            if g1pend[0] is not None:
                g1pend[0]()
            g1pend[0] = chunk_states
    if g1pend[0] is not None:
        g1pend[0]()
    P.barrier()
    ar.release()
    b.copy("dve", Z[0], S[:, :, :, :, 0], ["S"], ["Z0"])
    for i in range(1, NJ):
        zc, zn = Z[(i - 1) % 2], Z[i % 2]
        kc_, kn_ = f"Z{(i - 1) % 2}", f"Z{i % 2}"
        b.tt("dve", ta, AR2, zc, ALU.mult, ["AR2", kc_], ["ta"])
        b.tt("dve", tb, AIS, zc[:, ::-1, :, :], ALU.mult, ["AIS", kc_], ["tb"])
        b.tt("dve", ta, ta, tb, ALU.add, ["ta", "tb"], ["ta"])
        b.tt("dve", zn, ta, S[:, :, :, :, i], ALU.add, ["ta", f"S{i}"], [kn_])
        b.copy("act", S[:, :, :, :, i], zn, [kn_], [f"S{i}"])
    P.barrier()
    ar.mark()
    ktf = [ar.alloc([8, 128], BF16) for _ in range(2)]
    wdf = [ar.alloc([4, 512], BF16) for _ in range(2)]
    ushf = ar.alloc([8, NJ], BF16)
    Ysh = ar.alloc([8, 512], BF16)
    Gb = ar.alloc([SEQ], BF16)
    for fb in range(8):
        fi_ = fb % 2
        b.dma("sp", ktf[fi_], d_kt[fb * 8:(fb + 1) * 8].rearrange("g p m -> p g m"), [], [f"ktf{fi_}"], f"gkt{fi_}")
        b.dma("sp", wdf[fi_], d_wd[fb * 4:(fb + 1) * 4].rearrange("q p d r m -> p q (d r m)"), [], [f"wdf{fi_}"], f"gwd{fi_}")
        b.dma("sp", ushf, d_ush[fb * 8:(fb + 1) * 8].rearrange("g p j -> p g j"), [], ["ushf"], "gush")
        for gl_ in range(8):
            pl, g2 = gl_ // 2, gl_ % 2
            pr_ = fb * 4 + pl
            rows = slice(g2 * 64, (g2 + 1) * 64)
            pi2 = pscnt % 2; pscnt += 1
            b.mm(PS(pi2), ktf[fi_][:, gl_, :], ushf[:, gl_, 32:NJ], True, False, [f"ktf{fi_}", "ushf"], [f"ps{pi2}"])
            for ri in range(2):
                b.mm(PS(pi2), wdf[fi_][rows, pl, (0 * 2 + ri) * 128:(0 * 2 + ri + 1) * 128], S[rows, ri, 0, pr_, 31:543],
                     False, False, [f"wdf{fi_}"], [f"ps{pi2}"])
            for ri in range(2):
                b.mm(PS(pi2), wdf[fi_][rows, pl, (1 * 2 + ri) * 128:(1 * 2 + ri + 1) * 128], S[rows, ri, 1, pr_, 542:30:-1],
                     False, ri == 1, [f"wdf{fi_}"], [f"ps{pi2}"])
            b.copy("act" if gl_ % 2 == 0 else "dve", Ysh[:, gl_, :], PS(pi2), [f"ps{pi2}"], [f"Ysh{gl_}"])
        Gv = Gb.rearrange("p (j t) -> p t j", t=8)
        for t in range(8):
            pi2 = 4 + t % 2
            for gl_ in range(8):
                lw = qsel[:, t, (7 - gl_) * 16:(7 - gl_) * 16 + 128]
                b.mm(PS(pi2), lw, Ysh[:, gl_, :], gl_ == 0, gl_ == 7, ["qsel", f"Ysh{gl_}"], [f"ps{pi2}"])
            b.act(Gv[:, t, :], PS(pi2), AF.Gelu_apprx_tanh, [f"ps{pi2}"], ["Gb"])
        b.dma("pool", d_g[fb], Gb, ["Gb"], [], "gGb")
    P.barrier()
    ar.release()
    ar.release()

    if stop == 'G':
        return done()
    ar.mark()
    wg = ar.alloc([8, D], BF16); wo1 = ar.alloc([8, D], BF16)
    wos = ar.alloc([4, D], F32)
    for wi, (wsrc, wdst) in enumerate(((od_glu_w, wg), (od_out_w, wo1))):
        for q2 in range(2):
            b.dma("sp", wos, wsrc[q2 * 512:(q2 + 1) * 512, :].rearrange("(kc p) f -> p kc f", p=128), [], ["wos"], "hwos")
            b.copy("dve" if q2 == 0 else "act", wdst[:, q2 * 4:(q2 + 1) * 4, :], wos, ["wos"], [f"hw{wi}{q2}"])
    glub = ar.alloc([8], F32)
    b.dma("sp", glub, od_glu_b, [], ["glub"], "h1")
    fg = ar.alloc([D], F32)
    b.dma("sp", fg, final_g.partition_broadcast(128), [], ["fg"], "h2")
    gt_ = [ar.alloc([8, 512], BF16) for _ in range(2)]
    zt_ = [ar.alloc([8, 512], BF16) for _ in range(2)]
    mT = [ar.alloc([8, 512], BF16) for _ in range(2)]
    sgm = [ar.alloc([512], F32) for _ in range(2)]
    gm = [ar.alloc([512], F32) for _ in range(2)]
    x1r = [ar.alloc([D], F32) for _ in range(2)]
    x2 = [ar.alloc([D], F32) for _ in range(2)]
    tmpo = [ar.alloc([512], F32) for _ in range(2)]
    junk = ar.alloc([D], BF16)
    ss2 = [ar.alloc([1], F32) for _ in range(2)]; sd2 = [ar.alloc([1], F32) for _ in range(2)]; rs2 = [ar.alloc([1], F32) for _ in range(2)]
    xf = [ar.alloc([D], F32) for _ in range(2)]
    for tt in range(8):
        i = tt % 2
        t0 = tt * 512
        b.dma("sp", gt_[i], d_g[:, :, t0:t0 + 512].rearrange("f p t -> p f t"), [], [f"gt{i}"], f"hgt{i}")
        b.dma("sp", zt_[i], d_sz[:, :, t0:t0 + 512].rearrange("f p t -> p f t"), [], [f"zt{i}"], f"hzt{i}")
        for fo in range(8):
            j = fo % 2
            for kc in range(8):
                b.mm(PS(j), wg[:, kc, fo * 128:(fo + 1) * 128], gt_[i][:, kc, :], kc == 0, kc == 7,
                     [f"gt{i}", "hw00", "hw01"], [f"ps{j}"])
            b.act(sgm[j], PS(j), AF.Sigmoid, [f"ps{j}", "glub"], [f"sgm{j}"], bias=glub[:, fo:fo + 1])
            b.tt("dve", gm[j], sgm[j], gt_[i][:, fo, :], ALU.mult, [f"sgm{j}", f"gt{i}"], [f"gm{j}"])
            b.tt("dve", mT[i][:, fo, :], gm[j], zt_[i][:, fo, :], ALU.mult, [f"gm{j}", f"zt{i}"], [f"mT{i}"])
        for sub in range(4):
            t = tt * 4 + sub
            k2 = t % 2
            b.dma("sp", x1r[k2], d_x1[CTX + t * 128:CTX + (t + 1) * 128, :], [], [f"x1r{k2}"], f"hx1{k2}")
            for half in range(2):
                for fo in range(8):
                    b.mm(PS(2 + half), mT[i][:, fo, sub * 128:(sub + 1) * 128], wo1[:, fo, half * 512:(half + 1) * 512],
                         fo == 0, fo == 7, [f"mT{i}", "hw10", "hw11"], [f"ps{2 + half}"])
                b.tt("dve", tmpo[half], PS(2 + half), gtrow[1][0][:, half * 512:(half + 1) * 512], ALU.mult,
                     [f"ps{2 + half}", "gtrow10"], [f"tmpo{half}"])
                b.tt("dve", x2[k2][:, half * 512:(half + 1) * 512], tmpo[half], x1r[k2][:, half * 512:(half + 1) * 512], ALU.add,
                     [f"tmpo{half}", f"x1r{k2}"], [f"x2{k2}"])
            b.act(junk, x2[k2], AF.Square, [f"x2{k2}"], ["junk", f"ss2{k2}"], accum_out=ss2[k2])
            b.act(sd2[k2], ss2[k2], AF.Sqrt, [f"ss2{k2}"], [f"sd2{k2}"], scale=1.0 / D, bias=eps_t)
            b.recip(rs2[k2], sd2[k2], [f"sd2{k2}"], [f"rs2{k2}"])
            b.stt(xf[k2], x2[k2], rs2[k2], fg, ALU.mult, ALU.mult, [f"x2{k2}", f"rs2{k2}", "fg"], [f"xf{k2}"])
            b.dma("pool", out_ap[t * 128:(t + 1) * 128, :], xf[k2], [f"xf{k2}"], [], f"hout{k2}")
    P.barrier()
    ar.release()
    P.emit()
    es.close()
    return nc, P, ar


def _fm(v):
    return np.ascontiguousarray(v.reshape(8, 128).T)


def _s5_state(a):
    a = a.reshape(2, 32, 2, 64)
    return np.ascontiguousarray(a.transpose(2, 3, 0, 1).reshape(128, 64))


def _consts():
    c = {}
    n = np.arange(SEQ)
    row = (n // 64).astype(np.float32); col = (n % 64).astype(np.float32)
    freqs = (np.float32(10000.0) ** (-np.arange(16, dtype=np.float32) / np.float32(16))).astype(np.float32)
    ang = np.concatenate([row[:, None] * freqs, col[:, None] * freqs], axis=-1).astype(np.float32)
    cos = np.cos(ang).astype(np.float32).T
    sin = np.sin(ang).astype(np.float32).T
    ccos = np.zeros((128, SEQ), np.float32); csin = np.zeros((128, SEQ), np.float32)
    for comp in range(2):
        for half in range(2):
            r0 = comp * 64 + half * 32
            ccos[r0:r0 + 32] = cos
            csin[r0:r0 + 32] = -sin if half == 0 else sin
    c["c_cos"] = ccos; c["c_sin"] = csin
    perm = np.zeros((128, 128), np.float32)
    for m in range(128):
        comp, r = divmod(m, 64)
        partner = comp * 64 + (r + 32) % 64
        perm[partner, m] = 1.0
    c["c_perm"] = perm.astype(ml_dtypes.bfloat16)
    c["c_ident"] = np.eye(128, dtype=np.float32).astype(ml_dtypes.bfloat16)
    zsel = np.zeros((128, 8, 240), np.float32)
    qsel = np.zeros((128, 8, 240), np.float32)
    for g in range(8):
        for h in range(16):
            zsel[g * 16 + h, g, 112 + h] = 1.0
            qsel[g * 16 + h, g, 112 + h] = 1.0
    c["c_zsel"] = zsel.astype(ml_dtypes.bfloat16); c["c_qsel"] = qsel.astype(ml_dtypes.bfloat16)
    s_idx = np.arange(128) // 16
    c["c_maskf"] = (s_idx[None, :] >= s_idx[:, None]).astype(np.float32)
    c["c_maskb"] = (s_idx[:, None] >= s_idx[None, :]).astype(np.float32)
    c["c_kvec"] = np.broadcast_to(np.arange(-8, 9, dtype=np.float32)[None, :], (128, 17)).copy()
    return c


def _prep(inputs):
    f = lambda a: np.ascontiguousarray(np.asarray(a, dtype=np.float32))
    shared = {}
    shared["ev_mod_w"] = f(inputs["ev_mod_w"][0])
    shared["ev_mod_bf"] = np.ascontiguousarray(f(inputs["ev_mod_b"][0]).reshape(24, 128).T)
    shared["ev_mod_bg"] = f(inputs["ev_mod_b"][0][2048:3072]).reshape(1, D)
    shared["ev_norm_g"] = _fm(f(inputs["ev_norm_g"][0]))
    shared["ev_in_w"] = f(inputs["ev_in_w"][0])
    shared["ev_lam"] = np.concatenate([f(inputs["ev_lam_q1"][0]), f(inputs["ev_lam_k1"][0]),
                                       f(inputs["ev_lam_q2"][0]), f(inputs["ev_lam_k2"][0])]).reshape(1, 256)
    shared["ev_subln"] = f(inputs["ev_subln_g"][0]).reshape(128, 1)
    shared["ev_dw_w"] = np.ascontiguousarray(f(inputs["ev_dw_w"][0]).reshape(31, 8, 128).transpose(2, 1, 0))
    shared["ev_dw_b"] = _fm(f(inputs["ev_dw_b"][0]))
    shared["ev_cln_g"] = _fm(f(inputs["ev_cln_g"][0]))
    shared["ev_cln_b"] = _fm(f(inputs["ev_cln_b"][0]))
    shared["ev_out_w"] = f(inputs["ev_out_w"][0])
    shared["od_mod_w"] = f(inputs["od_mod_w"][0])
    shared["od_mod_bf"] = np.ascontiguousarray(f(inputs["od_mod_b"][0]).reshape(24, 128).T)
    shared["od_mod_bg"] = f(inputs["od_mod_b"][0][2048:3072]).reshape(1, D)
    shared["od_norm_g"] = _fm(f(inputs["od_norm_g"][0]))
    shared["od_in_w"] = f(inputs["od_in_w"][0])
    shared["s5_lre"] = _s5_state(f(inputs["s5_lam_re"][0]))
    shared["s5_lim"] = _s5_state(f(inputs["s5_lam_im"][0]))
    shared["s5_ldt"] = _s5_state(np.broadcast_to(f(inputs["s5_log_dt"][0])[:, :, None], (2, 64, 64)))
    for nm, key in (("s5_bre", "s5_b_re"), ("s5_bim", "s5_b_im")):
        a = f(inputs[key][0]).reshape(2, 32, 2, 64, 16)
        shared[nm] = np.ascontiguousarray(a.transpose(2, 3, 0, 1, 4).reshape(128, 64, 16))
    for nm, key in (("s5_cre", "s5_c_re"), ("s5_cim", "s5_c_im")):
        a = f(inputs[key][0]).reshape(2, 32, 2, 16, 64)
        shared[nm] = np.ascontiguousarray(a.transpose(2, 4, 0, 1, 3).reshape(128, 64, 16))
    dd = f(inputs["s5_d"][0]).reshape(64, 16)
    shared["s5_d"] = np.ascontiguousarray(np.tile(dd.T, (8, 1)))
    shared["od_glu_w"] = f(inputs["od_glu_w"][0])
    shared["od_glu_b"] = _fm(f(inputs["od_glu_b"][0]))
    shared["od_out_w"] = f(inputs["od_out_w"][0])
    shared["final_g"] = f(inputs["final_g"]).reshape(1, D)
    shared.update(_consts())
    x = f(inputs["x"]); ctx = f(inputs["ctx"]); c = f(inputs["c"]); cctx = f(inputs["c_ctx"])
    maps = []
    for bi in range(x.shape[0]):
        m = dict(shared)
        m["x"] = x[bi]
        m["ctx"] = ctx[bi]
        m["cc"] = np.ascontiguousarray(np.stack([_fm(c[bi]), _fm(cctx)], axis=-1))
        maps.append(m)
    return maps


_CACHE = {}


def kernel(**inputs):
    maps = _prep(inputs)
    if "nc" not in _CACHE:
        _CACHE["nc"] = build(False)[0]
    nc = _CACHE["nc"]
    res = run_bass_kernel_spmd(nc, maps, core_ids=list(range(NB)))
    return np.stack([np.asarray(r["out"], dtype=np.float32) for r in res.results], axis=0)
```

```python
import math
import re
import numpy as np
import ml_dtypes
from contextlib import ExitStack
import concourse.bass as bass
import concourse.mybir as mybir
from concourse.bass_utils import run_bass_kernel_spmd

F32 = mybir.dt.float32
BF16 = mybir.dt.bfloat16
I32 = mybir.dt.int32
AF = mybir.ActivationFunctionType
ALU = mybir.AluOpType

ENGS = ("pe", "act", "dve", "pool", "sp")
EPOCH = 30000
DMA_EPOCH = 1800

D = 1024
SEQ = 4096
CTX = 256
NTOK = SEQ + CTX
EPS = 1e-6
NB = 8


class _Op:
    __slots__ = ("eng", "fn", "is_dma", "dkey", "dcount", "pos", "deps", "signal", "sigval", "barrier")

    def __init__(self, eng, fn, is_dma=False, dkey=None):
        self.eng = eng
        self.fn = fn
        self.is_dma = is_dma
        self.dkey = dkey
        self.dcount = 0
        self.pos = 0
        self.deps = []
        self.signal = False
        self.sigval = 0
        self.barrier = False


class Prog:
    def __init__(self, nc):
        self.nc = nc
        self.streams = {e: [] for e in ENGS}
        self.last_w = {}
        self.readers = {}
        self.seen = {e: {} for e in ENGS}
        self.dma_counts = {}
        self.all_dma_last = {}
        self._npos = {e: 0 for e in ENGS}
        self.last_comp = {}

    def _add_dep(self, op, ev):
        if ev is None or ev is op:
            return
        if ev.is_dma:
            k = ("dma", ev.dkey)
            v = ev.dcount
        else:
            k = ("eng", ev.eng)
            v = ev.pos
        s = self.seen[op.eng]
        if s.get(k, 0) >= v:
            return
        s[k] = v
        op.deps.append(ev)
        ev.signal = True

    def op(self, eng, fn, reads=(), writes=(), accum=False):
        psr = [r for r in reads if r.startswith("ps")]
        if psr:
            reads = [r for r in reads if not r.startswith("ps")]
            writes = list(writes) + [r for r in psr if r not in writes]
        o = _Op(eng, fn)
        self._npos[eng] += 1
        o.pos = self._npos[eng]
        for r in reads:
            self._add_dep(o, self.last_w.get(r))
        for w in writes:
            lw = self.last_w.get(w)
            if not (accum and lw is not None and (not lw.is_dma) and lw.eng == eng):
                self._add_dep(o, lw)
            for rd in self.readers.get(w, ()):
                self._add_dep(o, rd)
        for r in reads:
            self.readers.setdefault(r, []).append(o)
        for w in writes:
            self.last_w[w] = o
            self.readers[w] = []
        self.streams[eng].append(o)
        self.last_comp[eng] = o
        return o

    def dma(self, eng, out, in_, reads=(), writes=(), key=None, **kw):
        assert key is not None
        o = _Op(eng, None, is_dma=True, dkey=key)
        c = self.dma_counts.get(key, 0) + 1
        self.dma_counts[key] = c
        o.dcount = c
        o.fn = lambda e, out=out, in_=in_, kw=kw: e.dma_start(out=out, in_=in_, **kw)
        for r in reads:
            self._add_dep(o, self.last_w.get(r))
        for w in writes:
            self._add_dep(o, self.last_w.get(w))
            for rd in self.readers.get(w, ()):
                self._add_dep(o, rd)
        self._add_dep(o, self.all_dma_last.get(key))
        for r in reads:
            self.readers.setdefault(r, []).append(o)
        for w in writes:
            self.last_w[w] = o
            self.readers[w] = []
        self.all_dma_last[key] = o
        self.streams[eng].append(o)
        return o

    def barrier(self, name=None):
        if name is None:
            name = f"b{len(getattr(self, 'bnames', []))}"
        self.bnames = getattr(self, "bnames", []) + [name]
        lasts = list(self.last_comp.values()) + list(self.all_dma_last.values())
        for e in ENGS:
            o = _Op(e, None)
            o.barrier = True
            for ev in lasts:
                self._add_dep(o, ev)
            self.streams[e].append(o)
        self.last_w = {}
        self.readers = {}
        if getattr(self, "mark_tile", None) is not None:
            mt = self.mark_tile
            val = float(len(self.bnames))
            self.op("pool", lambda e, mt=mt, val=val: e.memset(mt, val), writes=["__mark"])

    def emit(self):
        nc = self.nc
        eng_sigs = {}
        for e in ENGS:
            n = 0
            for o in self.streams[e]:
                if o.is_dma or o.barrier:
                    continue
                if o.signal:
                    n += 1
                    o.sigval = n
            eng_sigs[e] = n
        with ExitStack() as es:
            esem = {}
            for e in ENGS:
                ne = eng_sigs[e] // EPOCH + 1
                esem[e] = [es.enter_context(nc.semaphore(f"s_{e}_{i}")) for i in range(ne)]
            dsem = {}
            for k, c in self.dma_counts.items():
                ne = (c - 1) // DMA_EPOCH + 1
                dsem[k] = [es.enter_context(nc.semaphore(f"d_{len(dsem)}_{i}")) for i in range(ne)]
            block = es.enter_context(nc.Block())

            def emit_stream(engname, engobj):
                for o in self.streams[engname]:
                    for ev in o.deps:
                        if ev.is_dma:
                            ep = (ev.dcount - 1) // DMA_EPOCH
                            engobj.wait_ge(dsem[ev.dkey][ep], 16 * (ev.dcount - ep * DMA_EPOCH))
                        else:
                            ep = (ev.sigval - 1) // EPOCH
                            engobj.wait_ge(esem[ev.eng][ep], ev.sigval - ep * EPOCH)
                    if o.barrier:
                        continue
                    inst = o.fn(engobj)
                    if o.is_dma:
                        ep = (o.dcount - 1) // DMA_EPOCH
                        inst.then_inc(dsem[o.dkey][ep], 16)
                    elif o.signal:
                        ep = (o.sigval - 1) // EPOCH
                        inst.then_inc(esem[o.eng][ep], 1)

            @block.tensor
            def _(t):
                emit_stream("pe", t)

            @block.scalar
            def _(t):
                emit_stream("act", t)

            @block.vector
            def _(t):
                emit_stream("dve", t)

            @block.gpsimd
            def _(t):
                emit_stream("pool", t)

            @block.sync
            def _(t):
                emit_stream("sp", t)

    def stats(self):
        return {e: len(self.streams[e]) for e in ENGS}


class Arena:
    def __init__(self, ap2d, nbytes):
        self.ap = ap2d
        self.nbytes = nbytes
        self.off = 0
        self.marks = []
        self.peak = 0

    def alloc(self, shape_free, dtype, parts=128):
        if isinstance(shape_free, int):
            shape_free = [shape_free]
        esz = 2 if dtype == BF16 else 4
        n = int(np.prod(shape_free))
        nb = n * esz
        self.off = (self.off + 31) // 32 * 32
        assert self.off + nb <= self.nbytes, f"SBUF arena overflow {self.off}+{nb}>{self.nbytes}"
        v = self.ap[0:parts, self.off // 2:(self.off + nb) // 2]
        if dtype != BF16:
            v = v.bitcast(dtype)
        self.off += nb
        self.peak = max(self.peak, self.off)
        if len(shape_free) == 2:
            v = v.rearrange("p (a b) -> p a b", a=shape_free[0])
        elif len(shape_free) == 3:
            v = v.rearrange("p (a b c) -> p a b c", a=shape_free[0], b=shape_free[1])
        elif len(shape_free) == 4:
            v = v.rearrange("p (a b c d) -> p a b c d", a=shape_free[0], b=shape_free[1], c=shape_free[2])
        return v

    def mark(self):
        self.marks.append(self.off)

    def release(self):
        self.off = self.marks.pop()


class B:
    def __init__(self, P):
        self.P = P
        self.uid = 0

    def key(self, s):
        self.uid += 1
        return f"{s}#{self.uid}"

    def mm(self, out, lhsT, rhs, start, stop, reads, writes):
        self.P.op("pe", lambda e: e.matmul(out, lhsT=lhsT, rhs=rhs, start=start, stop=stop),
                  reads=reads, writes=writes, accum=True)

    def tr(self, out, in_, ident, reads, writes):
        self.P.op("pe", lambda e: e.transpose(out, in_, ident), reads=reads, writes=writes, accum=True)

    def act(self, out, in_, func, reads, writes, bias=None, scale=None, accum_out=None, eng="act"):
        kw = {}
        if bias is not None:
            kw["bias"] = bias
        if scale is not None:
            kw["scale"] = scale
        if accum_out is not None:
            kw["accum_out"] = accum_out
        self.P.op("act", lambda e: e.activation(out=out, in_=in_, func=func, **kw), reads=reads, writes=writes)

    def tt(self, eng, out, in0, in1, op, reads, writes):
        self.P.op(eng, lambda e: e.tensor_tensor(out=out, in0=in0, in1=in1, op=op), reads=reads, writes=writes)

    def ts(self, eng, out, in0, s1, s2, op0, op1, reads, writes):
        if s2 is None:
            self.P.op(eng, lambda e: e.tensor_scalar(out=out, in0=in0, scalar1=s1, scalar2=None, op0=op0),
                      reads=reads, writes=writes)
        else:
            self.P.op(eng, lambda e: e.tensor_scalar(out=out, in0=in0, scalar1=s1, scalar2=s2, op0=op0, op1=op1),
                      reads=reads, writes=writes)

    def stt(self, out, in0, scalar, in1, op0, op1, reads, writes):
        self.P.op("dve", lambda e: e.scalar_tensor_tensor(out=out, in0=in0, scalar=scalar, in1=in1, op0=op0, op1=op1),
                  reads=reads, writes=writes)

    def copy(self, eng, out, in_, reads, writes):
        if eng == "act":
            self.P.op("act", lambda e: e.activation(out=out, in_=in_, func=AF.Copy), reads=reads, writes=writes)
        else:
            self.P.op(eng, lambda e: e.tensor_copy(out=out, in_=in_), reads=reads, writes=writes)

    def recip(self, out, in_, reads, writes):
        self.P.op("dve", lambda e: e.reciprocal(out=out, in_=in_), reads=reads, writes=writes)

    def memset(self, eng, out, val, writes):
        self.P.op(eng, lambda e: e.memset(out, val), writes=writes)

    def dma(self, eng, out, in_, reads, writes, key, **kw):
        if re.match(r"^(c|a|f|g|h|fk|dbg)\d+$", key):
            self.misc = getattr(self, "misc", 0) + 1
            key = f"misc{self.misc % 4}"
        self.P.dma(eng, out, in_, reads=reads, writes=writes, key=key, **kw)


def build(debug=False, stop=None):
    nc = bass.Bass("TRN2", target_bir_lowering=False)
    IN = {}

    def din(name, shape, dt=F32):
        IN[name] = nc.dram_tensor(name, list(shape), dt, kind="ExternalInput").ap()
        return IN[name]

    dbg_kind = "ExternalOutput" if debug else "Internal"

    def dscr(name, shape, dt):
        return nc.dram_tensor(name, list(shape), dt, kind=dbg_kind).ap()

    x_in = din("x", [SEQ, D])
    ctx_in = din("ctx", [CTX, D])
    cc_in = din("cc", [128, 8, 2])
    ev_mod_w = din("ev_mod_w", [D, 3 * D]); ev_mod_bf = din("ev_mod_bf", [128, 24]); ev_mod_bg = din("ev_mod_bg", [1, D])
    ev_norm_g = din("ev_norm_g", [128, 8])
    ev_in_w = din("ev_in_w", [D, 7168])
    ev_lam = din("ev_lam", [1, 256])
    ev_subln = din("ev_subln", [128, 1])
    ev_dw_w = din("ev_dw_w", [128, 8, 31]); ev_dw_b = din("ev_dw_b", [128, 8])
    ev_cln_g = din("ev_cln_g", [128, 8]); ev_cln_b = din("ev_cln_b", [128, 8])
    ev_out_w = din("ev_out_w", [2048, D])
    od_mod_w = din("od_mod_w", [D, 3 * D]); od_mod_bf = din("od_mod_bf", [128, 24]); od_mod_bg = din("od_mod_bg", [1, D])
    od_norm_g = din("od_norm_g", [128, 8])
    od_in_w = din("od_in_w", [D, 2048])
    s5_lre = din("s5_lre", [128, 64]); s5_lim = din("s5_lim", [128, 64]); s5_ldt = din("s5_ldt", [128, 64])
    s5_bre = din("s5_bre", [128, 64, 16]); s5_bim = din("s5_bim", [128, 64, 16])
    s5_cre = din("s5_cre", [128, 64, 16]); s5_cim = din("s5_cim", [128, 64, 16])
    s5_d = din("s5_d", [128, 64])
    od_glu_w = din("od_glu_w", [D, D]); od_glu_b = din("od_glu_b", [128, 8]); od_out_w = din("od_out_w", [D, D])
    final_g = din("final_g", [1, D])
    c_cos = din("c_cos", [128, SEQ]); c_sin = din("c_sin", [128, SEQ])
    c_perm = din("c_perm", [128, 128], BF16); c_ident = din("c_ident", [128, 128], BF16)
    c_zsel = din("c_zsel", [128, 8, 240], BF16); c_qsel = din("c_qsel", [128, 8, 240], BF16)
    c_maskf = din("c_maskf", [128, 128]); c_maskb = din("c_maskb", [128, 128])
    c_kvec = din("c_kvec", [128, 17])

    out_ap = nc.dram_tensor("out", [SEQ, D], F32, kind="ExternalOutput").ap()

    d_q = dscr("d_q", [8, 128, NTOK], BF16)
    d_k = dscr("d_k", [8, 128, NTOK], BF16)
    d_v = dscr("d_v", [NTOK, D], BF16)
    d_sga = dscr("d_sga", [8, 128, NTOK], BF16)
    d_h = dscr("d_h", [8, 128, NTOK], BF16)
    d_sgb = dscr("d_sgb", [8, 128, NTOK], BF16)
    d_y = dscr("d_y", [8, 128, NTOK], F32)
    d_mix = dscr("d_mix", [16, 128, NTOK], BF16)
    d_x1 = dscr("d_x1", [NTOK, D], F32)
    d_u = dscr("d_u", [8, 128, NTOK], BF16)
    d_sz = dscr("d_sz", [8, 128, SEQ], BF16)
    d_g = dscr("d_g", [8, 128, SEQ], BF16)
    d_kt = dscr("d_kt", [64, 128, 128], BF16)
    d_wbt = dscr("d_wbt", [32, 128, 2, 2, 128], BF16)
    d_wd = dscr("d_wd", [32, 128, 2, 2, 128], BF16)
    d_ush = dscr("d_ush", [64, 128, 544], BF16)

    es = ExitStack()
    ARENA_BYTES = 207 * 1024
    arena_t = es.enter_context(nc.sbuf_tensor("arena", [128, ARENA_BYTES // 2], BF16))
    ps_t = es.enter_context(nc.psum_tensor("ps", [128, 8, 512], F32))
    ar = Arena(arena_t[:, :], ARENA_BYTES)
    P = Prog(nc)
    b = B(P)
    if debug == "mark":
        P.mark_tile = es.enter_context(nc.sbuf_tensor("phasemark", [128, 1], F32))[:, :]

    def PS(i):
        return ps_t[:, i, :]

    dbg_n = [0]

    def dbg(name, ap, shape, dt=F32, keys=()):
        if not debug:
            return
        o = nc.dram_tensor("dbg_" + name, list(shape), dt, kind="ExternalOutput").ap()
        dbg_n[0] += 1
        b.dma("sp", o, ap, list(keys), [], f"dbg{dbg_n[0]}")

    def done():
        P.barrier()
        P.emit()
        es.close()
        return nc, P, ar

    def PSB(i):
        return ps_t[:, i, :].bitcast(BF16)

    ident = ar.alloc([128], BF16)
    perm = ar.alloc([128], BF16)
    ones_bf = ar.alloc([128], BF16)
    ones_f = ar.alloc([128], F32)
    b.dma("sp", ident, c_ident, [], ["ident"], "c1")
    b.dma("sp", perm, c_perm, [], ["perm"], "c2")
    b.memset("pool", ones_bf, 1.0, ["ones_bf"])
    b.memset("pool", ones_f, 1.0, ["ones_f"])
    modA = [ar.alloc([8, 2], F32) for _ in range(2)]
    modB = [ar.alloc([8, 2], F32) for _ in range(2)]
    gtrow = [[ar.alloc([D], F32) for _ in range(2)] for _ in range(2)]
    neglam = ar.alloc([1], F32)
    gsub = ar.alloc([1], F32)
    eps_t = ar.alloc([1], F32)
    b.memset("pool", eps_t, EPS, ["eps_t"])

    ar.mark()
    cc = ar.alloc([8, 2], F32)
    b.dma("sp", cc, cc_in, [], ["cc"], "a1")
    scb = ar.alloc([8, 2], F32)
    b.act(scb, cc, AF.Silu, ["cc"], ["scb"])
    screp = ar.alloc([8, 2, 128], F32)
    b.copy("dve", screp, scb.unsqueeze(3).to_broadcast([128, 8, 2, 128]), ["scb"], ["screp"])
    Wb = ar.alloc([8, 3 * D], F32)
    for L, (mw, mbf, mbg, ng) in enumerate([(ev_mod_w, ev_mod_bf, ev_mod_bg, ev_norm_g),
                                            (od_mod_w, od_mod_bf, od_mod_bg, od_norm_g)]):
        for kc in range(8):
            b.dma("sp", Wb[:, kc, :], mw[kc * 128:(kc + 1) * 128, :], [], [f"Wb{kc}"], f"mstg{kc}")
        wkeys = [f"Wb{kc}" for kc in range(8)]
        for fc in range(24):
            for kc in range(8):
                b.mm(PS(0)[:, fc * 2:fc * 2 + 2], Wb[:, kc, fc * 128:(fc + 1) * 128], scb[:, kc, :],
                     kc == 0, kc == 7, ["scb"] + wkeys, ["psA0"])
        mbf_t = ar.alloc([24], F32)
        b.dma("sp", mbf_t, mbf, [], ["mbf_t"], "a2")
        modT = ar.alloc([24, 2], F32)
        b.tt("dve", modT, PS(0)[:, 0:48].rearrange("p (f c) -> p f c", c=2),
             mbf_t.unsqueeze(2).to_broadcast([128, 24, 2]), ALU.add, ["psA0", "mbf_t"], ["modT"])
        ng_t = ar.alloc([8], F32)
        b.dma("sp", ng_t, ng, [], ["ng_t"], "a3")
        tmpA = ar.alloc([8, 2], F32)
        b.ts("dve", tmpA, modT[:, 8:16, :], 1.0, None, ALU.add, None, ["modT"], ["tmpA"])
        b.tt("dve", modA[L], tmpA, ng_t.unsqueeze(2).to_broadcast([128, 8, 2]), ALU.mult, ["tmpA", "ng_t"], [f"modA{L}"])
        b.copy("dve", modB[L], modT[:, 0:8, :], ["modT"], [f"modB{L}"])
        mbg_t = ar.alloc([D], F32)
        b.dma("sp", mbg_t, mbg.partition_broadcast(128), [], ["mbg_t"], "a4")
        for col in range(2):
            for half in range(2):
                for kc in range(8):
                    b.mm(PS(1 + half), screp[:, kc, col, :], Wb[:, kc, 2048 + half * 512:2048 + (half + 1) * 512],
                         kc == 0, kc == 7, ["screp"] + wkeys, [f"psA{1 + half}"])
                b.tt("dve", gtrow[L][col][:, half * 512:(half + 1) * 512], PS(1 + half), mbg_t[:, half * 512:(half + 1) * 512],
                     ALU.add, [f"psA{1 + half}", "mbg_t"], [f"gtrow{L}{col}"])
    lam_t = ar.alloc([256], F32)
    b.dma("sp", lam_t, ev_lam.partition_broadcast(128), [], ["lam_t"], "a5")
    lprod = ar.alloc([2, 64], F32)
    lam_v = lam_t.rearrange("p (a b) -> p a b", a=4)
    b.tt("dve", lprod[:, 0, :], lam_v[:, 0, :], lam_v[:, 1, :], ALU.mult, ["lam_t"], ["lprod0"])
    b.tt("dve", lprod[:, 1, :], lam_v[:, 2, :], lam_v[:, 3, :], ALU.mult, ["lam_t"], ["lprod1"])
    lsum = ar.alloc([2], F32)
    P.op("dve", lambda e: e.reduce_sum(out=lsum, in_=lprod, axis=mybir.AxisListType.X), reads=["lprod0", "lprod1"], writes=["lsum"])
    lexp = ar.alloc([2], F32)
    b.act(lexp, lsum, AF.Exp, ["lsum"], ["lexp"])
    LAM_INIT0 = 0.8 - 0.6 * math.exp(-0.3 * 0)
    b.tt("dve", neglam, lexp[:, 1:2], lexp[:, 0:1], ALU.subtract, ["lexp"], ["neglam_a"])
    b.ts("dve", neglam, neglam, -LAM_INIT0, None, ALU.add, None, ["neglam_a"], ["neglam"])
    sub_t = ar.alloc([1], F32)
    b.dma("sp", sub_t, ev_subln, [], ["sub_t"], "a6")
    b.ts("dve", gsub, sub_t, 1.0 - LAM_INIT0, None, ALU.mult, None, ["sub_t"], ["gsub"])
    P.barrier()
    for L in range(2):
        dbg(f"modA{L}", modA[L], [128, 8, 2]); dbg(f"modB{L}", modB[L], [128, 8, 2])
        for col in range(2):
            dbg(f"gtrow{L}{col}", gtrow[L][col], [128, D])
    dbg("neglam", neglam, [128, 1]); dbg("gsub", gsub, [128, 1])
    P.barrier()
    ar.release()

    if stop == 'A':
        return done()
    def prologue(L, src_ctx, src_x, hT):
        ar.mark()
        xs = [ar.alloc([D], F32) for _ in range(4)]
        xnb = [ar.alloc([D], BF16) for _ in range(4)]
        junk = ar.alloc([D], BF16)
        ss = [ar.alloc([1], F32) for _ in range(4)]
        sd = [ar.alloc([1], F32) for _ in range(4)]
        rs = [ar.alloc([1], F32) for _ in range(4)]
        evt = [ar.alloc([8, 128], F32) for _ in range(2)]
        hkeys_ = [f"hT{fb}" for fb in range(8)]
        def stage_a(t):
            i = t % 4
            src = src_ctx[t * 128:(t + 1) * 128, :] if t < 2 else src_x[(t - 2) * 128:(t - 1) * 128, :]
            b.dma("sp", xs[i], src, [], [f"xs{i}"], f"pxs{i}")
            b.act(junk, xs[i], AF.Square, [f"xs{i}"], ["junk", f"ss{i}"], accum_out=ss[i])
            b.act(sd[i], ss[i], AF.Sqrt, [f"ss{i}"], [f"sd{i}"], scale=1.0 / D, bias=eps_t)
            b.recip(rs[i], sd[i], [f"sd{i}"], [f"rs{i}"])

        def stage_b(t):
            i = t % 4
            col = 1 if t < 2 else 0
            b.act(xnb[i], xs[i], AF.Copy, [f"xs{i}", f"rs{i}"], [f"xnb{i}"], scale=rs[i])
            pb = PSB(i)
            for fb in range(8):
                b.tr(pb[:, fb * 128:(fb + 1) * 128], xnb[i][:, fb * 128:(fb + 1) * 128], ident,
                     [f"xnb{i}", "ident"], [f"ps{i}"])
            pv = pb.rearrange("p (f t) -> p f t", f=8)
            b.tt("dve", evt[i % 2], pv, modA[L][:, :, col].unsqueeze(2).to_broadcast([128, 8, 128]), ALU.mult,
                 [f"ps{i}", f"modA{L}"], [f"evt{i % 2}"])
            b.tt("dve", hT[:, :, t * 128:(t + 1) * 128], evt[i % 2], modB[L][:, :, col].unsqueeze(2).to_broadcast([128, 8, 128]),
                 ALU.add, [f"evt{i % 2}", f"modB{L}"], hkeys_)

        stage_a(0); stage_a(1)
        for t in range(34):
            if t + 2 < 34:
                stage_a(t + 2)
            stage_b(t)
        P.barrier()
        ar.release()

    hkeys = [f"hT{fb}" for fb in range(8)]

    TT = [(0, 256)] + [(256 + i * 512, 512) for i in range(8)]

    ar.mark()
    hT = ar.alloc([8, NTOK], BF16)
    prologue(0, ctx_in, x_in, hT)
    if stop == 'B':
        P.barrier()
        dbg("hT", hT, [128, 8, NTOK], BF16)

    if stop == 'B':
        return done()
    ar.mark()
    cos_t = ar.alloc([SEQ], F32)
    sin_t = ar.alloc([SEQ], F32)
    b.dma("sp", cos_t, c_cos, [], ["cos_t"], "c3")
    b.dma("sp", sin_t, c_sin, [], ["sin_t"], "c4")
    wst = [ar.alloc([8, 128], F32) for _ in range(2)]
    wbf = [ar.alloc([8, 128], BF16) for _ in range(4)]
    ostg = [ar.alloc([NTOK], BF16) for _ in range(2)]
    qb = [ar.alloc([512], BF16) for _ in range(2)]
    r1 = [ar.alloc([512], F32) for _ in range(2)]
    r2 = [ar.alloc([512], F32) for _ in range(2)]
    sg = [ar.alloc([512], F32) for _ in range(2)]
    cnt = {"w": 0, "o": 0, "ps": 0, "t": 0}

    def proj_fm(wslot_keys, wtile, ps_idx, t0, n):
        for kc in range(8):
            b.mm(PS(ps_idx)[:, 0:n], wtile[:, kc, :], hT[:, kc, t0:t0 + n], kc == 0, kc == 7,
                 hkeys + wslot_keys, [f"ps{ps_idx}"])

    def load_w(c0):
        i = cnt["w"]; cnt["w"] += 1
        s2, s4 = i % 2, i % 4
        b.dma("sp", wst[s2], ev_in_w[:, c0:c0 + 128].rearrange("(kc p) f -> p kc f", p=128), [], [f"wstc{s2}"], f"wstc{s2}")
        b.copy("dve" if i % 2 == 0 else "act", wbf[s4], wst[s2], [f"wstc{s2}"], [f"wbfc{s4}"])
        return wbf[s4], [f"wbfc{s4}"]

    PROJ_BANKS = [0, 1, 4, 5]
    for which, dst in ((0, d_q), (1, d_k)):
        for h in range(8):
            wt, wk = load_w(which * 1024 + h * 128)
            oi = cnt["o"] % 2; cnt["o"] += 1
            og = ostg[oi]
            pend = None
            for (t0, n) in TT:
                pi = PROJ_BANKS[cnt["ps"] % 4]; cnt["ps"] += 1
                proj_fm(wk, wt, pi, t0, n)
                if pend is not None:
                    pend(); pend = None
                if t0 == 0:
                    b.copy("act", og[:, 0:n], PS(pi)[:, 0:n], [f"ps{pi}"], [f"ostg{oi}"])
                else:
                    ti = cnt["t"] % 2; cnt["t"] += 1
                    b.copy("act", qb[ti], PS(pi), [f"ps{pi}"], [f"qb{ti}"])

                    def rope(ti=ti, t0=t0, og=og, oi=oi):
                        b.mm(PS(2 + ti), perm, qb[ti], True, True, ["perm", f"qb{ti}"], [f"ps{2 + ti}"])
                        s0 = t0 - CTX
                        b.tt("dve", r1[ti], qb[ti], cos_t[:, s0:s0 + 512], ALU.mult, [f"qb{ti}", "cos_t"], [f"r1{ti}"])
                        b.tt("dve", r2[ti], PS(2 + ti), sin_t[:, s0:s0 + 512], ALU.mult, [f"ps{2 + ti}", "sin_t"], [f"r2{ti}"])
                        b.tt("dve", og[:, t0:t0 + 512], r1[ti], r2[ti], ALU.add, [f"r1{ti}", f"r2{ti}"], [f"ostg{oi}"])
                    pend = rope
            if pend is not None:
                pend(); pend = None
            b.dma("pool", dst[h], og, [f"ostg{oi}"], [], f"ost{oi}")
    for base, dst in ((3072, d_sga), (6144, d_sgb)):
        for fb in range(8):
            wt, wk = load_w(base + fb * 128)
            oi = cnt["o"] % 2; cnt["o"] += 1
            og = ostg[oi]
            for (t0, n) in TT:
                pi = PROJ_BANKS[cnt["ps"] % 4]; cnt["ps"] += 1
                proj_fm(wk, wt, pi, t0, n)
                b.act(og[:, t0:t0 + n], PS(pi)[:, 0:n], AF.Silu, [f"ps{pi}"], [f"ostg{oi}"])
            b.dma("pool", dst[fb], og, [f"ostg{oi}"], [], f"ost{oi}")
    gc_ = 0
    for fb in range(8):
        wa, wka = load_w(4096 + fb * 128)
        wb_, wkb = load_w(5120 + fb * 128)
        oi = cnt["o"] % 2; cnt["o"] += 1
        og = ostg[oi]
        for (t0, n) in TT:
            ba = 4 + 2 * (gc_ % 2); bb_ = 5 + 2 * (gc_ % 2); gc_ += 1
            proj_fm(wka, wa, ba, t0, n)
            proj_fm(wkb, wb_, bb_, t0, n)
            ti = cnt["t"] % 2; cnt["t"] += 1
            b.act(sg[ti][:, 0:n], PS(bb_)[:, 0:n], AF.Sigmoid, [f"ps{bb_}"], [f"sg{ti}"])
            b.tt("dve", og[:, t0:t0 + n], PS(ba)[:, 0:n], sg[ti][:, 0:n], ALU.mult, [f"ps{ba}", f"sg{ti}"], [f"ostg{oi}"])
        b.dma("pool", d_h[fb], og, [f"ostg{oi}"], [], f"ost{oi}")
    P.barrier()
    ar.release()
    ar.mark()
    wvs = ar.alloc([8, 512], F32)
    wv = ar.alloc([8, D], BF16)
    for half in range(2):
        b.dma("sp", wvs, ev_in_w[:, 2048 + half * 512:2048 + (half + 1) * 512].rearrange("(kc p) f -> p kc f", p=128),
              [], ["wvs"], "wvs")
        b.copy("dve", wv[:, :, half * 512:(half + 1) * 512], wvs, ["wvs"], [f"wv{half}"])
    vst = [ar.alloc([D], BF16) for _ in range(2)]
    for t in range(34):
        i = t % 2
        for half in range(2):
            for kc in range(8):
                b.mm(PS(6 + half), hT[:, kc, t * 128:(t + 1) * 128], wv[:, kc, half * 512:(half + 1) * 512],
                     kc == 0, kc == 7, hkeys + [f"wv{half}"], [f"ps{6 + half}"])
            b.copy("act" if half == 0 else "dve", vst[i][:, half * 512:(half + 1) * 512], PS(6 + half),
                   [f"ps{6 + half}"], [f"vst{i}"])
        b.dma("pool", d_v[t * 128:(t + 1) * 128, :], vst[i], [f"vst{i}"], [], f"vst{i}")
    P.barrier()
    ar.release()
    ar.release()
    if stop == 'C1':
        return done()
    ar.mark()
    qz = [[ar.alloc([NTOK], BF16) for _ in range(2)] for _ in range(2)]
    for i_ in range(2):
        b.memset("pool", qz[i_][0][64:128, :], 0.0, [f"qz{i_}0"])
        b.memset("pool", qz[i_][1][0:64, :], 0.0, [f"qz{i_}1"])
    kT = [ar.alloc([NTOK], BF16) for _ in range(2)]
    vh = [ar.alloc([34, 128], BF16) for _ in range(2)]
    sga = [ar.alloc([NTOK], BF16) for _ in range(2)]
    mixo = [ar.alloc([NTOK], BF16) for _ in range(2)]
    NE = 8
    E = [ar.alloc([512], BF16) for _ in range(NE)]
    EP = [ar.alloc([512], BF16) for _ in range(6)]
    EQ = [ar.alloc([512], BF16) for _ in range(4)]
    R = [ar.alloc([512], F32) for _ in range(2)]
    Osb = [ar.alloc([512], F32) for _ in range(2)]
    o1 = ar.alloc([512], F32); oo = ar.alloc([512], F32)
    sq = ar.alloc([512], BF16); sdn = ar.alloc([512], F32); rsn = ar.alloc([512], F32); an = ar.alloc([512], F32)
    gcnt = {"s": 0, "e": 0, "p": 0, "q": 0}
    for h in range(8):
        hi_ = h % 2
        b.dma("sp", qz[hi_][0][0:64, :], d_q[h][0:64, :], [], [f"qz{hi_}0"], f"aq{hi_}0")
        b.dma("sp", qz[hi_][1][64:128, :], d_q[h][64:128, :], [], [f"qz{hi_}1"], f"aq{hi_}1")
        b.dma("sp", kT[hi_], d_k[h], [], [f"kT{hi_}"], f"ak{hi_}")
        b.dma("sp", vh[hi_], d_v[:, h * 128:(h + 1) * 128].rearrange("(kt p) f -> p kt f", p=128), [], [f"vh{hi_}"], f"av{hi_}")
        b.dma("sp", sga[hi_], d_sga[h], [], [f"sga{hi_}"], f"ag{hi_}")
        for (t0, n) in TT:
            kts = list(range(2)) if t0 == 0 else list(range(34))
            items = [(kt, c) for kt in kts for c in range(2)]
            sb = {}
            eb = {}
            zq = []
            pend = [None, None]
            zstarted = [False, False]

            def emit_S(idx):
                kt, c = items[idx]
                si = gcnt["s"] % 4; gcnt["s"] += 1
                sb[idx] = si
                b.mm(PS(si)[:, 0:n], kT[hi_][:, kt * 128:(kt + 1) * 128],
                     qz[hi_][c][:, t0:t0 + n], True, True, [f"qz{hi_}{c}", f"kT{hi_}"], [f"ps{si}"])

            LOOK = 3
            for idx in range(min(LOOK, len(items))):
                emit_S(idx)
            for idx, (kt, c) in enumerate(items):
                if idx + LOOK < len(items):
                    emit_S(idx + LOOK)
                si = sb[idx]
                ei = gcnt["e"] % NE; gcnt["e"] += 1
                eb[idx] = ei
                b.act(E[ei][:, 0:n], PS(si)[:, 0:n], AF.Exp, [f"ps{si}"], [f"E{ei}"], scale=0.125)
                first = kt == kts[0]; last = kt == kts[-1]
                b.mm(PS(4 + c)[:, 0:n], vh[hi_][:, kt, :], E[ei][:, 0:n], first, last, [f"vh{hi_}", f"E{ei}"], [f"ps{4 + c}"])
                if kt % 2 == 1:
                    e_prev = eb[idx - 2]
                    pi_ = gcnt["p"] % 6; gcnt["p"] += 1
                    b.tt("dve", EP[pi_][:, 0:n], E[e_prev][:, 0:n], E[ei][:, 0:n], ALU.add, [f"E{e_prev}", f"E{ei}"], [f"EP{pi_}"])
                    if pend[c] is None and not last:
                        pend[c] = pi_
                    else:
                        if pend[c] is not None:
                            qi_ = gcnt["q"] % 4; gcnt["q"] += 1
                            b.tt("dve", EQ[qi_][:, 0:n], EP[pend[c]][:, 0:n], EP[pi_][:, 0:n], ALU.add,
                                 [f"EP{pend[c]}", f"EP{pi_}"], [f"EQ{qi_}"])
                            src, skey = EQ[qi_], f"EQ{qi_}"
                            pend[c] = None
                        else:
                            src, skey = EP[pi_], f"EP{pi_}"
                        zq.append((idx, c, src, skey, not zstarted[c], last))
                        zstarted[c] = True
                while zq and (zq[0][0] + 3 <= idx or idx == len(items) - 1):
                    _, c_, src_, sk_, f_, l_ = zq.pop(0)
                    b.mm(PS(6 + c_)[:, 0:n], ones_bf, src_[:, 0:n], f_, l_, ["ones_bf", sk_], [f"ps{6 + c_}"])
            b.copy("act", Osb[0][:, 0:n], PS(4)[:, 0:n], ["ps4"], ["Osb0"])
            b.copy("dve", Osb[1][:, 0:n], PS(5)[:, 0:n], ["ps5"], ["Osb1"])
            b.recip(R[0][:, 0:n], PS(6)[:, 0:n], ["ps6"], ["R0"])
            b.recip(R[1][:, 0:n], PS(7)[:, 0:n], ["ps7"], ["R1"])
            b.tt("dve", Osb[0][:, 0:n], Osb[0][:, 0:n], R[0][:, 0:n], ALU.mult, ["Osb0", "R0"], ["Osb0"])
            b.tt("dve", o1[:, 0:n], Osb[1][:, 0:n], R[1][:, 0:n], ALU.mult, ["Osb1", "R1"], ["o1"])
            b.stt(oo[:, 0:n], o1[:, 0:n], neglam, Osb[0][:, 0:n], ALU.mult, ALU.add, ["Osb0", "o1", "neglam"], ["oo"])
            b.tt("dve", sq[:, 0:n], oo[:, 0:n], oo[:, 0:n], ALU.mult, ["oo"], ["sq"])
            sN = gcnt["s"] % 4; gcnt["s"] += 1
            b.mm(PS(sN)[:, 0:n], ones_bf, sq[:, 0:n], True, True, ["ones_bf", "sq"], [f"ps{sN}"])
            b.ts("dve", sdn[:, 0:n], PS(sN)[:, 0:n], 1.0 / 128, EPS, ALU.mult, ALU.add, [f"ps{sN}"], ["sdn"])
            b.act(rsn[:, 0:n], sdn[:, 0:n], AF.Ln, ["sdn"], ["rsn"])
            b.act(rsn[:, 0:n], rsn[:, 0:n], AF.Exp, ["rsn"], ["rsn"], scale=-0.5)
            b.tt("dve", an[:, 0:n], oo[:, 0:n], rsn[:, 0:n], ALU.mult, ["oo", "rsn"], ["an"])
            b.stt(mixo[hi_][:, t0:t0 + n], an[:, 0:n], gsub, sga[hi_][:, t0:t0 + n], ALU.mult, ALU.mult,
                  ["an", "gsub", f"sga{hi_}"], [f"mixo{hi_}"])
        b.dma("pool", d_mix[h], mixo[hi_], [f"mixo{hi_}"], [], f"amix{hi_}")
    P.barrier()
    ar.release()
    if stop == 'C2':
        return done()
    ar.mark()
    dww = ar.alloc([8, 31], F32); dwb = ar.alloc([8], F32)
    b.dma("sp", dww, ev_dw_w, [], ["dww"], "c5")
    b.dma("sp", dwb, ev_dw_b, [], ["dwb"], "c6")
    identf2 = ar.alloc([128], F32)
    b.copy("dve", identf2, ident, ["ident"], ["identf2"])
    dg = [ar.alloc([31, 128], BF16) for _ in range(2)]
    PADW = 15 + CTX + 30 + SEQ + 15
    XOFF = 15 + CTX + 30
    hp = [ar.alloc([PADW], BF16) for _ in range(2)]
    for i_ in range(2):
        b.memset("pool", hp[i_], 0.0, [f"hp{i_}"])
    yc = [ar.alloc([NTOK], F32) for _ in range(2)]
    yd = [ar.alloc([NTOK], F32) for _ in range(2)]
    ccnt = 0
    DVE_TAPS = [0, 1, 2, 3, 4, 26, 27, 28, 29, 30]
    PE_TAPS = [k for k in range(31) if k not in DVE_TAPS]
    SEGS = ((0, CTX, 15), (CTX, NTOK, XOFF - CTX))

    def conv_load(fb):
        i = fb % 2
        b.dma("sp", hp[i][:, 15:15 + CTX], d_h[fb][:, 0:CTX], [], [f"hp{i}"], f"chc{i}")
        b.dma("sp", hp[i][:, XOFF:XOFF + SEQ], d_h[fb][:, CTX:NTOK], [], [f"hp{i}"], f"chx{i}")
        for k in PE_TAPS:
            b.ts("dve", dg[i][:, k, :], identf2, dww[:, fb, k:k + 1], None, ALU.mult, None, ["identf2", "dww"], [f"dg{i}"])
        for (c0, c1, off) in SEGS:
            for kk, k in enumerate(DVE_TAPS):
                src = hp[i][:, off + c0 + k - 15:off + c1 + k - 15]
                if kk == 0:
                    b.ts("dve", yd[i][:, c0:c1], src, dww[:, fb, k:k + 1], None, ALU.mult, None, [f"hp{i}", "dww"], [f"yd{i}_{c0}"])
                else:
                    b.stt(yd[i][:, c0:c1], src, dww[:, fb, k:k + 1], yd[i][:, c0:c1], ALU.mult, ALU.add,
                          [f"hp{i}", "dww", f"yd{i}_{c0}"], [f"yd{i}_{c0}"])

    def conv_mm(fb):
        nonlocal ccnt
        i = fb % 2
        keys = []
        for (t0, n) in TT:
            base = 15 if t0 == 0 else XOFF + (t0 - CTX)
            pi2 = ccnt % 2; ccnt += 1
            for kk, k in enumerate(PE_TAPS):
                b.mm(PS(pi2)[:, 0:n], dg[i][:, k, :], hp[i][:, base + k - 15:base + k - 15 + n], kk == 0, kk == len(PE_TAPS) - 1,
                     [f"dg{i}", f"hp{i}"], [f"ps{pi2}"])
            yk_ = f"yc{i}_{t0}"
            ydk = f"yd{i}_{0 if t0 == 0 else CTX}"
            b.stt(yc[i][:, t0:t0 + n], PS(pi2)[:, 0:n], dwb[:, fb:fb + 1], yd[i][:, t0:t0 + n], ALU.add, ALU.add,
                  [f"ps{pi2}", "dwb", ydk], [yk_])
            keys.append(yk_)
        b.dma("pool", d_y[fb], yc[i], keys, [], f"cyc{i}")

    conv_load(0)
    for fb in range(8):
        if fb + 1 < 8:
            conv_load(fb + 1)
        conv_mm(fb)
    P.barrier()
    ar.release()
    if stop == 'C3a':
        return done()
    ar.mark()
    clg = ar.alloc([8], F32); clb = ar.alloc([8], F32)
    b.dma("sp", clg, ev_cln_g, [], ["clg"], "c7")
    b.dma("sp", clb, ev_cln_b, [], ["clb"], "c8")
    yl = [ar.alloc([8, 512], F32) for _ in range(2)]
    gl = [ar.alloc([8, 512], BF16) for _ in range(2)]
    ysq = ar.alloc([8, 512], F32)
    mean = [ar.alloc([512], F32) for _ in range(2)]; msq = ar.alloc([512], F32); var = ar.alloc([512], F32)
    sdl = ar.alloc([512], F32); rsl = [ar.alloc([512], F32) for _ in range(2)]
    tAll = ar.alloc([8, 512], F32); tC = ar.alloc([8, 512], F32)
    mo = [ar.alloc([8, 512], BF16) for _ in range(2)]

    def ln_a(ti_):
        t0, n = TT[ti_]
        i = ti_ % 2
        b.dma("sp", yl[i][:, :, 0:n], d_y[:, :, t0:t0 + n].rearrange("f p t -> p f t"), [], [f"yl{i}"], f"lyl{i}")
        b.dma("sp", gl[i][:, :, 0:n], d_sgb[:, :, t0:t0 + n].rearrange("f p t -> p f t"), [], [f"gl{i}"], f"lgl{i}")
        b.act(ysq[:, :, 0:n], yl[i][:, :, 0:n], AF.Square, [f"yl{i}"], ["ysq"])
        for fb in range(8):
            b.mm(PS(0)[:, 0:n], ones_f, yl[i][:, fb, 0:n], fb == 0, fb == 7, ["ones_f", f"yl{i}"], ["ps0"])
        for fb in range(8):
            b.mm(PS(1)[:, 0:n], ones_f, ysq[:, fb, 0:n], fb == 0, fb == 7, ["ones_f", "ysq"], ["ps1"])
        b.ts("dve", mean[i][:, 0:n], PS(0)[:, 0:n], 1.0 / D, None, ALU.mult, None, ["ps0"], [f"mean{i}"])
        b.tt("dve", msq[:, 0:n], mean[i][:, 0:n], mean[i][:, 0:n], ALU.mult, [f"mean{i}"], ["msq"])
        b.stt(var[:, 0:n], PS(1)[:, 0:n], 1.0 / D, msq[:, 0:n], ALU.mult, ALU.subtract, ["ps1", "msq"], ["var"])
        b.act(sdl[:, 0:n], var[:, 0:n], AF.Sqrt, ["var"], ["sdl"], bias=eps_t)
        b.recip(rsl[i][:, 0:n], sdl[:, 0:n], ["sdl"], [f"rsl{i}"])

    def ln_b(ti_):
        t0, n = TT[ti_]
        i = ti_ % 2
        b.tt("dve", tAll[:, :, 0:n], yl[i][:, :, 0:n], mean[i][:, 0:n].unsqueeze(1).to_broadcast([128, 8, n]), ALU.subtract,
             [f"yl{i}", f"mean{i}"], ["tAll"])
        b.tt("dve", tAll[:, :, 0:n], tAll[:, :, 0:n], rsl[i][:, 0:n].unsqueeze(1).to_broadcast([128, 8, n]), ALU.mult,
             ["tAll", f"rsl{i}"], ["tAll"])
        for fb in range(8):
            b.act(tC[:, fb, 0:n], tAll[:, fb, 0:n], AF.Silu, ["tAll", "clg", "clb"], ["tC"],
                  scale=clg[:, fb:fb + 1], bias=clb[:, fb:fb + 1])
        b.tt("dve", mo[i][:, :, 0:n], tC[:, :, 0:n], gl[i][:, :, 0:n], ALU.mult, ["tC", f"gl{i}"], [f"mo{i}"])
        b.dma("pool", d_mix[8:16, :, t0:t0 + n].rearrange("f p t -> p f t"), mo[i][:, :, 0:n], [f"mo{i}"], [], f"lmo{i}")

    ln_a(0)
    for ti_ in range(len(TT)):
        if ti_ + 1 < len(TT):
            ln_a(ti_ + 1)
        ln_b(ti_)
    P.barrier()
    ar.release()
    if stop == 'C3b':
        return done()
    ar.mark()
    wo = ar.alloc([16, D], BF16)
    wos = ar.alloc([4, D], F32)
    for q4 in range(4):
        b.dma("sp", wos, ev_out_w[q4 * 512:(q4 + 1) * 512, :].rearrange("(kc p) f -> p kc f", p=128), [], ["wos"], "wos")
        b.copy("dve" if q4 % 2 == 0 else "act", wo[:, q4 * 4:(q4 + 1) * 4, :], wos, ["wos"], [f"wo{q4}"])
    wokeys = [f"wo{q4}" for q4 in range(4)]
    mt = [ar.alloc([16, 128], BF16) for _ in range(2)]
    xr = [ar.alloc([D], F32) for _ in range(2)]
    xo = [ar.alloc([D], F32) for _ in range(2)]
    tmpo = [ar.alloc([512], F32) for _ in range(2)]
    for t in range(34):
        i = t % 2
        col = 1 if t < 2 else 0
        src = ctx_in[t * 128:(t + 1) * 128, :] if t < 2 else x_in[(t - 2) * 128:(t - 1) * 128, :]
        b.dma("sp", mt[i], d_mix[:, :, t * 128:(t + 1) * 128].rearrange("c p t -> p c t"), [], [f"mt{i}"], f"omt{i}")
        b.dma("sp", xr[i], src, [], [f"xr{i}"], f"oxr{i}")
        for half in range(2):
            for ck in range(16):
                b.mm(PS(half), mt[i][:, ck, :], wo[:, ck, half * 512:(half + 1) * 512], ck == 0, ck == 15,
                     [f"mt{i}"] + wokeys, [f"ps{half}"])
            b.tt("dve", tmpo[half], PS(half), gtrow[0][col][:, half * 512:(half + 1) * 512], ALU.mult,
                 [f"ps{half}", f"gtrow0{col}"], [f"tmpo{half}"])
            b.tt("dve", xo[i][:, half * 512:(half + 1) * 512], tmpo[half], xr[i][:, half * 512:(half + 1) * 512], ALU.add,
                 [f"tmpo{half}", f"xr{i}"], [f"xo{i}"])
        b.dma("pool", d_x1[t * 128:(t + 1) * 128, :], xo[i], [f"xo{i}"], [], f"oxo{i}")
    P.barrier()
    ar.release()

    if stop == 'C4':
        return done()
    ar.mark()
    hT = ar.alloc([8, NTOK], BF16)
    prologue(1, d_x1[0:CTX, :], d_x1[CTX:NTOK, :], hT)
    ar.mark()
    wst = [ar.alloc([8, 128], F32) for _ in range(2)]
    wbf = [ar.alloc([8, 128], BF16) for _ in range(4)]
    ostg = [ar.alloc([NTOK], BF16) for _ in range(2)]
    cnt = {"w": 0, "o": 0, "ps": 0}

    def load_w1(c0):
        i = cnt["w"]; cnt["w"] += 1
        s2, s4 = i % 2, i % 4
        b.dma("sp", wst[s2], od_in_w[:, c0:c0 + 128].rearrange("(kc p) f -> p kc f", p=128), [], [f"wstc{s2}"], f"wste{s2}")
        b.copy("dve" if i % 2 == 0 else "act", wbf[s4], wst[s2], [f"wstc{s2}"], [f"wbfc{s4}"])
        return wbf[s4], [f"wbfc{s4}"]

    for fb in range(16):
        wt, wk = load_w1(fb * 128)
        oi = cnt["o"] % 2; cnt["o"] += 1
        og = ostg[oi]
        for (t0, n) in TT:
            if fb >= 8 and t0 == 0:
                continue
            pi = cnt["ps"] % 2; cnt["ps"] += 1
            for kc in range(8):
                b.mm(PS(pi)[:, 0:n], wt[:, kc, :], hT[:, kc, t0:t0 + n], kc == 0, kc == 7, hkeys + wk, [f"ps{pi}"])
            if fb < 8:
                b.copy("act" if pi == 0 else "dve", og[:, t0:t0 + n], PS(pi)[:, 0:n], [f"ps{pi}"], [f"ostg{oi}"])
            else:
                b.act(og[:, t0:t0 + n], PS(pi)[:, 0:n], AF.Silu, [f"ps{pi}"], [f"ostg{oi}"])
        if fb < 8:
            b.dma("pool", d_u[fb], og, [f"ostg{oi}"], [], f"eo{oi}")
        else:
            b.dma("pool", d_sz[fb - 8], og[:, CTX:NTOK], [f"ostg{oi}"], [], f"eo{oi}")
    P.barrier()
    ar.release()
    ar.release()

    if stop == 'E':
        return done()
    A8r = ar.alloc([2, 32], F32)
    A8i = ar.alloc([2, 32], F32)
    A8in = ar.alloc([2, 32], F32)
    ar.mark()
    Kt = ar.alloc([64, 128], BF16)
    Wbt = ar.alloc([2, 32, 2, 128], BF16)
    Wd = ar.alloc([2, 32, 2, 128], BF16)
    ar.mark()
    lre = ar.alloc([64], F32); lim = ar.alloc([64], F32); ldt = ar.alloc([64], F32)
    b.dma("sp", lre, s5_lre, [], ["lre"], "f1")
    b.dma("sp", lim, s5_lim, [], ["lim"], "f2")
    b.dma("sp", ldt, s5_ldt, [], ["ldt"], "f3")
    kvec = ar.alloc([17], F32)
    b.dma("sp", kvec, c_kvec, [], ["kvec"], "f4")
    maskf = ar.alloc([128], F32); maskb = ar.alloc([128], F32)
    b.dma("sp", maskf, c_maskf, [], ["maskf"], "f5")
    b.dma("sp", maskb, c_maskb, [], ["maskb"], "f6")
    dts = ar.alloc([64], F32)
    b.act(dts, ldt, AF.Exp, ["ldt"], ["dts"])
    xr_ = ar.alloc([64], F32); th = ar.alloc([64], F32)
    b.tt("dve", xr_, lre, dts, ALU.mult, ["lre", "dts"], ["xr_"])
    b.tt("dve", th, lim, dts, ALU.mult, ["lim", "dts"], ["th"])
    PWr = ar.alloc([17, 64], F32); PWi = ar.alloc([17, 64], F32)
    ar.mark()
    ARG = ar.alloc([17, 2, 64], F32)
    b.tt("dve", ARG[:, :, 0, :], th.unsqueeze(1).to_broadcast([128, 17, 64]), kvec.unsqueeze(2).to_broadcast([128, 17, 64]),
         ALU.mult, ["th", "kvec"], ["ARG0"])
    b.ts("dve", ARG[:, :, 1, :], ARG[:, :, 0, :], math.pi / 2, None, ALU.add, None, ["ARG0"], ["ARG1"])
    ARGf = ARG.rearrange("p a b c -> p (a b c)")
    T1 = ar.alloc([17 * 2 * 64], F32)
    ki = ar.alloc([17 * 2 * 64], I32)
    b.ts("dve", T1, ARGf, 1.0 / (2 * math.pi), None, ALU.mult, None, ["ARG0", "ARG1"], ["T1"])
    b.copy("dve", ki, T1, ["T1"], ["ki"])
    b.copy("dve", T1, ki, ["ki"], ["T1"])
    b.stt(T1, T1, -2 * math.pi, ARGf, ALU.mult, ALU.add, ["T1", "ARG0", "ARG1"], ["T1"])
    b.ts("dve", T1, T1, 3.1415925, -3.1415925, ALU.min, ALU.max, ["T1"], ["T1"])
    SC = ar.alloc([17, 2, 64], F32)
    b.act(SC.rearrange("p a b c -> p (a b c)"), T1, AF.Sin, ["T1"], ["SC"])
    MG = ar.alloc([17, 64], F32)
    b.tt("dve", MG, xr_.unsqueeze(1).to_broadcast([128, 17, 64]), kvec.unsqueeze(2).to_broadcast([128, 17, 64]),
         ALU.mult, ["xr_", "kvec"], ["MGa"])
    b.act(MG, MG, AF.Exp, ["MGa"], ["MG"])
    b.tt("dve", PWr, MG, SC[:, :, 1, :], ALU.mult, ["MG", "SC"], ["PWr"])
    b.tt("dve", PWi, MG, SC[:, :, 0, :], ALU.mult, ["MG", "SC"], ["PWi"])
    P.barrier()
    ar.release()
    b.copy("dve", A8r.rearrange("p a b -> p (a b)"), PWr[:, 16, :], ["PWr"], ["A8r"])
    b.copy("dve", A8i.rearrange("p a b -> p (a b)"), PWi[:, 16, :], ["PWi"], ["A8i"])
    b.ts("dve", A8in.rearrange("p a b -> p (a b)"), PWi[:, 16, :], -1.0, None, ALU.mult, None, ["PWi"], ["A8in"])
    ar1 = ar.alloc([64], F32); den = ar.alloc([64], F32); t_a = ar.alloc([64], F32); t_b = ar.alloc([64], F32)
    fr = ar.alloc([64], F32); fi = ar.alloc([64], F32); rden = ar.alloc([64], F32)
    b.ts("dve", ar1, PWr[:, 9, :], -1.0, None, ALU.add, None, ["PWr"], ["ar1"])
    b.tt("dve", den, lre, lre, ALU.mult, ["lre"], ["den_a"])
    b.tt("dve", t_a, lim, lim, ALU.mult, ["lim"], ["t_a"])
    b.tt("dve", den, den, t_a, ALU.add, ["den_a", "t_a"], ["den"])
    b.recip(rden, den, ["den"], ["rden"])
    b.tt("dve", t_a, ar1, lre, ALU.mult, ["ar1", "lre", "den"], ["t_a"])
    b.tt("dve", t_b, PWi[:, 9, :], lim, ALU.mult, ["PWi", "lim"], ["t_b"])
    b.tt("dve", t_a, t_a, t_b, ALU.add, ["t_a", "t_b"], ["t_a"])
    b.tt("dve", fr, t_a, rden, ALU.mult, ["t_a", "rden"], ["fr"])
    b.tt("dve", t_a, PWi[:, 9, :], lre, ALU.mult, ["PWi", "lre", "fr"], ["t_a"])
    b.tt("dve", t_b, ar1, lim, ALU.mult, ["ar1", "lim", "t_a"], ["t_b"])
    b.tt("dve", t_a, t_a, t_b, ALU.subtract, ["t_a", "t_b"], ["t_a"])
    b.tt("dve", fi, t_a, rden, ALU.mult, ["t_a", "rden"], ["fi"])
    bre = ar.alloc([64, 16], F32); bim = ar.alloc([64, 16], F32)
    cre = ar.alloc([64, 16], F32); cim = ar.alloc([64, 16], F32)
    b.dma("sp", bre, s5_bre, [], ["bre"], "f7")
    b.dma("sp", bim, s5_bim, [], ["bim"], "f8")
    b.dma("sp", cre, s5_cre, [], ["cre"], "f9")
    b.dma("sp", cim, s5_cim, [], ["cim"], "f10")
    bbr = ar.alloc([64, 16], F32); bbi = ar.alloc([64, 16], F32)
    v1 = ar.alloc([16, 8, 16], F32); v2 = ar.alloc([16, 8, 16], F32)
    u1 = v1.rearrange("p a b c -> p (a b c)")[:, 0:1024].rearrange("p (a b) -> p a b", a=64)
    u2 = v2.rearrange("p a b c -> p (a b c)")[:, 0:1024].rearrange("p (a b) -> p a b", a=64)
    frb = fr.unsqueeze(2).to_broadcast([128, 64, 16]); fib = fi.unsqueeze(2).to_broadcast([128, 64, 16])
    b.tt("dve", u1, bre, frb, ALU.mult, ["bre", "fr"], ["v1"])
    b.tt("dve", u2, bim, fib, ALU.mult, ["bim", "fi"], ["v2"])
    b.tt("dve", bbr, u1, u2, ALU.subtract, ["v1", "v2"], ["bbr"])
    b.tt("dve", u1, bim, frb, ALU.mult, ["bim", "fr", "bbr"], ["v1"])
    b.tt("dve", u2, bre, fib, ALU.mult, ["bre", "fi", "bbr"], ["v2"])
    b.tt("dve", bbi, u1, u2, ALU.add, ["v1", "v2"], ["bbi"])
    dsh = ar.alloc([64], F32)
    b.dma("sp", dsh, s5_d, [], ["dsh"], "f11")
    identf = ar.alloc([128], F32)
    b.copy("dve", identf, ident, ["ident"], ["identf"])
    Xr = ar.alloc([32, 8, 16], BF16); Xi = ar.alloc([32, 8, 16], BF16)

    def cexp(pw_slice, dr, fac_r, fac_i, fkeys, out_r, out_i, kr, ki_, neg_im):
        for hf in range(2):
            q0 = dr * 32 + hf * 16
            pr = PWr[:, pw_slice, q0:q0 + 16].rearrange("p s q -> p q s").unsqueeze(3).to_broadcast([128, 16, 8, 16])
            pi_ = PWi[:, pw_slice, q0:q0 + 16].rearrange("p s q -> p q s").unsqueeze(3).to_broadcast([128, 16, 8, 16])
            fr_ = fac_r[:, q0:q0 + 16, :].unsqueeze(2).to_broadcast([128, 16, 8, 16])
            fi_ = fac_i[:, q0:q0 + 16, :].unsqueeze(2).to_broadcast([128, 16, 8, 16])
            o_r = out_r[:, hf * 16:(hf + 1) * 16]
            o_i = out_i[:, hf * 16:(hf + 1) * 16]
            b.tt("dve", v1, pr, fr_, ALU.mult, ["PWr"] + fkeys, ["v1"])
            b.tt("dve", v2, pi_, fi_, ALU.mult, ["PWi"] + fkeys, ["v2"])
            b.tt("dve", o_r, v1, v2, ALU.subtract, ["v1", "v2"], [kr])
            b.tt("dve", v1, pr, fi_, ALU.mult, ["PWr", kr] + fkeys, ["v1"])
            b.tt("dve", v2, pi_, fr_, ALU.mult, ["PWi", kr] + fkeys, ["v2"])
            if neg_im:
                b.stt(o_i, v1, -1.0, v2, ALU.mult, ALU.subtract, ["v1", "v2"], [ki_])
            else:
                b.tt("dve", o_i, v1, v2, ALU.add, ["v1", "v2"], [ki_])

    for dr in range(2):
        if dr == 0:
            sX = slice(15, 7, -1); sX2 = slice(7, None, -1); sY = slice(9, 17)
        else:
            sX = slice(8, 16); sX2 = slice(0, 8); sY = slice(16, 8, -1)
        Yr = Wd[:, dr, :, 0, :].rearrange("p q (t h) -> p q t h", t=8)
        Yin = Wd[:, dr, :, 1, :].rearrange("p q (t h) -> p q t h", t=8)
        cexp(sY, dr, cre, cim, ["cre", "cim"], Yr, Yin, f"Wdr{dr}", f"Wdi{dr}", True)
        cexp(sX, dr, bbr, bbi, ["bbr", "bbi"], Xr, Xi, "Xr", "Xi", False)
        for pr_ in range(32):
            for ri, Xs, kx in ((0, Xr, "Xr"), (1, Xi, "Xi")):
                pb = PSB(pr_ % 2)
                col = ri * 128
                b.tr(pb[:, col:col + 128], Xs[:, pr_, :, :].rearrange("p s h -> p (s h)"), ident, [kx, "ident"], [f"ps{pr_ % 2}"])
            b.copy("act" if pr_ % 2 == 0 else "dve", Wbt[:, dr, pr_, :, :].rearrange("p r m -> p (r m)"), PSB(pr_ % 2)[:, 0:256],
                   [f"ps{pr_ % 2}"], ["Wbt"])
        cexp(sX2, dr, bbr, bbi, ["bbr", "bbi"], Xr, Xi, "Xr", "Xi", False)
        tmpk = v1.rearrange("p a b c -> p (a b c)")[:, 0:128]
        for g in range(64):
            pr_, g2 = g // 2, g % 2
            rows = slice(g2 * 64, (g2 + 1) * 64)
            pi2 = 2 + g % 2
            b.mm(PS(pi2)[:, 0:128], Xr[rows, pr_, :, :].rearrange("p s h -> p (s h)"), Wd[rows, dr, pr_, 0, :], True, False,
                 ["Xr", f"Wdr{dr}"], [f"ps{pi2}"])
            b.mm(PS(pi2)[:, 0:128], Xi[rows, pr_, :, :].rearrange("p s h -> p (s h)"), Wd[rows, dr, pr_, 1, :], False, True,
                 ["Xi", f"Wdi{dr}"], [f"ps{pi2}"])
            if dr == 0:
                b.tt("dve", tmpk, PS(pi2)[:, 0:128], maskf, ALU.mult, [f"ps{pi2}", "maskf"], ["v1"])
                b.stt(Kt[:, g, :], identf, dsh[:, g:g + 1], tmpk, ALU.mult, ALU.add, ["identf", "dsh", "v1"], ["Kt"])
            else:
                b.tt("dve", tmpk, PS(pi2)[:, 0:128], maskb, ALU.mult, [f"ps{pi2}", "maskb"], ["v1"])
                b.tt("dve", Kt[:, g, :], Kt[:, g, :], tmpk, ALU.add, ["Kt", "v1"], ["Kt"])
    P.barrier()
    ar.release()
    b.dma("sp", d_kt.rearrange("g p m -> p g m"), Kt, ["Kt"], [], "fk1")
    b.dma("sp", d_wbt.rearrange("q p d r m -> p d q r m"), Wbt, ["Wbt"], [], "fk2")
    b.dma("sp", d_wd.rearrange("q p d r m -> p d q r m"), Wd, ["Wdr0", "Wdi0", "Wdr1", "Wdi1"], [], "fk3")
    P.barrier()
    ar.release()

    if stop == 'F':
        return done()
    ar.mark()
    zsel = ar.alloc([8, 240], BF16); qsel = ar.alloc([8, 240], BF16)
    b.dma("sp", zsel, c_zsel, [], ["zsel"], "g1")
    b.dma("sp", qsel, c_qsel, [], ["qsel"], "g2")
    NJ = 544
    S = ar.alloc([2, 2, 32, NJ], BF16)
    AR2 = ar.alloc([2, 2, 32], F32); AIS = ar.alloc([2, 2, 32], F32)
    Z = [ar.alloc([2, 2, 32], F32) for _ in range(2)]
    ta = ar.alloc([2, 2, 32], F32); tb = ar.alloc([2, 2, 32], F32)
    for ri in range(2):
        b.copy("dve", AR2[:, ri, :, :], A8r, ["A8r"], ["AR2"])
    b.copy("dve", AIS[:, 0, :, :], A8in, ["A8in"], ["AIS"])
    b.copy("dve", AIS[:, 1, :, :], A8i, ["A8i"], ["AIS"])
    ar.mark()
    Uf1 = ar.alloc([NTOK], BF16)
    Uf = [Uf1, Uf1]
    wbtf = [ar.alloc([4, 512], BF16) for _ in range(2)]
    Ushp = [ar.alloc([2, NJ], BF16) for _ in range(2)]
    Udi = [ar.alloc([8, NJ], BF16) for _ in range(2)]
    pscnt = 0
    g1pend = [None]
    for fb in range(8):
        fi_ = fb % 2
        b.dma("sp", Uf[fi_], d_u[fb], [], ["Uf"], "gUf")
        b.dma("sp", wbtf[fi_], d_wbt[fb * 4:(fb + 1) * 4].rearrange("q p d r m -> p q (d r m)"), [], [f"wbtf{fi_}"], f"gwb{fi_}")
        Ud = Udi[fi_]
        b.copy("act", Ud[:, :, 32:NJ], Uf[fi_][:, CTX:NTOK].rearrange("p (j s) -> p s j", s=8), ["Uf"], [f"Ud{fi_}"])
        b.copy("dve", Ud[:, :, 0:32], Uf[fi_][:, 0:CTX].rearrange("p (j s) -> p s j", s=8), ["Uf"], [f"Ud{fi_}"])
        Ux = Ud[:, :, 32:NJ]
        Uc = Ud[:, :, 0:32]
        for pl in range(4):
            pr_ = fb * 4 + pl
            ui = pr_ % 2
            ub = Ushp[ui]
            for g2 in range(2):
                gl_ = pl * 2 + g2
                g = fb * 8 + gl_
                pi2 = pscnt % 2; pscnt += 1
                for s_ in range(8):
                    lw = zsel[:, gl_, (7 - s_) * 16:(7 - s_) * 16 + 128]
                    b.mm(PS(pi2), lw, Ux[:, s_, :], s_ == 0, s_ == 7, ["zsel", f"Ud{fi_}"], [f"ps{pi2}"])
                for s_ in range(8):
                    lw = zsel[:, gl_, (7 - s_) * 16:(7 - s_) * 16 + 128]
                    b.mm(PS(2 + pi2)[:, 0:32], lw, Uc[:, s_, :], s_ == 0, s_ == 7, ["zsel", f"Ud{fi_}"], [f"ps{2 + pi2}"])
                b.copy("act", ub[:, g2, 32:NJ], PS(pi2), [f"ps{pi2}"], [f"ub{ui}"])
                b.copy("dve", ub[:, g2, 0:32], PS(2 + pi2)[:, 0:32], [f"ps{2 + pi2}"], [f"ub{ui}"])
                b.dma("pool", d_ush[g], ub[:, g2, :], [f"ub{ui}"], [], f"gus{ui}{g2}")
            def chunk_states(pl=pl, pr_=pr_, ui=ui, ub=ub, fi_=fi_):
                nonlocal pscnt
                for dr in range(2):
                    for ri in range(2):
                        pi2 = pscnt % 2; pscnt += 1
                        c0 = (dr * 2 + ri) * 128
                        for g2 in range(2):
                            rows = slice(g2 * 64, (g2 + 1) * 64)
                            lw = wbtf[fi_][:, pl, c0 + g2 * 64:c0 + (g2 + 1) * 64]
                            b.mm(ps_t[rows, 4 + pi2, :], lw, ub[:, g2, 32:NJ], True, True, [f"wbtf{fi_}", f"ub{ui}"], [f"ps{4 + pi2}"])
                            b.mm(ps_t[rows, 6 + pi2, 0:32], lw, ub[:, g2, 0:32], True, True, [f"wbtf{fi_}", f"ub{ui}"], [f"ps{6 + pi2}"])
                        sk = f"Sw{pr_}_{dr}_{ri}"
                        if dr == 0:
                            b.copy("act", S[:, ri, dr, pr_, 32:NJ], PS(4 + pi2), [f"ps{4 + pi2}"], [sk + "x"])
                            b.copy("dve", S[:, ri, dr, pr_, 0:32], PS(6 + pi2)[:, 0:32], [f"ps{6 + pi2}"], [sk + "c"])
                        else:
                            b.copy("act", S[:, ri, dr, pr_, 543:31:-1], PS(4 + pi2), [f"ps{4 + pi2}"], [sk + "x"])
                            b.copy("dve", S[:, ri, dr, pr_, 31::-1], PS(6 + pi2)[:, 0:32], [f"ps{6 + pi2}"], [sk + "c"])
            if g1pend[0] is not None:
                g1pend[0]()
            g1pend[0] = chunk_states
    if g1pend[0] is not None:
        g1pend[0]()
    P.barrier()
    ar.release()
    b.copy("dve", Z[0], S[:, :, :, :, 0], ["S"], ["Z0"])
    for i in range(1, NJ):
        zc, zn = Z[(i - 1) % 2], Z[i % 2]
        kc_, kn_ = f"Z{(i - 1) % 2}", f"Z{i % 2}"
        b.tt("dve", ta, AR2, zc, ALU.mult, ["AR2", kc_], ["ta"])
        b.tt("dve", tb, AIS, zc[:, ::-1, :, :], ALU.mult, ["AIS", kc_], ["tb"])
        b.tt("dve", ta, ta, tb, ALU.add, ["ta", "tb"], ["ta"])
        b.tt("dve", zn, ta, S[:, :, :, :, i], ALU.add, ["ta", f"S{i}"], [kn_])
        b.copy("act", S[:, :, :, :, i], zn, [kn_], [f"S{i}"])
    P.barrier()
    ar.mark()
    ktf = [ar.alloc([8, 128], BF16) for _ in range(2)]
    wdf = [ar.alloc([4, 512], BF16) for _ in range(2)]
    ushf = ar.alloc([8, NJ], BF16)
    Ysh = ar.alloc([8, 512], BF16)
    Gb = ar.alloc([SEQ], BF16)
    for fb in range(8):
        fi_ = fb % 2
        b.dma("sp", ktf[fi_], d_kt[fb * 8:(fb + 1) * 8].rearrange("g p m -> p g m"), [], [f"ktf{fi_}"], f"gkt{fi_}")
        b.dma("sp", wdf[fi_], d_wd[fb * 4:(fb + 1) * 4].rearrange("q p d r m -> p q (d r m)"), [], [f"wdf{fi_}"], f"gwd{fi_}")
        b.dma("sp", ushf, d_ush[fb * 8:(fb + 1) * 8].rearrange("g p j -> p g j"), [], ["ushf"], "gush")
        for gl_ in range(8):
            pl, g2 = gl_ // 2, gl_ % 2
            pr_ = fb * 4 + pl
            rows = slice(g2 * 64, (g2 + 1) * 64)
            pi2 = pscnt % 2; pscnt += 1
            b.mm(PS(pi2), ktf[fi_][:, gl_, :], ushf[:, gl_, 32:NJ], True, False, [f"ktf{fi_}", "ushf"], [f"ps{pi2}"])
            for ri in range(2):
                b.mm(PS(pi2), wdf[fi_][rows, pl, (0 * 2 + ri) * 128:(0 * 2 + ri + 1) * 128], S[rows, ri, 0, pr_, 31:543],
                     False, False, [f"wdf{fi_}"], [f"ps{pi2}"])
            for ri in range(2):
                b.mm(PS(pi2), wdf[fi_][rows, pl, (1 * 2 + ri) * 128:(1 * 2 + ri + 1) * 128], S[rows, ri, 1, pr_, 542:30:-1],
                     False, ri == 1, [f"wdf{fi_}"], [f"ps{pi2}"])
            b.copy("act" if gl_ % 2 == 0 else "dve", Ysh[:, gl_, :], PS(pi2), [f"ps{pi2}"], [f"Ysh{gl_}"])
        Gv = Gb.rearrange("p (j t) -> p t j", t=8)
        for t in range(8):
            pi2 = 4 + t % 2
            for gl_ in range(8):
                lw = qsel[:, t, (7 - gl_) * 16:(7 - gl_) * 16 + 128]
                b.mm(PS(pi2), lw, Ysh[:, gl_, :], gl_ == 0, gl_ == 7, ["qsel", f"Ysh{gl_}"], [f"ps{pi2}"])
            b.act(Gv[:, t, :], PS(pi2), AF.Gelu_apprx_tanh, [f"ps{pi2}"], ["Gb"])
        b.dma("pool", d_g[fb], Gb, ["Gb"], [], "gGb")
    P.barrier()
    ar.release()
    ar.release()

    if stop == 'G':
        return done()
    ar.mark()
    wg = ar.alloc([8, D], BF16); wo1 = ar.alloc([8, D], BF16)
    wos = ar.alloc([4, D], F32)
    for wi, (wsrc, wdst) in enumerate(((od_glu_w, wg), (od_out_w, wo1))):
        for q2 in range(2):
            b.dma("sp", wos, wsrc[q2 * 512:(q2 + 1) * 512, :].rearrange("(kc p) f -> p kc f", p=128), [], ["wos"], "hwos")
            b.copy("dve" if q2 == 0 else "act", wdst[:, q2 * 4:(q2 + 1) * 4, :], wos, ["wos"], [f"hw{wi}{q2}"])
    glub = ar.alloc([8], F32)
    b.dma("sp", glub, od_glu_b, [], ["glub"], "h1")
    fg = ar.alloc([D], F32)
    b.dma("sp", fg, final_g.partition_broadcast(128), [], ["fg"], "h2")
    gt_ = [ar.alloc([8, 512], BF16) for _ in range(2)]
    zt_ = [ar.alloc([8, 512], BF16) for _ in range(2)]
    mT = [ar.alloc([8, 512], BF16) for _ in range(2)]
    sgm = [ar.alloc([512], F32) for _ in range(2)]
    gm = [ar.alloc([512], F32) for _ in range(2)]
    x1r = [ar.alloc([D], F32) for _ in range(2)]
    x2 = [ar.alloc([D], F32) for _ in range(2)]
    tmpo = [ar.alloc([512], F32) for _ in range(2)]
    junk = ar.alloc([D], BF16)
    ss2 = [ar.alloc([1], F32) for _ in range(2)]; sd2 = [ar.alloc([1], F32) for _ in range(2)]; rs2 = [ar.alloc([1], F32) for _ in range(2)]
    xf = [ar.alloc([D], F32) for _ in range(2)]
    for tt in range(8):
        i = tt % 2
        t0 = tt * 512
        b.dma("sp", gt_[i], d_g[:, :, t0:t0 + 512].rearrange("f p t -> p f t"), [], [f"gt{i}"], f"hgt{i}")
        b.dma("sp", zt_[i], d_sz[:, :, t0:t0 + 512].rearrange("f p t -> p f t"), [], [f"zt{i}"], f"hzt{i}")
        for fo in range(8):
            j = fo % 2
            for kc in range(8):
                b.mm(PS(j), wg[:, kc, fo * 128:(fo + 1) * 128], gt_[i][:, kc, :], kc == 0, kc == 7,
                     [f"gt{i}", "hw00", "hw01"], [f"ps{j}"])
            b.act(sgm[j], PS(j), AF.Sigmoid, [f"ps{j}", "glub"], [f"sgm{j}"], bias=glub[:, fo:fo + 1])
            b.tt("dve", gm[j], sgm[j], gt_[i][:, fo, :], ALU.mult, [f"sgm{j}", f"gt{i}"], [f"gm{j}"])
            b.tt("dve", mT[i][:, fo, :], gm[j], zt_[i][:, fo, :], ALU.mult, [f"gm{j}", f"zt{i}"], [f"mT{i}"])
        for sub in range(4):
            t = tt * 4 + sub
            k2 = t % 2
            b.dma("sp", x1r[k2], d_x1[CTX + t * 128:CTX + (t + 1) * 128, :], [], [f"x1r{k2}"], f"hx1{k2}")
            for half in range(2):
                for fo in range(8):
                    b.mm(PS(2 + half), mT[i][:, fo, sub * 128:(sub + 1) * 128], wo1[:, fo, half * 512:(half + 1) * 512],
                         fo == 0, fo == 7, [f"mT{i}", "hw10", "hw11"], [f"ps{2 + half}"])
                b.tt("dve", tmpo[half], PS(2 + half), gtrow[1][0][:, half * 512:(half + 1) * 512], ALU.mult,
                     [f"ps{2 + half}", "gtrow10"], [f"tmpo{half}"])
                b.tt("dve", x2[k2][:, half * 512:(half + 1) * 512], tmpo[half], x1r[k2][:, half * 512:(half + 1) * 512], ALU.add,
                     [f"tmpo{half}", f"x1r{k2}"], [f"x2{k2}"])
            b.act(junk, x2[k2], AF.Square, [f"x2{k2}"], ["junk", f"ss2{k2}"], accum_out=ss2[k2])
            b.act(sd2[k2], ss2[k2], AF.Sqrt, [f"ss2{k2}"], [f"sd2{k2}"], scale=1.0 / D, bias=eps_t)
            b.recip(rs2[k2], sd2[k2], [f"sd2{k2}"], [f"rs2{k2}"])
            b.stt(xf[k2], x2[k2], rs2[k2], fg, ALU.mult, ALU.mult, [f"x2{k2}", f"rs2{k2}", "fg"], [f"xf{k2}"])
            b.dma("pool", out_ap[t * 128:(t + 1) * 128, :], xf[k2], [f"xf{k2}"], [], f"hout{k2}")
    P.barrier()
    ar.release()
    P.emit()
    es.close()
    return nc, P, ar


def _fm(v):
    return np.ascontiguousarray(v.reshape(8, 128).T)


def _s5_state(a):
    a = a.reshape(2, 32, 2, 64)
    return np.ascontiguousarray(a.transpose(2, 3, 0, 1).reshape(128, 64))


def _consts():
    c = {}
    n = np.arange(SEQ)
    row = (n // 64).astype(np.float32); col = (n % 64).astype(np.float32)
    freqs = (np.float32(10000.0) ** (-np.arange(16, dtype=np.float32) / np.float32(16))).astype(np.float32)
    ang = np.concatenate([row[:, None] * freqs, col[:, None] * freqs], axis=-1).astype(np.float32)
    cos = np.cos(ang).astype(np.float32).T
    sin = np.sin(ang).astype(np.float32).T
    ccos = np.zeros((128, SEQ), np.float32); csin = np.zeros((128, SEQ), np.float32)
    for comp in range(2):
        for half in range(2):
            r0 = comp * 64 + half * 32
            ccos[r0:r0 + 32] = cos
            csin[r0:r0 + 32] = -sin if half == 0 else sin
    c["c_cos"] = ccos; c["c_sin"] = csin
    perm = np.zeros((128, 128), np.float32)
    for m in range(128):
        comp, r = divmod(m, 64)
        partner = comp * 64 + (r + 32) % 64
        perm[partner, m] = 1.0
    c["c_perm"] = perm.astype(ml_dtypes.bfloat16)
    c["c_ident"] = np.eye(128, dtype=np.float32).astype(ml_dtypes.bfloat16)
    zsel = np.zeros((128, 8, 240), np.float32)
    qsel = np.zeros((128, 8, 240), np.float32)
    for g in range(8):
        for h in range(16):
            zsel[g * 16 + h, g, 112 + h] = 1.0
            qsel[g * 16 + h, g, 112 + h] = 1.0
    c["c_zsel"] = zsel.astype(ml_dtypes.bfloat16); c["c_qsel"] = qsel.astype(ml_dtypes.bfloat16)
    s_idx = np.arange(128) // 16
    c["c_maskf"] = (s_idx[None, :] >= s_idx[:, None]).astype(np.float32)
    c["c_maskb"] = (s_idx[:, None] >= s_idx[None, :]).astype(np.float32)
    c["c_kvec"] = np.broadcast_to(np.arange(-8, 9, dtype=np.float32)[None, :], (128, 17)).copy()
    return c


def _prep(inputs):
    f = lambda a: np.ascontiguousarray(np.asarray(a, dtype=np.float32))
    shared = {}
    shared["ev_mod_w"] = f(inputs["ev_mod_w"][0])
    shared["ev_mod_bf"] = np.ascontiguousarray(f(inputs["ev_mod_b"][0]).reshape(24, 128).T)
    shared["ev_mod_bg"] = f(inputs["ev_mod_b"][0][2048:3072]).reshape(1, D)
    shared["ev_norm_g"] = _fm(f(inputs["ev_norm_g"][0]))
    shared["ev_in_w"] = f(inputs["ev_in_w"][0])
    shared["ev_lam"] = np.concatenate([f(inputs["ev_lam_q1"][0]), f(inputs["ev_lam_k1"][0]),
                                       f(inputs["ev_lam_q2"][0]), f(inputs["ev_lam_k2"][0])]).reshape(1, 256)
    shared["ev_subln"] = f(inputs["ev_subln_g"][0]).reshape(128, 1)
    shared["ev_dw_w"] = np.ascontiguousarray(f(inputs["ev_dw_w"][0]).reshape(31, 8, 128).transpose(2, 1, 0))
    shared["ev_dw_b"] = _fm(f(inputs["ev_dw_b"][0]))
    shared["ev_cln_g"] = _fm(f(inputs["ev_cln_g"][0]))
    shared["ev_cln_b"] = _fm(f(inputs["ev_cln_b"][0]))
    shared["ev_out_w"] = f(inputs["ev_out_w"][0])
    shared["od_mod_w"] = f(inputs["od_mod_w"][0])
    shared["od_mod_bf"] = np.ascontiguousarray(f(inputs["od_mod_b"][0]).reshape(24, 128).T)
    shared["od_mod_bg"] = f(inputs["od_mod_b"][0][2048:3072]).reshape(1, D)
    shared["od_norm_g"] = _fm(f(inputs["od_norm_g"][0]))
    shared["od_in_w"] = f(inputs["od_in_w"][0])
    shared["s5_lre"] = _s5_state(f(inputs["s5_lam_re"][0]))
    shared["s5_lim"] = _s5_state(f(inputs["s5_lam_im"][0]))
    shared["s5_ldt"] = _s5_state(np.broadcast_to(f(inputs["s5_log_dt"][0])[:, :, None], (2, 64, 64)))
    for nm, key in (("s5_bre", "s5_b_re"), ("s5_bim", "s5_b_im")):
        a = f(inputs[key][0]).reshape(2, 32, 2, 64, 16)
        shared[nm] = np.ascontiguousarray(a.transpose(2, 3, 0, 1, 4).reshape(128, 64, 16))
    for nm, key in (("s5_cre", "s5_c_re"), ("s5_cim", "s5_c_im")):
        a = f(inputs[key][0]).reshape(2, 32, 2, 16, 64)
        shared[nm] = np.ascontiguousarray(a.transpose(2, 4, 0, 1, 3).reshape(128, 64, 16))
    dd = f(inputs["s5_d"][0]).reshape(64, 16)
    shared["s5_d"] = np.ascontiguousarray(np.tile(dd.T, (8, 1)))
    shared["od_glu_w"] = f(inputs["od_glu_w"][0])
    shared["od_glu_b"] = _fm(f(inputs["od_glu_b"][0]))
    shared["od_out_w"] = f(inputs["od_out_w"][0])
    shared["final_g"] = f(inputs["final_g"]).reshape(1, D)
    shared.update(_consts())
    x = f(inputs["x"]); ctx = f(inputs["ctx"]); c = f(inputs["c"]); cctx = f(inputs["c_ctx"])
    maps = []
    for bi in range(x.shape[0]):
        m = dict(shared)
        m["x"] = x[bi]
        m["ctx"] = ctx[bi]
        m["cc"] = np.ascontiguousarray(np.stack([_fm(c[bi]), _fm(cctx)], axis=-1))
        maps.append(m)
    return maps


_CACHE = {}


def kernel(**inputs):
    maps = _prep(inputs)
    if "nc" not in _CACHE:
        _CACHE["nc"] = build(False)[0]
    nc = _CACHE["nc"]
    res = run_bass_kernel_spmd(nc, maps, core_ids=list(range(NB)))
    return np.stack([np.asarray(r["out"], dtype=np.float32) for r in res.results], axis=0)
```

```python
import math
import re
import numpy as np
import ml_dtypes
from contextlib import ExitStack
import concourse.bass as bass
import concourse.mybir as mybir
from concourse.bass_utils import run_bass_kernel_spmd

F32 = mybir.dt.float32
BF16 = mybir.dt.bfloat16
I32 = mybir.dt.int32
AF = mybir.ActivationFunctionType
ALU = mybir.AluOpType

ENGS = ("pe", "act", "dve", "pool", "sp")
EPOCH = 30000
DMA_EPOCH = 1800

D = 1024
SEQ = 4096
CTX = 256
NTOK = SEQ + CTX
EPS = 1e-6
NB = 8


class _Op:
    __slots__ = ("eng", "fn", "is_dma", "dkey", "dcount", "pos", "deps", "signal", "sigval", "barrier")

    def __init__(self, eng, fn, is_dma=False, dkey=None):
        self.eng = eng
        self.fn = fn
        self.is_dma = is_dma
        self.dkey = dkey
        self.dcount = 0
        self.pos = 0
        self.deps = []
        self.signal = False
        self.sigval = 0
        self.barrier = False


class Prog:
    def __init__(self, nc):
        self.nc = nc
        self.streams = {e: [] for e in ENGS}
        self.last_w = {}
        self.readers = {}
        self.seen = {e: {} for e in ENGS}
        self.dma_counts = {}
        self.all_dma_last = {}
        self._npos = {e: 0 for e in ENGS}
        self.last_comp = {}

    def _add_dep(self, op, ev):
        if ev is None or ev is op:
            return
        if ev.is_dma:
            k = ("dma", ev.dkey)
            v = ev.dcount
        else:
            k = ("eng", ev.eng)
            v = ev.pos
        s = self.seen[op.eng]
        if s.get(k, 0) >= v:
            return
        s[k] = v
        op.deps.append(ev)
        ev.signal = True

    def op(self, eng, fn, reads=(), writes=(), accum=False):
        psr = [r for r in reads if r.startswith("ps")]
        if psr:
            reads = [r for r in reads if not r.startswith("ps")]
            writes = list(writes) + [r for r in psr if r not in writes]
        o = _Op(eng, fn)
        self._npos[eng] += 1
        o.pos = self._npos[eng]
        for r in reads:
            self._add_dep(o, self.last_w.get(r))
        for w in writes:
            lw = self.last_w.get(w)
            if not (accum and lw is not None and (not lw.is_dma) and lw.eng == eng):
                self._add_dep(o, lw)
            for rd in self.readers.get(w, ()):
                self._add_dep(o, rd)
        for r in reads:
            self.readers.setdefault(r, []).append(o)
        for w in writes:
            self.last_w[w] = o
            self.readers[w] = []
        self.streams[eng].append(o)
        self.last_comp[eng] = o
        return o

    def dma(self, eng, out, in_, reads=(), writes=(), key=None, **kw):
        assert key is not None
        o = _Op(eng, None, is_dma=True, dkey=key)
        c = self.dma_counts.get(key, 0) + 1
        self.dma_counts[key] = c
        o.dcount = c
        o.fn = lambda e, out=out, in_=in_, kw=kw: e.dma_start(out=out, in_=in_, **kw)
        for r in reads:
            self._add_dep(o, self.last_w.get(r))
        for w in writes:
            self._add_dep(o, self.last_w.get(w))
            for rd in self.readers.get(w, ()):
                self._add_dep(o, rd)
        self._add_dep(o, self.all_dma_last.get(key))
        for r in reads:
            self.readers.setdefault(r, []).append(o)
        for w in writes:
            self.last_w[w] = o
            self.readers[w] = []
        self.all_dma_last[key] = o
        self.streams[eng].append(o)
        return o

    def barrier(self, name=None):
        if name is None:
            name = f"b{len(getattr(self, 'bnames', []))}"
        self.bnames = getattr(self, "bnames", []) + [name]
        lasts = list(self.last_comp.values()) + list(self.all_dma_last.values())
        for e in ENGS:
            o = _Op(e, None)
            o.barrier = True
            for ev in lasts:
                self._add_dep(o, ev)
            self.streams[e].append(o)
        self.last_w = {}
        self.readers = {}
        if getattr(self, "mark_tile", None) is not None:
            mt = self.mark_tile
            val = float(len(self.bnames))
            self.op("pool", lambda e, mt=mt, val=val: e.memset(mt, val), writes=["__mark"])

    def emit(self):
        nc = self.nc
        eng_sigs = {}
        for e in ENGS:
            n = 0
            for o in self.streams[e]:
                if o.is_dma or o.barrier:
                    continue
                if o.signal:
                    n += 1
                    o.sigval = n
            eng_sigs[e] = n
        with ExitStack() as es:
            esem = {}
            for e in ENGS:
                ne = eng_sigs[e] // EPOCH + 1
                esem[e] = [es.enter_context(nc.semaphore(f"s_{e}_{i}")) for i in range(ne)]
            dsem = {}
            for k, c in self.dma_counts.items():
                ne = (c - 1) // DMA_EPOCH + 1
                dsem[k] = [es.enter_context(nc.semaphore(f"d_{len(dsem)}_{i}")) for i in range(ne)]
            block = es.enter_context(nc.Block())

            def emit_stream(engname, engobj):
                for o in self.streams[engname]:
                    for ev in o.deps:
                        if ev.is_dma:
                            ep = (ev.dcount - 1) // DMA_EPOCH
                            engobj.wait_ge(dsem[ev.dkey][ep], 16 * (ev.dcount - ep * DMA_EPOCH))
                        else:
                            ep = (ev.sigval - 1) // EPOCH
                            engobj.wait_ge(esem[ev.eng][ep], ev.sigval - ep * EPOCH)
                    if o.barrier:
                        continue
                    inst = o.fn(engobj)
                    if o.is_dma:
                        ep = (o.dcount - 1) // DMA_EPOCH
                        inst.then_inc(dsem[o.dkey][ep], 16)
                    elif o.signal:
                        ep = (o.sigval - 1) // EPOCH
                        inst.then_inc(esem[o.eng][ep], 1)

            @block.tensor
            def _(t):
                emit_stream("pe", t)

            @block.scalar
            def _(t):
                emit_stream("act", t)

            @block.vector
            def _(t):
                emit_stream("dve", t)

            @block.gpsimd
            def _(t):
                emit_stream("pool", t)

            @block.sync
            def _(t):
                emit_stream("sp", t)

    def stats(self):
        return {e: len(self.streams[e]) for e in ENGS}


class Arena:
    def __init__(self, ap2d, nbytes):
        self.ap = ap2d
        self.nbytes = nbytes
        self.off = 0
        self.marks = []
        self.peak = 0

    def alloc(self, shape_free, dtype, parts=128):
        if isinstance(shape_free, int):
            shape_free = [shape_free]
        esz = 2 if dtype == BF16 else 4
        n = int(np.prod(shape_free))
        nb = n * esz
        self.off = (self.off + 31) // 32 * 32
        assert self.off + nb <= self.nbytes, f"SBUF arena overflow {self.off}+{nb}>{self.nbytes}"
        v = self.ap[0:parts, self.off // 2:(self.off + nb) // 2]
        if dtype != BF16:
            v = v.bitcast(dtype)
        self.off += nb
        self.peak = max(self.peak, self.off)
        if len(shape_free) == 2:
            v = v.rearrange("p (a b) -> p a b", a=shape_free[0])
        elif len(shape_free) == 3:
            v = v.rearrange("p (a b c) -> p a b c", a=shape_free[0], b=shape_free[1])
        elif len(shape_free) == 4:
            v = v.rearrange("p (a b c d) -> p a b c d", a=shape_free[0], b=shape_free[1], c=shape_free[2])
        return v

    def mark(self):
        self.marks.append(self.off)

    def release(self):
        self.off = self.marks.pop()


class B:
    def __init__(self, P):
        self.P = P
        self.uid = 0

    def key(self, s):
        self.uid += 1
        return f"{s}#{self.uid}"

    def mm(self, out, lhsT, rhs, start, stop, reads, writes):
        self.P.op("pe", lambda e: e.matmul(out, lhsT=lhsT, rhs=rhs, start=start, stop=stop),
                  reads=reads, writes=writes, accum=True)

    def tr(self, out, in_, ident, reads, writes):
        self.P.op("pe", lambda e: e.transpose(out, in_, ident), reads=reads, writes=writes, accum=True)

    def act(self, out, in_, func, reads, writes, bias=None, scale=None, accum_out=None, eng="act"):
        kw = {}
        if bias is not None:
            kw["bias"] = bias
        if scale is not None:
            kw["scale"] = scale
        if accum_out is not None:
            kw["accum_out"] = accum_out
        self.P.op("act", lambda e: e.activation(out=out, in_=in_, func=func, **kw), reads=reads, writes=writes)

    def tt(self, eng, out, in0, in1, op, reads, writes):
        self.P.op(eng, lambda e: e.tensor_tensor(out=out, in0=in0, in1=in1, op=op), reads=reads, writes=writes)

    def ts(self, eng, out, in0, s1, s2, op0, op1, reads, writes):
        if s2 is None:
            self.P.op(eng, lambda e: e.tensor_scalar(out=out, in0=in0, scalar1=s1, scalar2=None, op0=op0),
                      reads=reads, writes=writes)
        else:
            self.P.op(eng, lambda e: e.tensor_scalar(out=out, in0=in0, scalar1=s1, scalar2=s2, op0=op0, op1=op1),
                      reads=reads, writes=writes)

    def stt(self, out, in0, scalar, in1, op0, op1, reads, writes):
        self.P.op("dve", lambda e: e.scalar_tensor_tensor(out=out, in0=in0, scalar=scalar, in1=in1, op0=op0, op1=op1),
                  reads=reads, writes=writes)

    def copy(self, eng, out, in_, reads, writes):
        if eng == "act":
            self.P.op("act", lambda e: e.activation(out=out, in_=in_, func=AF.Copy), reads=reads, writes=writes)
        else:
            self.P.op(eng, lambda e: e.tensor_copy(out=out, in_=in_), reads=reads, writes=writes)

    def recip(self, out, in_, reads, writes):
        self.P.op("dve", lambda e: e.reciprocal(out=out, in_=in_), reads=reads, writes=writes)

    def memset(self, eng, out, val, writes):
        self.P.op(eng, lambda e: e.memset(out, val), writes=writes)

    def dma(self, eng, out, in_, reads, writes, key, **kw):
        if re.match(r"^(c|a|f|g|h|fk|dbg)\d+$", key):
            self.misc = getattr(self, "misc", 0) + 1
            key = f"misc{self.misc % 4}"
        self.P.dma(eng, out, in_, reads=reads, writes=writes, key=key, **kw)


def build(debug=False, stop=None):
    nc = bass.Bass("TRN2", target_bir_lowering=False)
    IN = {}

    def din(name, shape, dt=F32):
        IN[name] = nc.dram_tensor(name, list(shape), dt, kind="ExternalInput").ap()
        return IN[name]

    dbg_kind = "ExternalOutput" if debug else "Internal"

    def dscr(name, shape, dt):
        return nc.dram_tensor(name, list(shape), dt, kind=dbg_kind).ap()

    x_in = din("x", [SEQ, D])
    ctx_in = din("ctx", [CTX, D])
    cc_in = din("cc", [128, 8, 2])
    ev_mod_w = din("ev_mod_w", [D, 3 * D]); ev_mod_bf = din("ev_mod_bf", [128, 24]); ev_mod_bg = din("ev_mod_bg", [1, D])
    ev_norm_g = din("ev_norm_g", [128, 8])
    ev_in_w = din("ev_in_w", [D, 7168])
    ev_lam = din("ev_lam", [1, 256])
    ev_subln = din("ev_subln", [128, 1])
    ev_dw_w = din("ev_dw_w", [128, 8, 31]); ev_dw_b = din("ev_dw_b", [128, 8])
    ev_cln_g = din("ev_cln_g", [128, 8]); ev_cln_b = din("ev_cln_b", [128, 8])
    ev_out_w = din("ev_out_w", [2048, D])
    od_mod_w = din("od_mod_w", [D, 3 * D]); od_mod_bf = din("od_mod_bf", [128, 24]); od_mod_bg = din("od_mod_bg", [1, D])
    od_norm_g = din("od_norm_g", [128, 8])
    od_in_w = din("od_in_w", [D, 2048])
    s5_lre = din("s5_lre", [128, 64]); s5_lim = din("s5_lim", [128, 64]); s5_ldt = din("s5_ldt", [128, 64])
    s5_bre = din("s5_bre", [128, 64, 16]); s5_bim = din("s5_bim", [128, 64, 16])
    s5_cre = din("s5_cre", [128, 64, 16]); s5_cim = din("s5_cim", [128, 64, 16])
    s5_d = din("s5_d", [128, 64])
    od_glu_w = din("od_glu_w", [D, D]); od_glu_b = din("od_glu_b", [128, 8]); od_out_w = din("od_out_w", [D, D])
    final_g = din("final_g", [1, D])
    c_cos = din("c_cos", [128, SEQ]); c_sin = din("c_sin", [128, SEQ])
    c_perm = din("c_perm", [128, 128], BF16); c_ident = din("c_ident", [128, 128], BF16)
    c_zsel = din("c_zsel", [128, 8, 240], BF16); c_qsel = din("c_qsel", [128, 8, 240], BF16)
    c_maskf = din("c_maskf", [128, 128]); c_maskb = din("c_maskb", [128, 128])
    c_kvec = din("c_kvec", [128, 17])

    out_ap = nc.dram_tensor("out", [SEQ, D], F32, kind="ExternalOutput").ap()

    d_q = dscr("d_q", [8, 128, NTOK], BF16)
    d_k = dscr("d_k", [8, 128, NTOK], BF16)
    d_v = dscr("d_v", [NTOK, D], BF16)
    d_sga = dscr("d_sga", [8, 128, NTOK], BF16)
    d_h = dscr("d_h", [8, 128, NTOK], BF16)
    d_sgb = dscr("d_sgb", [8, 128, NTOK], BF16)
    d_y = dscr("d_y", [8, 128, NTOK], F32)
    d_mix = dscr("d_mix", [16, 128, NTOK], BF16)
    d_x1 = dscr("d_x1", [NTOK, D], F32)
    d_u = dscr("d_u", [8, 128, NTOK], BF16)
    d_sz = dscr("d_sz", [8, 128, SEQ], BF16)
    d_g = dscr("d_g", [8, 128, SEQ], BF16)
    d_kt = dscr("d_kt", [64, 128, 128], BF16)
    d_wbt = dscr("d_wbt", [32, 128, 2, 2, 128], BF16)
    d_wd = dscr("d_wd", [32, 128, 2, 2, 128], BF16)
    d_ush = dscr("d_ush", [64, 128, 544], BF16)

    es = ExitStack()
    ARENA_BYTES = 207 * 1024
    arena_t = es.enter_context(nc.sbuf_tensor("arena", [128, ARENA_BYTES // 2], BF16))
    ps_t = es.enter_context(nc.psum_tensor("ps", [128, 8, 512], F32))
    ar = Arena(arena_t[:, :], ARENA_BYTES)
    P = Prog(nc)
    b = B(P)
    if debug == "mark":
        P.mark_tile = es.enter_context(nc.sbuf_tensor("phasemark", [128, 1], F32))[:, :]

    def PS(i):
        return ps_t[:, i, :]

    dbg_n = [0]

    def dbg(name, ap, shape, dt=F32, keys=()):
        if not debug:
            return
        o = nc.dram_tensor("dbg_" + name, list(shape), dt, kind="ExternalOutput").ap()
        dbg_n[0] += 1
        b.dma("sp", o, ap, list(keys), [], f"dbg{dbg_n[0]}")

    def done():
        P.barrier()
        P.emit()
        es.close()
        return nc, P, ar

    def PSB(i):
        return ps_t[:, i, :].bitcast(BF16)

    ident = ar.alloc([128], BF16)
    perm = ar.alloc([128], BF16)
    ones_bf = ar.alloc([128], BF16)
    ones_f = ar.alloc([128], F32)
    b.dma("sp", ident, c_ident, [], ["ident"], "c1")
    b.dma("sp", perm, c_perm, [], ["perm"], "c2")
    b.memset("pool", ones_bf, 1.0, ["ones_bf"])
    b.memset("pool", ones_f, 1.0, ["ones_f"])
    modA = [ar.alloc([8, 2], F32) for _ in range(2)]
    modB = [ar.alloc([8, 2], F32) for _ in range(2)]
    gtrow = [[ar.alloc([D], F32) for _ in range(2)] for _ in range(2)]
    neglam = ar.alloc([1], F32)
    gsub = ar.alloc([1], F32)
    eps_t = ar.alloc([1], F32)
    b.memset("pool", eps_t, EPS, ["eps_t"])

    ar.mark()
    cc = ar.alloc([8, 2], F32)
    b.dma("sp", cc, cc_in, [], ["cc"], "a1")
    scb = ar.alloc([8, 2], F32)
    b.act(scb, cc, AF.Silu, ["cc"], ["scb"])
    screp = ar.alloc([8, 2, 128], F32)
    b.copy("dve", screp, scb.unsqueeze(3).to_broadcast([128, 8, 2, 128]), ["scb"], ["screp"])
    Wb = ar.alloc([8, 3 * D], F32)
    for L, (mw, mbf, mbg, ng) in enumerate([(ev_mod_w, ev_mod_bf, ev_mod_bg, ev_norm_g),
                                            (od_mod_w, od_mod_bf, od_mod_bg, od_norm_g)]):
        for kc in range(8):
            b.dma("sp", Wb[:, kc, :], mw[kc * 128:(kc + 1) * 128, :], [], [f"Wb{kc}"], f"mstg{kc}")
        wkeys = [f"Wb{kc}" for kc in range(8)]
        for fc in range(24):
            for kc in range(8):
                b.mm(PS(0)[:, fc * 2:fc * 2 + 2], Wb[:, kc, fc * 128:(fc + 1) * 128], scb[:, kc, :],
                     kc == 0, kc == 7, ["scb"] + wkeys, ["psA0"])
        mbf_t = ar.alloc([24], F32)
        b.dma("sp", mbf_t, mbf, [], ["mbf_t"], "a2")
        modT = ar.alloc([24, 2], F32)
        b.tt("dve", modT, PS(0)[:, 0:48].rearrange("p (f c) -> p f c", c=2),
             mbf_t.unsqueeze(2).to_broadcast([128, 24, 2]), ALU.add, ["psA0", "mbf_t"], ["modT"])
        ng_t = ar.alloc([8], F32)
        b.dma("sp", ng_t, ng, [], ["ng_t"], "a3")
        tmpA = ar.alloc([8, 2], F32)
        b.ts("dve", tmpA, modT[:, 8:16, :], 1.0, None, ALU.add, None, ["modT"], ["tmpA"])
        b.tt("dve", modA[L], tmpA, ng_t.unsqueeze(2).to_broadcast([128, 8, 2]), ALU.mult, ["tmpA", "ng_t"], [f"modA{L}"])
        b.copy("dve", modB[L], modT[:, 0:8, :], ["modT"], [f"modB{L}"])
        mbg_t = ar.alloc([D], F32)
        b.dma("sp", mbg_t, mbg.partition_broadcast(128), [], ["mbg_t"], "a4")
        for col in range(2):
            if L == 1 and col == 1:
                continue
            for half in range(2):
                for kc in range(8):
                    b.mm(PS(1 + half), screp[:, kc, col, :], Wb[:, kc, 2048 + half * 512:2048 + (half + 1) * 512],
                         kc == 0, kc == 7, ["screp"] + wkeys, [f"psA{1 + half}"])
                b.tt("dve", gtrow[L][col][:, half * 512:(half + 1) * 512], PS(1 + half), mbg_t[:, half * 512:(half + 1) * 512],
                     ALU.add, [f"psA{1 + half}", "mbg_t"], [f"gtrow{L}{col}"])
    lam_t = ar.alloc([256], F32)
    b.dma("sp", lam_t, ev_lam.partition_broadcast(128), [], ["lam_t"], "a5")
    lprod = ar.alloc([2, 64], F32)
    lam_v = lam_t.rearrange("p (a b) -> p a b", a=4)
    b.tt("dve", lprod[:, 0, :], lam_v[:, 0, :], lam_v[:, 1, :], ALU.mult, ["lam_t"], ["lprod0"])
    b.tt("dve", lprod[:, 1, :], lam_v[:, 2, :], lam_v[:, 3, :], ALU.mult, ["lam_t"], ["lprod1"])
    lsum = ar.alloc([2], F32)
    P.op("dve", lambda e: e.reduce_sum(out=lsum, in_=lprod, axis=mybir.AxisListType.X), reads=["lprod0", "lprod1"], writes=["lsum"])
    lexp = ar.alloc([2], F32)
    b.act(lexp, lsum, AF.Exp, ["lsum"], ["lexp"])
    LAM_INIT0 = 0.8 - 0.6 * math.exp(-0.3 * 0)
    b.tt("dve", neglam, lexp[:, 1:2], lexp[:, 0:1], ALU.subtract, ["lexp"], ["neglam_a"])
    b.ts("dve", neglam, neglam, -LAM_INIT0, None, ALU.add, None, ["neglam_a"], ["neglam"])
    sub_t = ar.alloc([1], F32)
    b.dma("sp", sub_t, ev_subln, [], ["sub_t"], "a6")
    b.ts("dve", gsub, sub_t, 1.0 - LAM_INIT0, None, ALU.mult, None, ["sub_t"], ["gsub"])
    P.barrier()
    for L in range(2):
        dbg(f"modA{L}", modA[L], [128, 8, 2]); dbg(f"modB{L}", modB[L], [128, 8, 2])
        for col in range(2):
            dbg(f"gtrow{L}{col}", gtrow[L][col], [128, D])
    dbg("neglam", neglam, [128, 1]); dbg("gsub", gsub, [128, 1])
    P.barrier()
    ar.release()

    if stop == 'A':
        return done()
    def prologue(L, src_ctx, src_x, hT):
        ar.mark()
        xs = [ar.alloc([D], F32) for _ in range(4)]
        xnb = [ar.alloc([D], BF16) for _ in range(4)]
        junk = ar.alloc([D], BF16)
        ss = [ar.alloc([1], F32) for _ in range(4)]
        sd = [ar.alloc([1], F32) for _ in range(4)]
        rs = [ar.alloc([1], F32) for _ in range(4)]
        evt = [ar.alloc([8, 128], F32) for _ in range(2)]
        hkeys_ = [f"hT{fb}" for fb in range(8)]
        def stage_a(t):
            i = t % 4
            src = src_ctx[t * 128:(t + 1) * 128, :] if t < 2 else src_x[(t - 2) * 128:(t - 1) * 128, :]
            b.dma("sp", xs[i], src, [], [f"xs{i}"], f"pxs{i}")
            b.act(junk, xs[i], AF.Square, [f"xs{i}"], ["junk", f"ss{i}"], accum_out=ss[i])
            b.act(sd[i], ss[i], AF.Sqrt, [f"ss{i}"], [f"sd{i}"], scale=1.0 / D, bias=eps_t)
            b.recip(rs[i], sd[i], [f"sd{i}"], [f"rs{i}"])

        def stage_b(t):
            i = t % 4
            col = 1 if t < 2 else 0
            b.act(xnb[i], xs[i], AF.Copy, [f"xs{i}", f"rs{i}"], [f"xnb{i}"], scale=rs[i])
            pb = PSB(i)
            for fb in range(8):
                b.tr(pb[:, fb * 128:(fb + 1) * 128], xnb[i][:, fb * 128:(fb + 1) * 128], ident,
                     [f"xnb{i}", "ident"], [f"ps{i}"])
            pv = pb.rearrange("p (f t) -> p f t", f=8)
            b.tt("dve", evt[i % 2], pv, modA[L][:, :, col].unsqueeze(2).to_broadcast([128, 8, 128]), ALU.mult,
                 [f"ps{i}", f"modA{L}"], [f"evt{i % 2}"])
            b.tt("dve", hT[:, :, t * 128:(t + 1) * 128], evt[i % 2], modB[L][:, :, col].unsqueeze(2).to_broadcast([128, 8, 128]),
                 ALU.add, [f"evt{i % 2}", f"modB{L}"], hkeys_)

        stage_a(0); stage_a(1)
        for t in range(34):
            if t + 2 < 34:
                stage_a(t + 2)
            stage_b(t)
        P.barrier()
        ar.release()

    hkeys = [f"hT{fb}" for fb in range(8)]

    TT = [(0, 256)] + [(256 + i * 512, 512) for i in range(8)]

    ar.mark()
    hT = ar.alloc([8, NTOK], BF16)
    prologue(0, ctx_in, x_in, hT)
    if stop == 'B':
        P.barrier()
        dbg("hT", hT, [128, 8, NTOK], BF16)

    if stop == 'B':
        return done()
    ar.mark()
    cos_t = ar.alloc([SEQ], F32)
    sin_t = ar.alloc([SEQ], F32)
    b.dma("sp", cos_t, c_cos, [], ["cos_t"], "c3")
    b.dma("sp", sin_t, c_sin, [], ["sin_t"], "c4")
    wst = [ar.alloc([8, 128], F32) for _ in range(2)]
    wbf = [ar.alloc([8, 128], BF16) for _ in range(4)]
    ostg = [ar.alloc([NTOK], BF16) for _ in range(2)]
    qb = [ar.alloc([512], BF16) for _ in range(2)]
    r1 = [ar.alloc([512], F32) for _ in range(2)]
    r2 = [ar.alloc([512], F32) for _ in range(2)]
    sg = [ar.alloc([512], F32) for _ in range(2)]
    cnt = {"w": 0, "o": 0, "ps": 0, "t": 0}

    def proj_fm(wslot_keys, wtile, ps_idx, t0, n):
        for kc in range(8):
            b.mm(PS(ps_idx)[:, 0:n], wtile[:, kc, :], hT[:, kc, t0:t0 + n], kc == 0, kc == 7,
                 hkeys + wslot_keys, [f"ps{ps_idx}"])

    def load_w(c0):
        i = cnt["w"]; cnt["w"] += 1
        s2, s4 = i % 2, i % 4
        b.dma("sp", wst[s2], ev_in_w[:, c0:c0 + 128].rearrange("(kc p) f -> p kc f", p=128), [], [f"wstc{s2}"], f"wstc{s2}")
        b.copy("dve" if i % 2 == 0 else "act", wbf[s4], wst[s2], [f"wstc{s2}"], [f"wbfc{s4}"])
        return wbf[s4], [f"wbfc{s4}"]

    PROJ_BANKS = [0, 1, 4, 5]
    for which, dst in ((0, d_q), (1, d_k)):
        for h in range(8):
            wt, wk = load_w(which * 1024 + h * 128)
            oi = cnt["o"] % 2; cnt["o"] += 1
            og = ostg[oi]
            pend = None
            for (t0, n) in TT:
                pi = PROJ_BANKS[cnt["ps"] % 4]; cnt["ps"] += 1
                proj_fm(wk, wt, pi, t0, n)
                if pend is not None:
                    pend(); pend = None
                if t0 == 0:
                    b.copy("act", og[:, 0:n], PS(pi)[:, 0:n], [f"ps{pi}"], [f"ostg{oi}"])
                else:
                    ti = cnt["t"] % 2; cnt["t"] += 1
                    b.copy("act", qb[ti], PS(pi), [f"ps{pi}"], [f"qb{ti}"])

                    def rope(ti=ti, t0=t0, og=og, oi=oi):
                        b.mm(PS(2 + ti), perm, qb[ti], True, True, ["perm", f"qb{ti}"], [f"ps{2 + ti}"])
                        s0 = t0 - CTX
                        b.tt("dve", r1[ti], qb[ti], cos_t[:, s0:s0 + 512], ALU.mult, [f"qb{ti}", "cos_t"], [f"r1{ti}"])
                        b.tt("dve", r2[ti], PS(2 + ti), sin_t[:, s0:s0 + 512], ALU.mult, [f"ps{2 + ti}", "sin_t"], [f"r2{ti}"])
                        b.tt("dve", og[:, t0:t0 + 512], r1[ti], r2[ti], ALU.add, [f"r1{ti}", f"r2{ti}"], [f"ostg{oi}"])
                    pend = rope
            if pend is not None:
                pend(); pend = None
            b.dma("pool", dst[h], og, [f"ostg{oi}"], [], f"ost{oi}")
    for base, dst in ((3072, d_sga), (6144, d_sgb)):
        for fb in range(8):
            wt, wk = load_w(base + fb * 128)
            oi = cnt["o"] % 2; cnt["o"] += 1
            og = ostg[oi]
            for (t0, n) in TT:
                pi = PROJ_BANKS[cnt["ps"] % 4]; cnt["ps"] += 1
                proj_fm(wk, wt, pi, t0, n)
                b.act(og[:, t0:t0 + n], PS(pi)[:, 0:n], AF.Silu, [f"ps{pi}"], [f"ostg{oi}"])
            b.dma("pool", dst[fb], og, [f"ostg{oi}"], [], f"ost{oi}")
    gc_ = 0
    for fb in range(8):
        wa, wka = load_w(4096 + fb * 128)
        wb_, wkb = load_w(5120 + fb * 128)
        oi = cnt["o"] % 2; cnt["o"] += 1
        og = ostg[oi]
        for (t0, n) in TT:
            ba = 4 + 2 * (gc_ % 2); bb_ = 5 + 2 * (gc_ % 2); gc_ += 1
            proj_fm(wka, wa, ba, t0, n)
            proj_fm(wkb, wb_, bb_, t0, n)
            ti = cnt["t"] % 2; cnt["t"] += 1
            b.act(sg[ti][:, 0:n], PS(bb_)[:, 0:n], AF.Sigmoid, [f"ps{bb_}"], [f"sg{ti}"])
            b.tt("dve", og[:, t0:t0 + n], PS(ba)[:, 0:n], sg[ti][:, 0:n], ALU.mult, [f"ps{ba}", f"sg{ti}"], [f"ostg{oi}"])
        b.dma("pool", d_h[fb], og, [f"ostg{oi}"], [], f"ost{oi}")
    P.barrier()
    ar.release()
    ar.mark()
    wvs = ar.alloc([8, 512], F32)
    wv = ar.alloc([8, D], BF16)
    for half in range(2):
        b.dma("sp", wvs, ev_in_w[:, 2048 + half * 512:2048 + (half + 1) * 512].rearrange("(kc p) f -> p kc f", p=128),
              [], ["wvs"], "wvs")
        b.copy("dve", wv[:, :, half * 512:(half + 1) * 512], wvs, ["wvs"], [f"wv{half}"])
    vst = [ar.alloc([D], BF16) for _ in range(2)]
    for t in range(34):
        i = t % 2
        for half in range(2):
            for kc in range(8):
                b.mm(PS(6 + half), hT[:, kc, t * 128:(t + 1) * 128], wv[:, kc, half * 512:(half + 1) * 512],
                     kc == 0, kc == 7, hkeys + [f"wv{half}"], [f"ps{6 + half}"])
            b.copy("act" if half == 0 else "dve", vst[i][:, half * 512:(half + 1) * 512], PS(6 + half),
                   [f"ps{6 + half}"], [f"vst{i}"])
        b.dma("pool", d_v[t * 128:(t + 1) * 128, :], vst[i], [f"vst{i}"], [], f"vst{i}")
    P.barrier()
    ar.release()
    ar.release()
    if stop == 'C1':
        return done()
    ar.mark()
    qz = [[ar.alloc([NTOK], BF16) for _ in range(2)] for _ in range(2)]
    for i_ in range(2):
        b.memset("pool", qz[i_][0][64:128, :], 0.0, [f"qz{i_}0"])
        b.memset("pool", qz[i_][1][0:64, :], 0.0, [f"qz{i_}1"])
    kT = [ar.alloc([NTOK], BF16) for _ in range(2)]
    vh = [ar.alloc([34, 128], BF16) for _ in range(2)]
    sga = [ar.alloc([NTOK], BF16) for _ in range(2)]
    mixo = [ar.alloc([NTOK], BF16) for _ in range(2)]
    NE = 8
    E = [ar.alloc([512], BF16) for _ in range(NE)]
    EP = [ar.alloc([512], BF16) for _ in range(6)]
    EQ = [ar.alloc([512], BF16) for _ in range(4)]
    R = [ar.alloc([512], F32) for _ in range(2)]
    Osb = [ar.alloc([512], F32) for _ in range(2)]
    o1 = ar.alloc([512], F32); oo = ar.alloc([512], F32)
    sq = ar.alloc([512], BF16); sdn = ar.alloc([512], F32); rsn = ar.alloc([512], F32); an = ar.alloc([512], F32)
    gcnt = {"s": 0, "e": 0, "p": 0, "q": 0}
    for h in range(8):
        hi_ = h % 2
        b.dma("sp", qz[hi_][0][0:64, :], d_q[h][0:64, :], [], [f"qz{hi_}0"], f"aq{hi_}0")
        b.dma("sp", qz[hi_][1][64:128, :], d_q[h][64:128, :], [], [f"qz{hi_}1"], f"aq{hi_}1")
        b.dma("sp", kT[hi_], d_k[h], [], [f"kT{hi_}"], f"ak{hi_}")
        b.dma("sp", vh[hi_], d_v[:, h * 128:(h + 1) * 128].rearrange("(kt p) f -> p kt f", p=128), [], [f"vh{hi_}"], f"av{hi_}")
        b.dma("sp", sga[hi_], d_sga[h], [], [f"sga{hi_}"], f"ag{hi_}")
        for (t0, n) in TT:
            kts = list(range(2)) if t0 == 0 else list(range(34))
            items = [(kt, c) for kt in kts for c in range(2)]
            sb = {}
            eb = {}
            zq = []
            pend = [None, None]
            zstarted = [False, False]

            def emit_S(idx):
                kt, c = items[idx]
                si = gcnt["s"] % 4; gcnt["s"] += 1
                sb[idx] = si
                b.mm(PS(si)[:, 0:n], kT[hi_][:, kt * 128:(kt + 1) * 128],
                     qz[hi_][c][:, t0:t0 + n], True, True, [f"qz{hi_}{c}", f"kT{hi_}"], [f"ps{si}"])

            LOOK = 3
            for idx in range(min(LOOK, len(items))):
                emit_S(idx)
            for idx, (kt, c) in enumerate(items):
                if idx + LOOK < len(items):
                    emit_S(idx + LOOK)
                si = sb[idx]
                ei = gcnt["e"] % NE; gcnt["e"] += 1
                eb[idx] = ei
                b.act(E[ei][:, 0:n], PS(si)[:, 0:n], AF.Exp, [f"ps{si}"], [f"E{ei}"], scale=0.125)
                first = kt == kts[0]; last = kt == kts[-1]
                b.mm(PS(4 + c)[:, 0:n], vh[hi_][:, kt, :], E[ei][:, 0:n], first, last, [f"vh{hi_}", f"E{ei}"], [f"ps{4 + c}"])
                if kt % 2 == 1:
                    e_prev = eb[idx - 2]
                    pi_ = gcnt["p"] % 6; gcnt["p"] += 1
                    b.tt("dve", EP[pi_][:, 0:n], E[e_prev][:, 0:n], E[ei][:, 0:n], ALU.add, [f"E{e_prev}", f"E{ei}"], [f"EP{pi_}"])
                    if pend[c] is None and not last:
                        pend[c] = pi_
                    else:
                        if pend[c] is not None:
                            qi_ = gcnt["q"] % 4; gcnt["q"] += 1
                            b.tt("dve", EQ[qi_][:, 0:n], EP[pend[c]][:, 0:n], EP[pi_][:, 0:n], ALU.add,
                                 [f"EP{pend[c]}", f"EP{pi_}"], [f"EQ{qi_}"])
                            src, skey = EQ[qi_], f"EQ{qi_}"
                            pend[c] = None
                        else:
                            src, skey = EP[pi_], f"EP{pi_}"
                        zq.append((idx, c, src, skey, not zstarted[c], last))
                        zstarted[c] = True
                while zq and (zq[0][0] + 3 <= idx or idx == len(items) - 1):
                    _, c_, src_, sk_, f_, l_ = zq.pop(0)
                    b.mm(PS(6 + c_)[:, 0:n], ones_bf, src_[:, 0:n], f_, l_, ["ones_bf", sk_], [f"ps{6 + c_}"])
            b.copy("act", Osb[0][:, 0:n], PS(4)[:, 0:n], ["ps4"], ["Osb0"])
            b.copy("dve", Osb[1][:, 0:n], PS(5)[:, 0:n], ["ps5"], ["Osb1"])
            b.recip(R[0][:, 0:n], PS(6)[:, 0:n], ["ps6"], ["R0"])
            b.recip(R[1][:, 0:n], PS(7)[:, 0:n], ["ps7"], ["R1"])
            b.tt("dve", Osb[0][:, 0:n], Osb[0][:, 0:n], R[0][:, 0:n], ALU.mult, ["Osb0", "R0"], ["Osb0"])
            b.tt("dve", o1[:, 0:n], Osb[1][:, 0:n], R[1][:, 0:n], ALU.mult, ["Osb1", "R1"], ["o1"])
            b.stt(oo[:, 0:n], o1[:, 0:n], neglam, Osb[0][:, 0:n], ALU.mult, ALU.add, ["Osb0", "o1", "neglam"], ["oo"])
            b.tt("dve", sq[:, 0:n], oo[:, 0:n], oo[:, 0:n], ALU.mult, ["oo"], ["sq"])
            sN = gcnt["s"] % 4; gcnt["s"] += 1
            b.mm(PS(sN)[:, 0:n], ones_bf, sq[:, 0:n], True, True, ["ones_bf", "sq"], [f"ps{sN}"])
            b.ts("dve", sdn[:, 0:n], PS(sN)[:, 0:n], 1.0 / 128, EPS, ALU.mult, ALU.add, [f"ps{sN}"], ["sdn"])
            b.act(rsn[:, 0:n], sdn[:, 0:n], AF.Ln, ["sdn"], ["rsn"])
            b.act(rsn[:, 0:n], rsn[:, 0:n], AF.Exp, ["rsn"], ["rsn"], scale=-0.5)
            b.tt("dve", an[:, 0:n], oo[:, 0:n], rsn[:, 0:n], ALU.mult, ["oo", "rsn"], ["an"])
            b.stt(mixo[hi_][:, t0:t0 + n], an[:, 0:n], gsub, sga[hi_][:, t0:t0 + n], ALU.mult, ALU.mult,
                  ["an", "gsub", f"sga{hi_}"], [f"mixo{hi_}"])
        b.dma("pool", d_mix[h], mixo[hi_], [f"mixo{hi_}"], [], f"amix{hi_}")
    P.barrier()
    ar.release()
    if stop == 'C2':
        return done()
    ar.mark()
    dww = ar.alloc([8, 31], F32); dwb = ar.alloc([8], F32)
    b.dma("sp", dww, ev_dw_w, [], ["dww"], "c5")
    b.dma("sp", dwb, ev_dw_b, [], ["dwb"], "c6")
    identf2 = ar.alloc([128], F32)
    b.copy("dve", identf2, ident, ["ident"], ["identf2"])
    dg = [ar.alloc([31, 128], BF16) for _ in range(2)]
    PADW = 15 + CTX + 30 + SEQ + 15
    XOFF = 15 + CTX + 30
    hp = [ar.alloc([PADW], BF16) for _ in range(2)]
    for i_ in range(2):
        b.memset("pool", hp[i_], 0.0, [f"hp{i_}"])
    yc = [ar.alloc([NTOK], F32) for _ in range(2)]
    ccnt = 0
    for fb in range(8):
        i = fb % 2
        b.dma("sp", hp[i][:, 15:15 + CTX], d_h[fb][:, 0:CTX], [], [f"hp{i}"], f"chc{i}")
        b.dma("sp", hp[i][:, XOFF:XOFF + SEQ], d_h[fb][:, CTX:NTOK], [], [f"hp{i}"], f"chx{i}")
        for k in range(31):
            b.ts("dve", dg[i][:, k, :], identf2, dww[:, fb, k:k + 1], None, ALU.mult, None, ["identf2", "dww"], [f"dg{i}"])
        for (t0, n) in TT:
            base = 15 if t0 == 0 else XOFF + (t0 - CTX)
            pi2 = ccnt % 2; ccnt += 1
            for k in range(31):
                b.mm(PS(pi2)[:, 0:n], dg[i][:, k, :], hp[i][:, base + k - 15:base + k - 15 + n], k == 0, k == 30,
                     [f"dg{i}", f"hp{i}"], [f"ps{pi2}"])
            if pi2 == 0:
                b.act(yc[i][:, t0:t0 + n], PS(pi2)[:, 0:n], AF.Identity, [f"ps{pi2}", "dwb"], [f"yc{i}"], bias=dwb[:, fb:fb + 1])
            else:
                b.ts("dve", yc[i][:, t0:t0 + n], PS(pi2)[:, 0:n], dwb[:, fb:fb + 1], None, ALU.add, None, [f"ps{pi2}", "dwb"], [f"yc{i}"])
        b.dma("pool", d_y[fb], yc[i], [f"yc{i}"], [], f"cyc{i}")
    P.barrier()
    ar.release()
    if stop == 'C3a':
        return done()
    ar.mark()
    clg = ar.alloc([8], F32); clb = ar.alloc([8], F32)
    b.dma("sp", clg, ev_cln_g, [], ["clg"], "c7")
    b.dma("sp", clb, ev_cln_b, [], ["clb"], "c8")
    yl = [ar.alloc([8, 512], F32) for _ in range(2)]
    gl = [ar.alloc([8, 512], BF16) for _ in range(2)]
    ysq = ar.alloc([8, 512], F32)
    mean = [ar.alloc([512], F32) for _ in range(2)]; msq = ar.alloc([512], F32); var = ar.alloc([512], F32)
    sdl = ar.alloc([512], F32); rsl = [ar.alloc([512], F32) for _ in range(2)]
    tAll = ar.alloc([8, 512], F32); tC = ar.alloc([8, 512], F32)
    mo = [ar.alloc([8, 512], BF16) for _ in range(2)]

    def ln_a(ti_):
        t0, n = TT[ti_]
        i = ti_ % 2
        b.dma("sp", yl[i][:, :, 0:n], d_y[:, :, t0:t0 + n].rearrange("f p t -> p f t"), [], [f"yl{i}"], f"lyl{i}")
        b.dma("sp", gl[i][:, :, 0:n], d_sgb[:, :, t0:t0 + n].rearrange("f p t -> p f t"), [], [f"gl{i}"], f"lgl{i}")
        b.act(ysq[:, :, 0:n], yl[i][:, :, 0:n], AF.Square, [f"yl{i}"], ["ysq"])
        for fb in range(8):
            b.mm(PS(0)[:, 0:n], ones_f, yl[i][:, fb, 0:n], fb == 0, fb == 7, ["ones_f", f"yl{i}"], ["ps0"])
        for fb in range(8):
            b.mm(PS(1)[:, 0:n], ones_f, ysq[:, fb, 0:n], fb == 0, fb == 7, ["ones_f", "ysq"], ["ps1"])
        b.ts("dve", mean[i][:, 0:n], PS(0)[:, 0:n], 1.0 / D, None, ALU.mult, None, ["ps0"], [f"mean{i}"])
        b.tt("dve", msq[:, 0:n], mean[i][:, 0:n], mean[i][:, 0:n], ALU.mult, [f"mean{i}"], ["msq"])
        b.stt(var[:, 0:n], PS(1)[:, 0:n], 1.0 / D, msq[:, 0:n], ALU.mult, ALU.subtract, ["ps1", "msq"], ["var"])
        b.act(sdl[:, 0:n], var[:, 0:n], AF.Sqrt, ["var"], ["sdl"], bias=eps_t)
        b.recip(rsl[i][:, 0:n], sdl[:, 0:n], ["sdl"], [f"rsl{i}"])

    def ln_b(ti_):
        t0, n = TT[ti_]
        i = ti_ % 2
        b.tt("dve", tAll[:, :, 0:n], yl[i][:, :, 0:n], mean[i][:, 0:n].unsqueeze(1).to_broadcast([128, 8, n]), ALU.subtract,
             [f"yl{i}", f"mean{i}"], ["tAll"])
        b.tt("dve", tAll[:, :, 0:n], tAll[:, :, 0:n], rsl[i][:, 0:n].unsqueeze(1).to_broadcast([128, 8, n]), ALU.mult,
             ["tAll", f"rsl{i}"], ["tAll"])
        for fb in range(8):
            b.act(tC[:, fb, 0:n], tAll[:, fb, 0:n], AF.Silu, ["tAll", "clg", "clb"], ["tC"],
                  scale=clg[:, fb:fb + 1], bias=clb[:, fb:fb + 1])
        b.tt("dve", mo[i][:, :, 0:n], tC[:, :, 0:n], gl[i][:, :, 0:n], ALU.mult, ["tC", f"gl{i}"], [f"mo{i}"])
        b.dma("pool", d_mix[8:16, :, t0:t0 + n].rearrange("f p t -> p f t"), mo[i][:, :, 0:n], [f"mo{i}"], [], f"lmo{i}")

    ln_a(0)
    for ti_ in range(len(TT)):
        if ti_ + 1 < len(TT):
            ln_a(ti_ + 1)
        ln_b(ti_)
    P.barrier()
    ar.release()
    if stop == 'C3b':
        return done()
    ar.mark()
    wo = ar.alloc([16, D], BF16)
    wos = ar.alloc([4, D], F32)
    for q4 in range(4):
        b.dma("sp", wos, ev_out_w[q4 * 512:(q4 + 1) * 512, :].rearrange("(kc p) f -> p kc f", p=128), [], ["wos"], "wos")
        b.copy("dve" if q4 % 2 == 0 else "act", wo[:, q4 * 4:(q4 + 1) * 4, :], wos, ["wos"], [f"wo{q4}"])
    wokeys = [f"wo{q4}" for q4 in range(4)]
    mt = [ar.alloc([16, 128], BF16) for _ in range(2)]
    xr = [ar.alloc([D], F32) for _ in range(2)]
    xo = [ar.alloc([D], F32) for _ in range(2)]
    tmpo = [ar.alloc([512], F32) for _ in range(2)]
    for t in range(34):
        i = t % 2
        col = 1 if t < 2 else 0
        src = ctx_in[t * 128:(t + 1) * 128, :] if t < 2 else x_in[(t - 2) * 128:(t - 1) * 128, :]
        b.dma("sp", mt[i], d_mix[:, :, t * 128:(t + 1) * 128].rearrange("c p t -> p c t"), [], [f"mt{i}"], f"omt{i}")
        b.dma("sp", xr[i], src, [], [f"xr{i}"], f"oxr{i}")
        for half in range(2):
            for ck in range(16):
                b.mm(PS(half), mt[i][:, ck, :], wo[:, ck, half * 512:(half + 1) * 512], ck == 0, ck == 15,
                     [f"mt{i}"] + wokeys, [f"ps{half}"])
            b.tt("dve", tmpo[half], PS(half), gtrow[0][col][:, half * 512:(half + 1) * 512], ALU.mult,
                 [f"ps{half}", f"gtrow0{col}"], [f"tmpo{half}"])
            b.tt("dve", xo[i][:, half * 512:(half + 1) * 512], tmpo[half], xr[i][:, half * 512:(half + 1) * 512], ALU.add,
                 [f"tmpo{half}", f"xr{i}"], [f"xo{i}"])
        b.dma("pool", d_x1[t * 128:(t + 1) * 128, :], xo[i], [f"xo{i}"], [], f"oxo{i}")
    P.barrier()
    ar.release()

    if stop == 'C4':
        return done()
    ar.mark()
    hT = ar.alloc([8, NTOK], BF16)
    prologue(1, d_x1[0:CTX, :], d_x1[CTX:NTOK, :], hT)
    ar.mark()
    wst = [ar.alloc([8, 128], F32) for _ in range(2)]
    wbf = [ar.alloc([8, 128], BF16) for _ in range(4)]
    ostg = [ar.alloc([NTOK], BF16) for _ in range(2)]
    cnt = {"w": 0, "o": 0, "ps": 0}

    def load_w1(c0):
        i = cnt["w"]; cnt["w"] += 1
        s2, s4 = i % 2, i % 4
        b.dma("sp", wst[s2], od_in_w[:, c0:c0 + 128].rearrange("(kc p) f -> p kc f", p=128), [], [f"wstc{s2}"], f"wste{s2}")
        b.copy("dve" if i % 2 == 0 else "act", wbf[s4], wst[s2], [f"wstc{s2}"], [f"wbfc{s4}"])
        return wbf[s4], [f"wbfc{s4}"]

    for fb in range(16):
        wt, wk = load_w1(fb * 128)
        oi = cnt["o"] % 2; cnt["o"] += 1
        og = ostg[oi]
        for (t0, n) in TT:
            if fb >= 8 and t0 == 0:
                continue
            pi = cnt["ps"] % 2; cnt["ps"] += 1
            for kc in range(8):
                b.mm(PS(pi)[:, 0:n], wt[:, kc, :], hT[:, kc, t0:t0 + n], kc == 0, kc == 7, hkeys + wk, [f"ps{pi}"])
            if fb < 8:
                b.copy("act" if pi == 0 else "dve", og[:, t0:t0 + n], PS(pi)[:, 0:n], [f"ps{pi}"], [f"ostg{oi}"])
            else:
                b.act(og[:, t0:t0 + n], PS(pi)[:, 0:n], AF.Silu, [f"ps{pi}"], [f"ostg{oi}"])
        if fb < 8:
            b.dma("pool", d_u[fb], og, [f"ostg{oi}"], [], f"eo{oi}")
        else:
            b.dma("pool", d_sz[fb - 8], og[:, CTX:NTOK], [f"ostg{oi}"], [], f"eo{oi}")
    P.barrier()
    ar.release()
    ar.release()

    if stop == 'E':
        return done()
    A8r = ar.alloc([2, 32], F32)
    A8i = ar.alloc([2, 32], F32)
    A8in = ar.alloc([2, 32], F32)
    ar.mark()
    Kt = ar.alloc([64, 128], BF16)
    Wbt = ar.alloc([2, 32, 2, 128], BF16)
    Wd = ar.alloc([2, 32, 2, 128], BF16)
    ar.mark()
    lre = ar.alloc([64], F32); lim = ar.alloc([64], F32); ldt = ar.alloc([64], F32)
    b.dma("sp", lre, s5_lre, [], ["lre"], "f1")
    b.dma("sp", lim, s5_lim, [], ["lim"], "f2")
    b.dma("sp", ldt, s5_ldt, [], ["ldt"], "f3")
    kvec = ar.alloc([17], F32)
    b.dma("sp", kvec, c_kvec, [], ["kvec"], "f4")
    maskf = ar.alloc([128], F32); maskb = ar.alloc([128], F32)
    b.dma("sp", maskf, c_maskf, [], ["maskf"], "f5")
    b.dma("sp", maskb, c_maskb, [], ["maskb"], "f6")
    dts = ar.alloc([64], F32)
    b.act(dts, ldt, AF.Exp, ["ldt"], ["dts"])
    xr_ = ar.alloc([64], F32); th = ar.alloc([64], F32)
    b.tt("dve", xr_, lre, dts, ALU.mult, ["lre", "dts"], ["xr_"])
    b.tt("dve", th, lim, dts, ALU.mult, ["lim", "dts"], ["th"])
    PWr = ar.alloc([17, 64], F32); PWi = ar.alloc([17, 64], F32)
    ar.mark()
    ARG = ar.alloc([17, 2, 64], F32)
    b.tt("dve", ARG[:, :, 0, :], th.unsqueeze(1).to_broadcast([128, 17, 64]), kvec.unsqueeze(2).to_broadcast([128, 17, 64]),
         ALU.mult, ["th", "kvec"], ["ARG0"])
    b.ts("dve", ARG[:, :, 1, :], ARG[:, :, 0, :], math.pi / 2, None, ALU.add, None, ["ARG0"], ["ARG1"])
    ARGf = ARG.rearrange("p a b c -> p (a b c)")
    T1 = ar.alloc([17 * 2 * 64], F32)
    ki = ar.alloc([17 * 2 * 64], I32)
    b.ts("dve", T1, ARGf, 1.0 / (2 * math.pi), None, ALU.mult, None, ["ARG0", "ARG1"], ["T1"])
    b.copy("dve", ki, T1, ["T1"], ["ki"])
    b.copy("dve", T1, ki, ["ki"], ["T1"])
    b.stt(T1, T1, -2 * math.pi, ARGf, ALU.mult, ALU.add, ["T1", "ARG0", "ARG1"], ["T1"])
    b.ts("dve", T1, T1, 3.1415925, -3.1415925, ALU.min, ALU.max, ["T1"], ["T1"])
    SC = ar.alloc([17, 2, 64], F32)
    b.act(SC.rearrange("p a b c -> p (a b c)"), T1, AF.Sin, ["T1"], ["SC"])
    MG = ar.alloc([17, 64], F32)
    b.tt("dve", MG, xr_.unsqueeze(1).to_broadcast([128, 17, 64]), kvec.unsqueeze(2).to_broadcast([128, 17, 64]),
         ALU.mult, ["xr_", "kvec"], ["MGa"])
    b.act(MG, MG, AF.Exp, ["MGa"], ["MG"])
    b.tt("dve", PWr, MG, SC[:, :, 1, :], ALU.mult, ["MG", "SC"], ["PWr"])
    b.tt("dve", PWi, MG, SC[:, :, 0, :], ALU.mult, ["MG", "SC"], ["PWi"])
    P.barrier()
    ar.release()
    b.copy("dve", A8r.rearrange("p a b -> p (a b)"), PWr[:, 16, :], ["PWr"], ["A8r"])
    b.copy("dve", A8i.rearrange("p a b -> p (a b)"), PWi[:, 16, :], ["PWi"], ["A8i"])
    b.ts("dve", A8in.rearrange("p a b -> p (a b)"), PWi[:, 16, :], -1.0, None, ALU.mult, None, ["PWi"], ["A8in"])
    ar1 = ar.alloc([64], F32); den = ar.alloc([64], F32); t_a = ar.alloc([64], F32); t_b = ar.alloc([64], F32)
    fr = ar.alloc([64], F32); fi = ar.alloc([64], F32); rden = ar.alloc([64], F32)
    b.ts("dve", ar1, PWr[:, 9, :], -1.0, None, ALU.add, None, ["PWr"], ["ar1"])
    b.tt("dve", den, lre, lre, ALU.mult, ["lre"], ["den_a"])
    b.tt("dve", t_a, lim, lim, ALU.mult, ["lim"], ["t_a"])
    b.tt("dve", den, den, t_a, ALU.add, ["den_a", "t_a"], ["den"])
    b.recip(rden, den, ["den"], ["rden"])
    b.tt("dve", t_a, ar1, lre, ALU.mult, ["ar1", "lre", "den"], ["t_a"])
    b.tt("dve", t_b, PWi[:, 9, :], lim, ALU.mult, ["PWi", "lim"], ["t_b"])
    b.tt("dve", t_a, t_a, t_b, ALU.add, ["t_a", "t_b"], ["t_a"])
    b.tt("dve", fr, t_a, rden, ALU.mult, ["t_a", "rden"], ["fr"])
    b.tt("dve", t_a, PWi[:, 9, :], lre, ALU.mult, ["PWi", "lre", "fr"], ["t_a"])
    b.tt("dve", t_b, ar1, lim, ALU.mult, ["ar1", "lim", "t_a"], ["t_b"])
    b.tt("dve", t_a, t_a, t_b, ALU.subtract, ["t_a", "t_b"], ["t_a"])
    b.tt("dve", fi, t_a, rden, ALU.mult, ["t_a", "rden"], ["fi"])
    bre = ar.alloc([64, 16], F32); bim = ar.alloc([64, 16], F32)
    cre = ar.alloc([64, 16], F32); cim = ar.alloc([64, 16], F32)
    b.dma("sp", bre, s5_bre, [], ["bre"], "f7")
    b.dma("sp", bim, s5_bim, [], ["bim"], "f8")
    b.dma("sp", cre, s5_cre, [], ["cre"], "f9")
    b.dma("sp", cim, s5_cim, [], ["cim"], "f10")
    bbr = ar.alloc([64, 16], F32); bbi = ar.alloc([64, 16], F32)
    v1 = ar.alloc([16, 8, 16], F32); v2 = ar.alloc([16, 8, 16], F32)
    u1 = v1.rearrange("p a b c -> p (a b c)")[:, 0:1024].rearrange("p (a b) -> p a b", a=64)
    u2 = v2.rearrange("p a b c -> p (a b c)")[:, 0:1024].rearrange("p (a b) -> p a b", a=64)
    frb = fr.unsqueeze(2).to_broadcast([128, 64, 16]); fib = fi.unsqueeze(2).to_broadcast([128, 64, 16])
    b.tt("dve", u1, bre, frb, ALU.mult, ["bre", "fr"], ["v1"])
    b.tt("dve", u2, bim, fib, ALU.mult, ["bim", "fi"], ["v2"])
    b.tt("dve", bbr, u1, u2, ALU.subtract, ["v1", "v2"], ["bbr"])
    b.tt("dve", u1, bim, frb, ALU.mult, ["bim", "fr", "bbr"], ["v1"])
    b.tt("dve", u2, bre, fib, ALU.mult, ["bre", "fi", "bbr"], ["v2"])
    b.tt("dve", bbi, u1, u2, ALU.add, ["v1", "v2"], ["bbi"])
    dsh = ar.alloc([64], F32)
    b.dma("sp", dsh, s5_d, [], ["dsh"], "f11")
    identf = ar.alloc([128], F32)
    b.copy("dve", identf, ident, ["ident"], ["identf"])
    Xr = ar.alloc([32, 8, 16], BF16); Xi = ar.alloc([32, 8, 16], BF16)

    def cexp(pw_slice, dr, fac_r, fac_i, fkeys, out_r, out_i, kr, ki_, neg_im):
        for hf in range(2):
            q0 = dr * 32 + hf * 16
            pr = PWr[:, pw_slice, q0:q0 + 16].rearrange("p s q -> p q s").unsqueeze(3).to_broadcast([128, 16, 8, 16])
            pi_ = PWi[:, pw_slice, q0:q0 + 16].rearrange("p s q -> p q s").unsqueeze(3).to_broadcast([128, 16, 8, 16])
            fr_ = fac_r[:, q0:q0 + 16, :].unsqueeze(2).to_broadcast([128, 16, 8, 16])
            fi_ = fac_i[:, q0:q0 + 16, :].unsqueeze(2).to_broadcast([128, 16, 8, 16])
            o_r = out_r[:, hf * 16:(hf + 1) * 16]
            o_i = out_i[:, hf * 16:(hf + 1) * 16]
            b.tt("dve", v1, pr, fr_, ALU.mult, ["PWr"] + fkeys, ["v1"])
            b.tt("dve", v2, pi_, fi_, ALU.mult, ["PWi"] + fkeys, ["v2"])
            b.tt("dve", o_r, v1, v2, ALU.subtract, ["v1", "v2"], [kr])
            b.tt("dve", v1, pr, fi_, ALU.mult, ["PWr", kr] + fkeys, ["v1"])
            b.tt("dve", v2, pi_, fr_, ALU.mult, ["PWi", kr] + fkeys, ["v2"])
            if neg_im:
                b.stt(o_i, v1, -1.0, v2, ALU.mult, ALU.subtract, ["v1", "v2"], [ki_])
            else:
                b.tt("dve", o_i, v1, v2, ALU.add, ["v1", "v2"], [ki_])

    for dr in range(2):
        if dr == 0:
            sX = slice(15, 7, -1); sX2 = slice(7, None, -1); sY = slice(9, 17)
        else:
            sX = slice(8, 16); sX2 = slice(0, 8); sY = slice(16, 8, -1)
        Yr = Wd[:, dr, :, 0, :].rearrange("p q (t h) -> p q t h", t=8)
        Yin = Wd[:, dr, :, 1, :].rearrange("p q (t h) -> p q t h", t=8)
        cexp(sY, dr, cre, cim, ["cre", "cim"], Yr, Yin, f"Wdr{dr}", f"Wdi{dr}", True)
        cexp(sX, dr, bbr, bbi, ["bbr", "bbi"], Xr, Xi, "Xr", "Xi", False)
        for pr_ in range(32):
            for ri, Xs, kx in ((0, Xr, "Xr"), (1, Xi, "Xi")):
                pb = PSB(pr_ % 2)
                col = ri * 128
                b.tr(pb[:, col:col + 128], Xs[:, pr_, :, :].rearrange("p s h -> p (s h)"), ident, [kx, "ident"], [f"ps{pr_ % 2}"])
            b.copy("act" if pr_ % 2 == 0 else "dve", Wbt[:, dr, pr_, :, :].rearrange("p r m -> p (r m)"), PSB(pr_ % 2)[:, 0:256],
                   [f"ps{pr_ % 2}"], ["Wbt"])
        cexp(sX2, dr, bbr, bbi, ["bbr", "bbi"], Xr, Xi, "Xr", "Xi", False)
        tmpk = v1.rearrange("p a b c -> p (a b c)")[:, 0:128]
        for g in range(64):
            pr_, g2 = g // 2, g % 2
            rows = slice(g2 * 64, (g2 + 1) * 64)
            pi2 = 2 + g % 2
            b.mm(PS(pi2)[:, 0:128], Xr[rows, pr_, :, :].rearrange("p s h -> p (s h)"), Wd[rows, dr, pr_, 0, :], True, False,
                 ["Xr", f"Wdr{dr}"], [f"ps{pi2}"])
            b.mm(PS(pi2)[:, 0:128], Xi[rows, pr_, :, :].rearrange("p s h -> p (s h)"), Wd[rows, dr, pr_, 1, :], False, True,
                 ["Xi", f"Wdi{dr}"], [f"ps{pi2}"])
            if dr == 0:
                b.tt("dve", tmpk, PS(pi2)[:, 0:128], maskf, ALU.mult, [f"ps{pi2}", "maskf"], ["v1"])
                b.stt(Kt[:, g, :], identf, dsh[:, g:g + 1], tmpk, ALU.mult, ALU.add, ["identf", "dsh", "v1"], ["Kt"])
            else:
                b.tt("dve", tmpk, PS(pi2)[:, 0:128], maskb, ALU.mult, [f"ps{pi2}", "maskb"], ["v1"])
                b.tt("dve", Kt[:, g, :], Kt[:, g, :], tmpk, ALU.add, ["Kt", "v1"], ["Kt"])
    P.barrier()
    ar.release()
    b.dma("sp", d_kt.rearrange("g p m -> p g m"), Kt, ["Kt"], [], "fk1")
    b.dma("sp", d_wbt.rearrange("q p d r m -> p d q r m"), Wbt, ["Wbt"], [], "fk2")
    b.dma("sp", d_wd.rearrange("q p d r m -> p d q r m"), Wd, ["Wdr0", "Wdi0", "Wdr1", "Wdi1"], [], "fk3")
    P.barrier()
    ar.release()

    if stop == 'F':
        return done()
    ar.mark()
    zsel = ar.alloc([8, 240], BF16); qsel = ar.alloc([8, 240], BF16)
    b.dma("sp", zsel, c_zsel, [], ["zsel"], "g1")
    b.dma("sp", qsel, c_qsel, [], ["qsel"], "g2")
    NJ = 544
    S = ar.alloc([2, 2, 32, NJ], BF16)
    AR2 = ar.alloc([2, 2, 32], F32); AIS = ar.alloc([2, 2, 32], F32)
    Z = [ar.alloc([2, 2, 32], F32) for _ in range(2)]
    ta = ar.alloc([2, 2, 32], F32); tb = ar.alloc([2, 2, 32], F32)
    for ri in range(2):
        b.copy("dve", AR2[:, ri, :, :], A8r, ["A8r"], ["AR2"])
    b.copy("dve", AIS[:, 0, :, :], A8in, ["A8in"], ["AIS"])
    b.copy("dve", AIS[:, 1, :, :], A8i, ["A8i"], ["AIS"])
    ar.mark()
    Uf1 = ar.alloc([NTOK], BF16)
    Uf = [Uf1, Uf1]
    wbtf = [ar.alloc([4, 512], BF16) for _ in range(2)]
    Ushp = [ar.alloc([2, NJ], BF16) for _ in range(2)]
    Udi = [ar.alloc([8, NJ], BF16) for _ in range(2)]
    pscnt = 0
    g1pend = [None]
    for fb in range(8):
        fi_ = fb % 2
        b.dma("sp", Uf[fi_], d_u[fb], [], ["Uf"], "gUf")
        b.dma("sp", wbtf[fi_], d_wbt[fb * 4:(fb + 1) * 4].rearrange("q p d r m -> p q (d r m)"), [], [f"wbtf{fi_}"], f"gwb{fi_}")
        Ud = Udi[fi_]
        b.copy("act", Ud[:, :, 32:NJ], Uf[fi_][:, CTX:NTOK].rearrange("p (j s) -> p s j", s=8), ["Uf"], [f"Ud{fi_}"])
        b.copy("dve", Ud[:, :, 0:32], Uf[fi_][:, 0:CTX].rearrange("p (j s) -> p s j", s=8), ["Uf"], [f"Ud{fi_}"])
        Ux = Ud[:, :, 32:NJ]
        Uc = Ud[:, :, 0:32]
        for pl in range(4):
            pr_ = fb * 4 + pl
            ui = pr_ % 2
            ub = Ushp[ui]
            for g2 in range(2):
                gl_ = pl * 2 + g2
                g = fb * 8 + gl_
                pi2 = pscnt % 2; pscnt += 1
                for s_ in range(8):
                    lw = zsel[:, gl_, (7 - s_) * 16:(7 - s_) * 16 + 128]
                    b.mm(PS(pi2), lw, Ux[:, s_, :], s_ == 0, s_ == 7, ["zsel", f"Ud{fi_}"], [f"ps{pi2}"])
                for s_ in range(8):
                    lw = zsel[:, gl_, (7 - s_) * 16:(7 - s_) * 16 + 128]
                    b.mm(PS(2 + pi2)[:, 0:32], lw, Uc[:, s_, :], s_ == 0, s_ == 7, ["zsel", f"Ud{fi_}"], [f"ps{2 + pi2}"])
                b.copy("act", ub[:, g2, 32:NJ], PS(pi2), [f"ps{pi2}"], [f"ub{ui}"])
                b.copy("dve", ub[:, g2, 0:32], PS(2 + pi2)[:, 0:32], [f"ps{2 + pi2}"], [f"ub{ui}"])
                b.dma("pool", d_ush[g], ub[:, g2, :], [f"ub{ui}"], [], f"gus{ui}{g2}")
            def chunk_states(pl=pl, pr_=pr_, ui=ui, ub=ub, fi_=fi_):
                nonlocal pscnt
                for dr in range(2):
                    for ri in range(2):
                        pi2 = pscnt % 2; pscnt += 1
                        c0 = (dr * 2 + ri) * 128
                        for g2 in range(2):
                            rows = slice(g2 * 64, (g2 + 1) * 64)
                            lw = wbtf[fi_][:, pl, c0 + g2 * 64:c0 + (g2 + 1) * 64]
                            b.mm(ps_t[rows, 4 + pi2, :], lw, ub[:, g2, 32:NJ], True, True, [f"wbtf{fi_}", f"ub{ui}"], [f"ps{4 + pi2}"])
                            b.mm(ps_t[rows, 6 + pi2, 0:32], lw, ub[:, g2, 0:32], True, True, [f"wbtf{fi_}", f"ub{ui}"], [f"ps{6 + pi2}"])
                        sk = f"Sw{pr_}_{dr}_{ri}"
                        if dr == 0:
                            b.copy("act", S[:, ri, dr, pr_, 32:NJ], PS(4 + pi2), [f"ps{4 + pi2}"], [sk + "x"])
                            b.copy("dve", S[:, ri, dr, pr_, 0:32], PS(6 + pi2)[:, 0:32], [f"ps{6 + pi2}"], [sk + "c"])
                        else:
                            b.copy("act", S[:, ri, dr, pr_, 543:31:-1], PS(4 + pi2), [f"ps{4 + pi2}"], [sk + "x"])
                            b.copy("dve", S[:, ri, dr, pr_, 31::-1], PS(6 + pi2)[:, 0:32], [f"ps{6 + pi2}"], [sk + "c"])
            if g1pend[0] is not None:
                g1pend[0]()
            g1pend[0] = chunk_states
    if g1pend[0] is not None:
        g1pend[0]()
    P.barrier()
    ar.release()
    b.copy("dve", Z[0], S[:, :, :, :, 0], ["S"], ["Z0"])
    for i in range(1, NJ):
        zc, zn = Z[(i - 1) % 2], Z[i % 2]
        kc_, kn_ = f"Z{(i - 1) % 2}", f"Z{i % 2}"
        b.tt("dve", ta, AR2, zc, ALU.mult, ["AR2", kc_], ["ta"])
        b.tt("dve", tb, AIS, zc[:, ::-1, :, :], ALU.mult, ["AIS", kc_], ["tb"])
        b.tt("dve", ta, ta, tb, ALU.add, ["ta", "tb"], ["ta"])
        b.tt("dve", zn, ta, S[:, :, :, :, i], ALU.add, ["ta", f"S{i}"], [kn_])
        b.copy("act", S[:, :, :, :, i], zn, [kn_], [f"S{i}"])
    P.barrier()
    ar.mark()
    ktf = [ar.alloc([8, 128], BF16) for _ in range(2)]
    wdf = [ar.alloc([4, 512], BF16) for _ in range(2)]
    ushf = ar.alloc([8, NJ], BF16)
    Ysh = ar.alloc([8, 512], BF16)
    Gb = ar.alloc([SEQ], BF16)
    for fb in range(8):
        fi_ = fb % 2
        b.dma("sp", ktf[fi_], d_kt[fb * 8:(fb + 1) * 8].rearrange("g p m -> p g m"), [], [f"ktf{fi_}"], f"gkt{fi_}")
        b.dma("sp", wdf[fi_], d_wd[fb * 4:(fb + 1) * 4].rearrange("q p d r m -> p q (d r m)"), [], [f"wdf{fi_}"], f"gwd{fi_}")
        b.dma("sp", ushf, d_ush[fb * 8:(fb + 1) * 8].rearrange("g p j -> p g j"), [], ["ushf"], "gush")
        for gl_ in range(8):
            pl, g2 = gl_ // 2, gl_ % 2
            pr_ = fb * 4 + pl
            rows = slice(g2 * 64, (g2 + 1) * 64)
            pi2 = pscnt % 2; pscnt += 1
            b.mm(PS(pi2), ktf[fi_][:, gl_, :], ushf[:, gl_, 32:NJ], True, False, [f"ktf{fi_}", "ushf"], [f"ps{pi2}"])
            for ri in range(2):
                b.mm(PS(pi2), wdf[fi_][rows, pl, (0 * 2 + ri) * 128:(0 * 2 + ri + 1) * 128], S[rows, ri, 0, pr_, 31:543],
                     False, False, [f"wdf{fi_}"], [f"ps{pi2}"])
            for ri in range(2):
                b.mm(PS(pi2), wdf[fi_][rows, pl, (1 * 2 + ri) * 128:(1 * 2 + ri + 1) * 128], S[rows, ri, 1, pr_, 542:30:-1],
                     False, ri == 1, [f"wdf{fi_}"], [f"ps{pi2}"])
            b.copy("act" if gl_ % 2 == 0 else "dve", Ysh[:, gl_, :], PS(pi2), [f"ps{pi2}"], [f"Ysh{gl_}"])
        Gv = Gb.rearrange("p (j t) -> p t j", t=8)
        for t in range(8):
            pi2 = 4 + t % 2
            for gl_ in range(8):
                lw = qsel[:, t, (7 - gl_) * 16:(7 - gl_) * 16 + 128]
                b.mm(PS(pi2), lw, Ysh[:, gl_, :], gl_ == 0, gl_ == 7, ["qsel", f"Ysh{gl_}"], [f"ps{pi2}"])
            b.act(Gv[:, t, :], PS(pi2), AF.Gelu_apprx_tanh, [f"ps{pi2}"], ["Gb"])
        b.dma("pool", d_g[fb], Gb, ["Gb"], [], "gGb")
    P.barrier()
    ar.release()
    ar.release()

    if stop == 'G':
        return done()
    ar.mark()
    wg = ar.alloc([8, D], BF16); wo1 = ar.alloc([8, D], BF16)
    wos = ar.alloc([4, D], F32)
    for wi, (wsrc, wdst) in enumerate(((od_glu_w, wg), (od_out_w, wo1))):
        for q2 in range(2):
            b.dma("sp", wos, wsrc[q2 * 512:(q2 + 1) * 512, :].rearrange("(kc p) f -> p kc f", p=128), [], ["wos"], "hwos")
            b.copy("dve" if q2 == 0 else "act", wdst[:, q2 * 4:(q2 + 1) * 4, :], wos, ["wos"], [f"hw{wi}{q2}"])
    glub = ar.alloc([8], F32)
    b.dma("sp", glub, od_glu_b, [], ["glub"], "h1")
    fg = ar.alloc([D], F32)
    b.dma("sp", fg, final_g.partition_broadcast(128), [], ["fg"], "h2")
    gt_ = [ar.alloc([8, 512], BF16) for _ in range(2)]
    zt_ = [ar.alloc([8, 512], BF16) for _ in range(2)]
    mT = [ar.alloc([8, 512], BF16) for _ in range(2)]
    sgm = [ar.alloc([512], F32) for _ in range(2)]
    gm = [ar.alloc([512], F32) for _ in range(2)]
    x1r = [ar.alloc([D], F32) for _ in range(2)]
    x2 = [ar.alloc([D], F32) for _ in range(2)]
    tmpo = [ar.alloc([512], F32) for _ in range(2)]
    junk = ar.alloc([D], BF16)
    ss2 = [ar.alloc([1], F32) for _ in range(2)]; sd2 = [ar.alloc([1], F32) for _ in range(2)]; rs2 = [ar.alloc([1], F32) for _ in range(2)]
    xf = [ar.alloc([D], F32) for _ in range(2)]
    for tt in range(8):
        i = tt % 2
        t0 = tt * 512
        b.dma("sp", gt_[i], d_g[:, :, t0:t0 + 512].rearrange("f p t -> p f t"), [], [f"gt{i}"], f"hgt{i}")
        b.dma("sp", zt_[i], d_sz[:, :, t0:t0 + 512].rearrange("f p t -> p f t"), [], [f"zt{i}"], f"hzt{i}")
        for fo in range(8):
            j = fo % 2
            for kc in range(8):
                b.mm(PS(j), wg[:, kc, fo * 128:(fo + 1) * 128], gt_[i][:, kc, :], kc == 0, kc == 7,
                     [f"gt{i}", "hw00", "hw01"], [f"ps{j}"])
            b.act(sgm[j], PS(j), AF.Sigmoid, [f"ps{j}", "glub"], [f"sgm{j}"], bias=glub[:, fo:fo + 1])
            b.tt("dve", gm[j], sgm[j], gt_[i][:, fo, :], ALU.mult, [f"sgm{j}", f"gt{i}"], [f"gm{j}"])
            b.tt("dve", mT[i][:, fo, :], gm[j], zt_[i][:, fo, :], ALU.mult, [f"gm{j}", f"zt{i}"], [f"mT{i}"])
        for sub in range(4):
            t = tt * 4 + sub
            k2 = t % 2
            b.dma("sp", x1r[k2], d_x1[CTX + t * 128:CTX + (t + 1) * 128, :], [], [f"x1r{k2}"], f"hx1{k2}")
            for half in range(2):
                for fo in range(8):
                    b.mm(PS(2 + half), mT[i][:, fo, sub * 128:(sub + 1) * 128], wo1[:, fo, half * 512:(half + 1) * 512],
                         fo == 0, fo == 7, [f"mT{i}", "hw10", "hw11"], [f"ps{2 + half}"])
                b.tt("dve", tmpo[half], PS(2 + half), gtrow[1][0][:, half * 512:(half + 1) * 512], ALU.mult,
                     [f"ps{2 + half}", "gtrow10"], [f"tmpo{half}"])
                b.tt("dve", x2[k2][:, half * 512:(half + 1) * 512], tmpo[half], x1r[k2][:, half * 512:(half + 1) * 512], ALU.add,
                     [f"tmpo{half}", f"x1r{k2}"], [f"x2{k2}"])
            b.act(junk, x2[k2], AF.Square, [f"x2{k2}"], ["junk", f"ss2{k2}"], accum_out=ss2[k2])
            b.act(sd2[k2], ss2[k2], AF.Sqrt, [f"ss2{k2}"], [f"sd2{k2}"], scale=1.0 / D, bias=eps_t)
            b.recip(rs2[k2], sd2[k2], [f"sd2{k2}"], [f"rs2{k2}"])
            b.stt(xf[k2], x2[k2], rs2[k2], fg, ALU.mult, ALU.mult, [f"x2{k2}", f"rs2{k2}", "fg"], [f"xf{k2}"])
            b.dma("pool", out_ap[t * 128:(t + 1) * 128, :], xf[k2], [f"xf{k2}"], [], f"hout{k2}")
    P.barrier()
    ar.release()
    P.emit()
    es.close()
    return nc, P, ar


def _fm(v):
    return np.ascontiguousarray(v.reshape(8, 128).T)


def _s5_state(a):
    a = a.reshape(2, 32, 2, 64)
    return np.ascontiguousarray(a.transpose(2, 3, 0, 1).reshape(128, 64))


def _consts():
    c = {}
    n = np.arange(SEQ)
    row = (n // 64).astype(np.float32); col = (n % 64).astype(np.float32)
    freqs = (np.float32(10000.0) ** (-np.arange(16, dtype=np.float32) / np.float32(16))).astype(np.float32)
    ang = np.concatenate([row[:, None] * freqs, col[:, None] * freqs], axis=-1).astype(np.float32)
    cos = np.cos(ang).astype(np.float32).T
    sin = np.sin(ang).astype(np.float32).T
    ccos = np.zeros((128, SEQ), np.float32); csin = np.zeros((128, SEQ), np.float32)
    for comp in range(2):
        for half in range(2):
            r0 = comp * 64 + half * 32
            ccos[r0:r0 + 32] = cos
            csin[r0:r0 + 32] = -sin if half == 0 else sin
    c["c_cos"] = ccos; c["c_sin"] = csin
    perm = np.zeros((128, 128), np.float32)
    for m in range(128):
        comp, r = divmod(m, 64)
        partner = comp * 64 + (r + 32) % 64
        perm[partner, m] = 1.0
    c["c_perm"] = perm.astype(ml_dtypes.bfloat16)
    c["c_ident"] = np.eye(128, dtype=np.float32).astype(ml_dtypes.bfloat16)
    zsel = np.zeros((128, 8, 240), np.float32)
    qsel = np.zeros((128, 8, 240), np.float32)
    for g in range(8):
        for h in range(16):
            zsel[g * 16 + h, g, 112 + h] = 1.0
            qsel[g * 16 + h, g, 112 + h] = 1.0
    c["c_zsel"] = zsel.astype(ml_dtypes.bfloat16); c["c_qsel"] = qsel.astype(ml_dtypes.bfloat16)
    s_idx = np.arange(128) // 16
    c["c_maskf"] = (s_idx[None, :] >= s_idx[:, None]).astype(np.float32)
    c["c_maskb"] = (s_idx[:, None] >= s_idx[None, :]).astype(np.float32)
    c["c_kvec"] = np.broadcast_to(np.arange(-8, 9, dtype=np.float32)[None, :], (128, 17)).copy()
    return c


def _prep(inputs):
    f = lambda a: np.ascontiguousarray(np.asarray(a, dtype=np.float32))
    shared = {}
    shared["ev_mod_w"] = f(inputs["ev_mod_w"][0])
    shared["ev_mod_bf"] = np.ascontiguousarray(f(inputs["ev_mod_b"][0]).reshape(24, 128).T)
    shared["ev_mod_bg"] = f(inputs["ev_mod_b"][0][2048:3072]).reshape(1, D)
    shared["ev_norm_g"] = _fm(f(inputs["ev_norm_g"][0]))
    shared["ev_in_w"] = f(inputs["ev_in_w"][0])
    shared["ev_lam"] = np.concatenate([f(inputs["ev_lam_q1"][0]), f(inputs["ev_lam_k1"][0]),
                                       f(inputs["ev_lam_q2"][0]), f(inputs["ev_lam_k2"][0])]).reshape(1, 256)
    shared["ev_subln"] = f(inputs["ev_subln_g"][0]).reshape(128, 1)
    shared["ev_dw_w"] = np.ascontiguousarray(f(inputs["ev_dw_w"][0]).reshape(31, 8, 128).transpose(2, 1, 0))
    shared["ev_dw_b"] = _fm(f(inputs["ev_dw_b"][0]))
    shared["ev_cln_g"] = _fm(f(inputs["ev_cln_g"][0]))
    shared["ev_cln_b"] = _fm(f(inputs["ev_cln_b"][0]))
    shared["ev_out_w"] = f(inputs["ev_out_w"][0])
    shared["od_mod_w"] = f(inputs["od_mod_w"][0])
    shared["od_mod_bf"] = np.ascontiguousarray(f(inputs["od_mod_b"][0]).reshape(24, 128).T)
    shared["od_mod_bg"] = f(inputs["od_mod_b"][0][2048:3072]).reshape(1, D)
    shared["od_norm_g"] = _fm(f(inputs["od_norm_g"][0]))
    shared["od_in_w"] = f(inputs["od_in_w"][0])
    shared["s5_lre"] = _s5_state(f(inputs["s5_lam_re"][0]))
    shared["s5_lim"] = _s5_state(f(inputs["s5_lam_im"][0]))
    shared["s5_ldt"] = _s5_state(np.broadcast_to(f(inputs["s5_log_dt"][0])[:, :, None], (2, 64, 64)))
    for nm, key in (("s5_bre", "s5_b_re"), ("s5_bim", "s5_b_im")):
        a = f(inputs[key][0]).reshape(2, 32, 2, 64, 16)
        shared[nm] = np.ascontiguousarray(a.transpose(2, 3, 0, 1, 4).reshape(128, 64, 16))
    for nm, key in (("s5_cre", "s5_c_re"), ("s5_cim", "s5_c_im")):
        a = f(inputs[key][0]).reshape(2, 32, 2, 16, 64)
        shared[nm] = np.ascontiguousarray(a.transpose(2, 4, 0, 1, 3).reshape(128, 64, 16))
    dd = f(inputs["s5_d"][0]).reshape(64, 16)
    shared["s5_d"] = np.ascontiguousarray(np.tile(dd.T, (8, 1)))
    shared["od_glu_w"] = f(inputs["od_glu_w"][0])
    shared["od_glu_b"] = _fm(f(inputs["od_glu_b"][0]))
    shared["od_out_w"] = f(inputs["od_out_w"][0])
    shared["final_g"] = f(inputs["final_g"]).reshape(1, D)
    shared.update(_consts())
    x = f(inputs["x"]); ctx = f(inputs["ctx"]); c = f(inputs["c"]); cctx = f(inputs["c_ctx"])
    maps = []
    for bi in range(x.shape[0]):
        m = dict(shared)
        m["x"] = x[bi]
        m["ctx"] = ctx[bi]
        m["cc"] = np.ascontiguousarray(np.stack([_fm(c[bi]), _fm(cctx)], axis=-1))
        maps.append(m)
    return maps


_CACHE = {}


def kernel(**inputs):
    maps = _prep(inputs)
    if "nc" not in _CACHE:
        _CACHE["nc"] = build(False)[0]
    nc = _CACHE["nc"]
    res = run_bass_kernel_spmd(nc, maps, core_ids=list(range(NB)))
    return np.stack([np.asarray(r["out"], dtype=np.float32) for r in res.results], axis=0)
```
